# Optimizing a Trainium2 kernel written in Bass

```python
import jax, jax.numpy as jnp
from jax import lax
import numpy as np

D_MODEL = 1024
BATCH = 2
SEQ = 16384
DEPTH = 4

N_MIXERS = 3
N_CONV_LAYERS = (DEPTH + 2) // 3
N_SGU_LAYERS = (DEPTH + 1) // 3
N_POOL_LAYERS = DEPTH // 3

CONV_WIDTH = 31
CHUNK = 128
SGU_HEADS = 8
SGU_HEAD_DIM = D_MODEL // SGU_HEADS
POOL_WINDOWS = (2, 4, 8, 16)
POOL_GROUPS = len(POOL_WINDOWS)
POOL_GROUP_DIM = D_MODEL // POOL_GROUPS
D_FF = 2816
FFN_CONV_WIDTH = 3
DEEPNORM_ALPHA = float((2 * DEPTH) ** 0.25)
DEEPNORM_BETA = float((8 * DEPTH) ** -0.25)
LN_EPS = 1e-5

kernel_name = "interleaved_conv_sgu_pool_deepnorm_trunk"


def layer_norm(x, g, b):
    xf = x.astype(jnp.float32)
    mu = jnp.mean(xf, axis=-1, keepdims=True)
    var = jnp.mean(jnp.square(xf - mu), axis=-1, keepdims=True)
    y = (xf - mu) * lax.rsqrt(var + LN_EPS) * g.astype(jnp.float32) + b.astype(jnp.float32)
    return y.astype(x.dtype)


def causal_depthwise_conv(x, w):
    k, c = w.shape
    return lax.conv_general_dilated(
        x, w[:, None, :].astype(x.dtype), window_strides=(1,), padding=[(k - 1, 0)],
        dimension_numbers=("NWC", "WIO", "NWC"), feature_group_count=c)


def conformer_conv(x, w_in, dw, dw_b, ln_g, ln_b, w_out):
    h = x @ w_in
    a, gate = jnp.split(h, 2, axis=-1)
    h = a * jax.nn.sigmoid(gate)
    h = causal_depthwise_conv(h, dw) + dw_b
    h = jax.nn.silu(layer_norm(h, ln_g, ln_b))
    return h @ w_out


def chunked_sgu(x, w_in, ln_g, ln_b, ws, bs, w_out):
    bsz, seq, _ = x.shape
    z = jax.nn.gelu(x @ w_in, approximate=False)
    u, v = jnp.split(z, 2, axis=-1)
    v = layer_norm(v, ln_g, ln_b)
    v = v.reshape(bsz, seq // CHUNK, CHUNK, SGU_HEADS, SGU_HEAD_DIM)
    mask = jnp.tril(jnp.ones((CHUNK, CHUNK), dtype=ws.dtype))
    s = jnp.einsum("hts,bnshc->bnthc", ws * mask, v)
    s = s + jnp.transpose(bs)[None, None, :, :, None]
    s = s.reshape(bsz, seq, D_MODEL)
    return (u * s) @ w_out


def multiscale_pool(x, w_in, w_grp, scale, w_out):
    bsz, seq, _ = x.shape
    y = x @ w_in
    yf = y.astype(jnp.float32)
    cs = jnp.concatenate([jnp.zeros((bsz, 1, D_MODEL), jnp.float32),
                          lax.cumsum(yf, axis=1)], axis=1)
    pos = jnp.arange(seq)
    groups = []
    for g, w in enumerate(POOL_WINDOWS):
        sl = slice(g * POOL_GROUP_DIM, (g + 1) * POOL_GROUP_DIM)
        c = cs[..., sl]
        upper = c[:, 1:]
        lower = jnp.pad(c[:, :seq - w + 1], ((0, 0), (w - 1, 0), (0, 0)))
        count = jnp.minimum(pos + 1, w).astype(jnp.float32)[None, :, None]
        groups.append((upper - lower) / count - yf[..., sl])
    p = jnp.stack(groups, axis=2).astype(y.dtype)
    z = jnp.einsum("bsgc,gcd->bsgd", p, w_grp).reshape(bsz, seq, D_MODEL) * scale
    return z @ w_out


def conv_ffn(x, w_up, dw, w_down):
    h = causal_depthwise_conv(x @ w_up, dw)
    g, v = jnp.split(h, 2, axis=-1)
    return (jax.nn.silu(g) * v) @ w_down


def setup_inputs(seed: int = 0) -> dict:
    key = jax.random.key(seed)
    ks = iter(jax.random.split(key, 32))

    def nrm(shape, scale):
        return jax.random.normal(next(ks), shape, jnp.float32) * scale

    d = D_MODEL
    return {
        "x": nrm((BATCH, SEQ, d), 1.0),
        "a_w_in": nrm((N_CONV_LAYERS, d, 2 * d), d ** -0.5),
        "a_dw": nrm((N_CONV_LAYERS, CONV_WIDTH, d), CONV_WIDTH ** -0.5),
        "a_dw_b": nrm((N_CONV_LAYERS, d), 0.02),
        "a_ln_g": 1.0 + nrm((N_CONV_LAYERS, d), 0.02),
        "a_ln_b": nrm((N_CONV_LAYERS, d), 0.02),
        "a_w_out": nrm((N_CONV_LAYERS, d, d), d ** -0.5 * DEEPNORM_BETA),
        "b_w_in": nrm((N_SGU_LAYERS, d, 2 * d), d ** -0.5),
        "b_ln_g": 1.0 + nrm((N_SGU_LAYERS, d), 0.02),
        "b_ln_b": nrm((N_SGU_LAYERS, d), 0.02),
        "b_ws": nrm((N_SGU_LAYERS, SGU_HEADS, CHUNK, CHUNK), CHUNK ** -0.5),
        "b_bs": 1.0 + nrm((N_SGU_LAYERS, SGU_HEADS, CHUNK), 0.01),
        "b_w_out": nrm((N_SGU_LAYERS, d, d), d ** -0.5 * DEEPNORM_BETA),
        "c_w_in": nrm((N_POOL_LAYERS, d, d), d ** -0.5),
        "c_w_grp": nrm((N_POOL_LAYERS, POOL_GROUPS, POOL_GROUP_DIM, POOL_GROUP_DIM), POOL_GROUP_DIM ** -0.5),
        "c_scale": 1.0 + nrm((N_POOL_LAYERS, d), 0.1),
        "c_w_out": nrm((N_POOL_LAYERS, d, d), d ** -0.5 * DEEPNORM_BETA),
        "f_w_up": nrm((DEPTH, d, 2 * D_FF), d ** -0.5),
        "f_dw": nrm((DEPTH, FFN_CONV_WIDTH, 2 * D_FF), FFN_CONV_WIDTH ** -0.5),
        "f_w_down": nrm((DEPTH, D_FF, d), D_FF ** -0.5 * DEEPNORM_BETA),
        "ln1_g": 1.0 + nrm((DEPTH, d), 0.02),
        "ln1_b": nrm((DEPTH, d), 0.02),
        "ln2_g": 1.0 + nrm((DEPTH, d), 0.02),
        "ln2_b": nrm((DEPTH, d), 0.02),
    }


def reference(x, a_w_in, a_dw, a_dw_b, a_ln_g, a_ln_b, a_w_out,
              b_w_in, b_ln_g, b_ln_b, b_ws, b_bs, b_w_out,
              c_w_in, c_w_grp, c_scale, c_w_out,
              f_w_up, f_dw, f_w_down,
              ln1_g, ln1_b, ln2_g, ln2_b):
    for i in range(DEPTH):
        kind, j = i % N_MIXERS, i // N_MIXERS
        if kind == 0:
            h = conformer_conv(x, a_w_in[j], a_dw[j], a_dw_b[j], a_ln_g[j], a_ln_b[j], a_w_out[j])
        elif kind == 1:
            h = chunked_sgu(x, b_w_in[j], b_ln_g[j], b_ln_b[j], b_ws[j], b_bs[j], b_w_out[j])
        else:
            h = multiscale_pool(x, c_w_in[j], c_w_grp[j], c_scale[j], c_w_out[j])
        x = layer_norm(DEEPNORM_ALPHA * x + h, ln1_g[i], ln1_b[i])
        x = layer_norm(DEEPNORM_ALPHA * x + conv_ffn(x, f_w_up[i], f_dw[i], f_w_down[i]), ln2_g[i], ln2_b[i])
    return x
```

```python
import numpy as np
import concourse.bass as bass
import concourse.mybir as mybir
from concourse.bass_utils import run_bass_kernel_spmd

F32 = mybir.dt.float32
BF16 = mybir.dt.bfloat16
AF = mybir.ActivationFunctionType
ALU = mybir.AluOpType

D = 1024
CT = 8
DFF = 2816
NP = 22
DEPTH = 4
NCORES = 8
HALO = 256
TMAX = 512
ALPHA = float((2 * DEPTH) ** 0.25)
EPS = 1e-5
SLOT_COLS = 3072
NSLOT = 7
LOOKAHEAD = 5
NCAST_SEM = 8
CAST_AHEAD = 10
CONVW = 31
NDVE = 7
HOFF = 32


class Layout:
    def __init__(self):
        self.off = {}
        self.n = 0

    def add(self, name, ncols):
        self.off[name] = (self.n, ncols)
        self.n += ncols

    def __getitem__(self, name):
        return self.off[name][0]


def make_cpk_layout():
    L = Layout()
    for li in range(DEPTH):
        for nm in ("ln1_g", "ln1_b", "ln2_g", "ln2_b"):
            L.add((nm, li), CT)
        L.add(("fdw", li), 3 * 2 * NP)
    for j in range(2):
        L.add(("a_dw", j), CT * CONVW)
        L.add(("a_dw_b", j), CT)
        L.add(("a_ln_g", j), CT)
        L.add(("a_ln_b", j), CT)
    L.add("b_ln_g", CT)
    L.add("b_ln_b", CT)
    L.add("c_scale", CT)
    L.add("bsb", CT * 128)
    L.add("flag", 1)
    L.add("eps", 1)
    L.add("icnt", 4 * 16)
    for li in range(DEPTH):
        for w in (1, 2):
            L.add(("ag", li, w), CT)
            L.add(("ab", li, w), CT)
    return L


CPK = make_cpk_layout()
STG_COLS = CT * 128 + 128 + 128


def layer_chunks(li):
    kind = li % 3
    ch = []
    if kind == 0:
        ch.append((("a_in", li, 0), 8 * 2 * 128))
        ch.append((("a_in", li, 1), 8 * 2 * 128))
        for m in range(CT):
            if m + 2 < CT:
                ch.append((("a_in", li, m + 2), 8 * 2 * 128))
            ch.append((("a_dg", li, m), (CONVW - NDVE) * 128))
        for h in range(4):
            ch.append((("mix_out", li, h), 8 * 2 * 128))
    elif kind == 1:
        for i in range(CT):
            ch.append((("b_in", li, i), 8 * 2 * 128))
        for h in range(4):
            ch.append((("mix_out", li, h), 8 * 2 * 128))
    else:
        for i in range(4):
            ch.append((("c_in", li, i), 8 * 2 * 128))
        ch.append((("c_grp", li), 4 * 2 * 2 * 128))
        for h in range(4):
            ch.append((("mix_out", li, h), 8 * 2 * 128))
    for j in range(NP):
        ch.append((("f_up", li, j), 8 * 2 * 128))
    for m in range(CT):
        ch.append((("f_down", li, m), NP * 128))
    return ch


def all_chunks(layers):
    off = 0
    out = []
    for li in layers:
        for key, n in layer_chunks(li):
            out.append((key, off, n))
            off += n
    return out, off


def pack_blocks(W, blocks):
    K = W.shape[0]
    kt = K // 128
    Wr = W.reshape(kt, 128, W.shape[1] // 128, 128)
    sel = Wr[:, :, blocks, :]
    return np.ascontiguousarray(sel.transpose(1, 0, 2, 3)).reshape(128, -1)


def pack_weights(inp, layers):
    chunks, total = all_chunks(layers)
    wpk = np.zeros((128, total), np.float32)
    for key, off, n in chunks:
        kind = key[0]
        li = key[1]
        j = li // 3
        if kind == "a_in":
            m = key[2]
            v = pack_blocks(inp["a_w_in"][j], [m, CT + m])
        elif kind == "a_dg":
            m = key[2]
            dw = inp["a_dw"][j][:, m * 128:(m + 1) * 128]
            v = np.zeros((128, CONVW, 128), np.float32)
            v[np.arange(128), :, np.arange(128)] = dw.T
            v = np.ascontiguousarray(v[:, NDVE:, :]).reshape(128, -1)
        elif kind == "b_in":
            i = key[2]
            if i < 4:
                v = pack_blocks(inp["b_w_in"][j], [CT + 2 * i, CT + 2 * i + 1])
            else:
                v = pack_blocks(inp["b_w_in"][j], [2 * (i - 4), 2 * (i - 4) + 1])
        elif kind == "c_in":
            i = key[2]
            v = pack_blocks(inp["c_w_in"][j], [2 * i, 2 * i + 1])
        elif kind == "c_grp":
            G = inp["c_w_grp"][j]
            Gr = G.reshape(4, 2, 128, 2, 128)
            v = np.ascontiguousarray(Gr.transpose(2, 0, 1, 3, 4)).reshape(128, -1)
        elif kind == "mix_out":
            h = key[2]
            W = {0: inp["a_w_out"], 1: inp["b_w_out"], 2: inp["c_w_out"]}[li % 3][j]
            v = pack_blocks(W, [2 * h + q for q in range(2)])
        elif kind == "f_up":
            v = pack_blocks(inp["f_w_up"][li], [key[2], NP + key[2]])
        elif kind == "f_down":
            v = pack_blocks(inp["f_w_down"][li], [key[2]])
        assert v.shape[1] == n, (key, v.shape, n)
        wpk[:, off:off + n] = v
    return wpk


def vec8(v):
    return np.ascontiguousarray(v.reshape(-1, 128).T)


def pack_cpk(inp, core_is_start):
    c = np.zeros((128, CPK.n), np.float32)

    def put(name, arr):
        o, n = CPK.off[name]
        assert arr.shape == (128, n), (name, arr.shape, n)
        c[:, o:o + n] = arr

    for li in range(DEPTH):
        for nm in ("ln1_g", "ln1_b", "ln2_g", "ln2_b"):
            put((nm, li), vec8(inp[nm][li]))
        fd = inp["f_dw"][li]
        put(("fdw", li), np.ascontiguousarray(
            fd.reshape(3, 2 * NP, 128).transpose(2, 0, 1)).reshape(128, -1))
    for j in range(2):
        dw = inp["a_dw"][j]
        put(("a_dw", j), np.ascontiguousarray(
            dw.reshape(CONVW, CT, 128).transpose(2, 1, 0)).reshape(128, -1))
        put(("a_dw_b", j), vec8(inp["a_dw_b"][j]))
        put(("a_ln_g", j), vec8(inp["a_ln_g"][j]))
        put(("a_ln_b", j), vec8(inp["a_ln_b"][j]))
    put("b_ln_g", vec8(inp["b_ln_g"][0]))
    put("b_ln_b", vec8(inp["b_ln_b"][0]))
    put("c_scale", vec8(inp["c_scale"][0]))
    put("bsb", np.broadcast_to(inp["b_bs"][0].reshape(1, CT * 128), (128, CT * 128)))
    put("flag", np.full((128, 1), 0.0 if core_is_start else 1.0, np.float32))
    put("eps", np.full((128, 1), EPS, np.float32))
    ic = np.zeros((4, 16), np.float32)
    for g, w in enumerate((2, 4, 8, 16)):
        for t in range(16):
            ic[g, t] = 1.0 / (min(t + 1, w) if core_is_start else w)
    put("icnt", np.broadcast_to(ic.reshape(1, 64), (128, 64)))
    return c


def pack_stage(inp):
    s = np.zeros((128, STG_COLS), np.float32)
    ws = inp["b_ws"][0]
    s[:, :CT * 128] = np.ascontiguousarray(ws.transpose(2, 0, 1)).reshape(128, CT * 128)
    idx = np.arange(128)
    s[:, CT * 128:CT * 128 + 128] = (idx[:, None] <= idx[None, :]).astype(np.float32)
    s[:, CT * 128 + 128:] = np.eye(128, dtype=np.float32)
    return s


class Planner:
    def __init__(self):
        self.streams = {e: [] for e in ("pe", "act", "dve", "pool", "sp")}
        self.cnt = {e: 0 for e in self.streams}
        self.seen = {e: {} for e in self.streams}
        self.lastw = {}
        self.readers = {}
        self.dmacnt = {}

    def _deps(self, eng, reads, writes):
        need = {}

        def add(d, is_raw):
            if d is None:
                return
            k, v = d
            if k == eng and (eng == "pe" or not is_raw):
                return
            if need.get(k, 0) < v:
                need[k] = v
        for r in reads:
            add(self.lastw.get(r), True)
        for r in writes:
            add(self.lastw.get(r), False)
            for rd in self.readers.get(r, ()):
                add(rd, False)
        for k, v in need.items():
            if self.seen[eng].get(k, 0) < v:
                self.seen[eng][k] = v
                self.streams[eng].append(("wait", k, v))

    def _record(self, tag, reads, writes):
        for r in writes:
            self.lastw[r] = tag
            self.readers[r] = []
        for r in reads:
            self.readers.setdefault(r, []).append(tag)

    def op(self, eng, fn, reads=(), writes=(), inc=True):
        self._deps(eng, reads, writes)
        if inc:
            self.cnt[eng] += 1
            tag = (eng, self.cnt[eng])
        else:
            tag = (eng, self.cnt[eng] + 1)
        self.streams[eng].append(("op", fn, eng if inc else None))
        self._record(tag, reads, writes)

    def dma(self, eng, fn, semkey, reads=(), writes=(), extra_waits=()):
        self._deps(eng, reads, writes)
        for k, v in extra_waits:
            if self.seen[eng].get(k, 0) < v:
                self.seen[eng][k] = v
                self.streams[eng].append(("wait", k, v))
        self.dmacnt[semkey] = self.dmacnt.get(semkey, 0) + 16
        tag = (semkey, self.dmacnt[semkey])
        self.streams[eng].append(("dma", fn, semkey))
        self._record(tag, reads, writes)
        return tag

    def wait(self, eng, tag):
        k, v = tag
        if self.seen[eng].get(k, 0) < v:
            self.seen[eng][k] = v
            self.streams[eng].append(("wait", k, v))


def build_program(layers, tiles, n_store_tiles, tok_in, tok_out):
    nc = bass.Bass("TRN2", target_bir_lowering=False)
    chunks, wcols = all_chunks(layers)
    chunk_by_key = {k: (i, off, n) for i, (k, off, n) in enumerate(chunks)}
    nchunk = len(chunks)

    xin = nc.dram_tensor("xin", [128, CT, tok_in], F32, kind="ExternalInput").ap()
    wpk = nc.dram_tensor("wpk", [128, wcols], F32, kind="ExternalInput").ap()
    cpk_d = nc.dram_tensor("cpk", [128, CPK.n], F32, kind="ExternalInput").ap()
    stg_d = nc.dram_tensor("stg", [128, STG_COLS], F32, kind="ExternalInput").ap()
    yout = nc.dram_tensor("yout", [128, CT, tok_out], F32, kind="ExternalOutput").ap()
    wsc = nc.dram_tensor("wsc", [128, wcols], BF16, kind="Internal").ap()

    P = Planner()
    import contextlib
    es = contextlib.ExitStack()
    with es:
        def sb(name, shape, dt):
            return es.enter_context(nc.sbuf_tensor(name, shape, dt))

        xres = sb("xres", [128, CT, TMAX], F32)
        xb = sb("xb", [128, CT, TMAX], BF16)
        zbuf = sb("zbuf", [128, CT, TMAX], F32)
        zb = sb("zb", [128, CT, TMAX], BF16)
        zsq = sb("zsq", [128, CT, TMAX], BF16)
        ub = sb("ub", [128, CT, TMAX], BF16)
        hid = sb("hid", [128, NP, TMAX], BF16)
        mix = sb("mix", [128, CT, HOFF + TMAX], F32)
        lnt = sb("lnt", [128, 3, TMAX], F32)
        tmp = sb("tmp", [128, 4, TMAX], F32)
        cacc = sb("cacc", [128, 4, TMAX], F32)
        cpk = sb("cpk_sb", [128, CPK.n], F32)
        wsTm = sb("wsTm", [128, CT, 128], BF16)
        identb = sb("identb", [128, 128], BF16)
        ones = sb("ones", [128, 128], BF16)
        hp = sb("hp", [128, DEPTH * 2, 2 * NP, 2], F32)
        cstate = sb("cstate", [128, DEPTH, CT, 32], F32)
        edge = sb("edge", [128, 2 * NP, 3], F32)
        scr = sb("scr", [128, 8], F32)
        wslot = [sb(f"wslot{i}", [128, SLOT_COLS], BF16) for i in range(NSLOT)]
        ps = [es.enter_context(nc.psum_tensor(f"ps{i}", [128, TMAX], F32)) for i in range(8)]

        sems = {}
        for e in ("pe", "act", "dve", "pool", "sp"):
            sems[e] = es.enter_context(nc.semaphore(f"c_{e}"))
        for i in range(NSLOT):
            sems[("w", i)] = es.enter_context(nc.semaphore(f"w{i}"))
        for i in range(NSLOT):
            sems[("wb", i)] = es.enter_context(nc.semaphore(f"wb{i}"))
        for k in ("xin", "out", "cst"):
            sems[k] = es.enter_context(nc.semaphore(f"d_{k}"))

        def pvcol(name, c=0, n=1):
            o = CPK[name] + c
            return cpk[:, o:o + n]

        def act(out, in_, func, reads, writes, scale=1.0, bias=0.0):
            P.op("act", lambda h: h.activation(out=out, in_=in_, func=func, bias=bias, scale=scale),
                 reads, writes)

        def tt(eng, out, in0, in1, op, reads, writes):
            P.op(eng, lambda h: h.tensor_tensor(out=out, in0=in0, in1=in1, op=op), reads, writes)

        def stt(eng, out, in0, scalar, in1, op0, op1, reads, writes):
            P.op(eng, lambda h: h.scalar_tensor_tensor(out=out, in0=in0, scalar=scalar, in1=in1,
                                                       op0=op0, op1=op1), reads, writes)

        def ts(eng, out, in0, s1, s2, op0, op1, reads, writes):
            if op1 is None:
                P.op(eng, lambda h: h.tensor_scalar(out=out, in0=in0, scalar1=s1, scalar2=None, op0=op0),
                     reads, writes)
            else:
                P.op(eng, lambda h: h.tensor_scalar(out=out, in0=in0, scalar1=s1, scalar2=s2,
                                                    op0=op0, op1=op1), reads, writes)

        def mm(out, lhsT, rhs, start, stop, reads, writes, inc):
            P.op("pe", lambda h: h.matmul(out, lhsT, rhs, start=start, stop=stop), reads, writes, inc=inc)

        bank_rr = [0]

        pinned = set()

        def next_bank():
            while True:
                b = bank_rr[0]
                bank_rr[0] = (b + 1) % 8
                if b not in pinned:
                    return b

        tmp_rr = [0]
        cacc_rr = [0]

        def next_tmp():
            i = tmp_rr[0]
            tmp_rr[0] = (i + 1) % 4
            return i

        wstate = {"n": 0, "issued": 0}

        def ensure_issued(upto):
            while wstate["issued"] < min(upto, nchunk):
                kk = wstate["issued"]
                key, off, n = chunks[kk]
                s = kk % NSLOT
                P.dma("pool", (lambda o, nn, ss: (lambda h: h.dma_start(out=wslot[ss][:, 0:nn], in_=wpk[:, o:o + nn])))(off, n, s),
                      ("w", s), reads=(), writes=(("wslot", s),))
                P.dma("sp", (lambda o, nn, ss: (lambda h: h.dma_start(out=wsc[:, o:o + nn], in_=wslot[ss][:, 0:nn])))(off, n, s),
                      ("wb", s), reads=(("wslot", s),), writes=(("wsc", kk),))
                wstate["issued"] += 1

        def load_chunk(key, first_pass):
            ci, off, n = chunk_by_key[key]
            k = wstate["n"]
            wstate["n"] += 1
            assert k % nchunk == ci, (key, k, ci)
            s = k % NSLOT
            if k < nchunk:
                ensure_issued(k + LOOKAHEAD)
                return s
            P.dma("sp", (lambda o, nn, ss: (lambda h: h.dma_start(out=wslot[ss][:, 0:nn], in_=wsc[:, o:o + nn])))(off, n, s),
                  ("w", s), reads=(("wsc", ci),), writes=(("wslot", s),))
            return s


        def lhs(slot, nblk, k, blk):
            o = (k * nblk + blk) * 128
            return wslot[slot][:, o:o + 128]

        def layer_norm(T, gname, bname, outs, zbias=None, t_to_xres=False, mid=None, after_rstd=None):
            z, zres = zbuf, "zbuf"
            bm, be = next_bank(), next_bank()
            for c in range(CT):
                zbv = pvcol(zbias, c) if zbias is not None else 0.0
                act(zb[:, c, :T], z[:, c, :T], AF.Identity if zbias is not None else AF.Copy,
                    [(zres, c)], [("zb", c)], bias=zbv)
                act(zsq[:, c, :T], z[:, c, :T], AF.Square, [(zres, c)], [("zsq", c)], bias=zbv)
            act(scr[:, 0:1], pvcol("eps"), AF.Ln, [], ["scr"])
            for c in range(CT):
                mm(ps[bm][:, :T], ones[:, :], zb[:, c, :T], c == 0, c == CT - 1,
                   [("zb", c), "ones"], [("ps", bm)], inc=(c == CT - 1))
            for c in range(CT):
                mm(ps[be][:, :T], ones[:, :], zsq[:, c, :T], c == 0, c == CT - 1,
                   [("zsq", c), "ones"], [("ps", be)], inc=(c == CT - 1))
            if mid is not None:
                pinned.update((bm, be))
                mid()
                pinned.difference_update((bm, be))
            msq, var, rstd = (lnt[:, i, :T] for i in range(3))
            act(msq, ps[bm][:, :T], AF.Square, [("ps", bm)], [("lnt", 0)])
            tt("dve", var, ps[be][:, :T], msq, ALU.subtract, [("ps", be), ("lnt", 0)], [("lnt", 1)])
            act(var, var, AF.Ln, [("lnt", 1)], [("lnt", 1)], bias=pvcol("eps"))
            act(rstd, var, AF.Exp, [("lnt", 1)], [("lnt", 2)], scale=-0.5)
            slots = {}

            def p1(c):
                i = next_tmp()
                slots[c] = i
                t = tmp[:, i, :T]
                if zbias is not None:
                    stt("dve", t, z[:, c, :T], pvcol(zbias, c), ps[bm][:, :T], ALU.add, ALU.subtract,
                        [(zres, c), ("ps", bm)], [("tmp", i)])
                else:
                    tt("dve", t, z[:, c, :T], ps[bm][:, :T], ALU.subtract, [(zres, c), ("ps", bm)], [("tmp", i)])

            def p2(c):
                i = slots[c]
                t = tmp[:, i, :T]
                if t_to_xres:
                    tt("dve", xres[:, c, :T], t, rstd, ALU.mult, [("tmp", i), ("lnt", 2)], [("xres", c)])
                    src_ap, src_res = xres[:, c, :T], ("xres", c)
                else:
                    tt("dve", t, t, rstd, ALU.mult, [("tmp", i), ("lnt", 2)], [("tmp", i)])
                    src_ap, src_res = t, ("tmp", i)
                for (apf, func, res) in outs:
                    act(apf(c), src_ap, func, [src_res], [(res, c)],
                        scale=pvcol(gname, c), bias=pvcol(bname, c))
            p1(0)
            p1(1)
            if after_rstd is not None:
                after_rstd()
            for c in range(CT):
                p2(c)
                if c + 2 < CT:
                    p1(c + 2)

        xmode = {"ag": None, "ab": None}

        def resid(m, bank, T):
            sc = ALPHA if xmode["ag"] is None else pvcol(xmode["ag"], m)
            stt("dve", zbuf[:, m, :T], xres[:, m, :T], sc, ps[bank][:, :T], ALU.mult, ALU.add,
                [("xres", m), ("ps", bank)], [("zbuf", m)])

        def post_ln(T, li, w, final, after_rstd=None):
            gname, bname = ("ln%d_g" % w, li), ("ln%d_b" % w, li)
            zbias = xmode["ab"]
            if final:
                layer_norm(T, gname, bname, [(lambda c: zbuf[:, c, :T], AF.Identity, "zbuf")], zbias=zbias,
                           after_rstd=after_rstd)
            else:
                layer_norm(T, gname, bname, [(lambda c: xb[:, c, :T], AF.Identity, "xb")], zbias=zbias,
                           t_to_xres=True)
                xmode["ag"], xmode["ab"] = ("ag", li, w), ("ab", li, w)

        def run_units(T, units, rhs, rhs_res, nK, nko):
            def one(s, nblk, banks, k):
                for q in range(nblk):
                    mm(ps[banks[q]][:, :T], lhs(s, nblk, k, q), rhs(k), k == 0, k == nK - 1,
                       [(rhs_res, k), ("wslot", s)], [("ps", banks[q])], inc=(k == nK - 1))
            head = []
            for (key, nblk, cb) in units[:nko]:
                s = load_chunk(key, True)
                head.append((s, nblk, [next_bank() for _ in range(nblk)], cb))
            for k in range(nK):
                for (s, nblk, banks, cb) in head:
                    one(s, nblk, banks, k)
            for (s, nblk, banks, cb) in head:
                cb(banks)
            for (key, nblk, cb) in units[nko:]:
                s = load_chunk(key, True)
                banks = [next_bank() for _ in range(nblk)]
                for q in range(nblk):
                    for k in range(nK):
                        mm(ps[banks[q]][:, :T], lhs(s, nblk, k, q), rhs(k), k == 0, k == nK - 1,
                           [(rhs_res, k), ("wslot", s)], [("ps", banks[q])], inc=(k == nK - 1))
                cb(banks)

        def mix_out_and_resid(li, T):
            def mk(h):
                def cb(banks):
                    for q in range(2):
                        resid(2 * h + q, banks[q], T)
                return cb
            run_units(T, [(("mix_out", li, h), 2, mk(h)) for h in range(4)],
                      lambda k: ub[:, k, :T], "ub", CT, 2)

        def mixer_conf(li, ti, T):
            j = li // 3
            mixb = mix[:].bitcast(BF16)
            base = HOFF - (CONVW - 1)

            def glu(m, banks):
                ba, bg = banks
                i = next_tmp()
                act(tmp[:, i, :T], ps[bg][:, :T], AF.Sigmoid, [("ps", bg)], [("tmp", i)])
                act(mixb[:, m, base:HOFF], cstate[:, li, m, 0:CONVW - 1], AF.Copy,
                    [("cstate", li, m)], [("mix", m)])
                tt("dve", mixb[:, m, HOFF:HOFF + T], ps[ba][:, :T], tmp[:, i, :T], ALU.mult,
                   [("ps", ba), ("tmp", i)], [("mix", m)])

            def conv(m):
                sd = load_chunk(("a_dg", li, m), True)
                bc = next_bank()
                a = cacc_rr[0]
                cacc_rr[0] = (a + 1) % 4
                acc = cacc[:, a, :T]
                o = CPK[("a_dw", j)] + m * CONVW
                ts("dve", acc, mixb[:, m, base:base + T], cpk[:, o:o + 1], None, ALU.mult, None,
                   [("mix", m)], [("cacc", a)])
                for k in range(1, NDVE):
                    stt("dve", acc, mixb[:, m, base + k:base + k + T], cpk[:, o + k:o + k + 1], acc,
                        ALU.mult, ALU.add, [("mix", m), ("cacc", a)], [("cacc", a)])
                for k in range(NDVE, CONVW):
                    mm(ps[bc][:, :T], wslot[sd][:, (k - NDVE) * 128:(k - NDVE + 1) * 128],
                       mixb[:, m, base + k:base + k + T], k == NDVE, k == CONVW - 1, [("mix", m), ("wslot", sd)], [("ps", bc)], inc=(k == CONVW - 1))
                stt("dve", zbuf[:, m, :T], ps[bc][:, :T], pvcol(("a_dw_b", j), m), acc, ALU.add, ALU.add,
                    [("ps", bc), ("cacc", a)], [("zbuf", m)])
                act(cstate[:, li, m, 0:CONVW - 1], mixb[:, m, base + T:HOFF + T], AF.Copy, [("mix", m)],
                    [("cstate", li, m)], scale=(pvcol("flag") if ti == 0 else 1.0))

            rhs = lambda k: xb[:, k, :T]
            run_units(T, [(("a_in", li, m), 2, (lambda mm_: (lambda banks: glu(mm_, banks)))(m)) for m in range(2)],
                      rhs, "xb", CT, 2)
            for m in range(CT):
                if m + 2 < CT:
                    run_units(T, [(("a_in", li, m + 2), 2, (lambda mm_: (lambda banks: glu(mm_, banks)))(m + 2))],
                              rhs, "xb", CT, 0)
                conv(m)
            layer_norm(T, ("a_ln_g", j), ("a_ln_b", j), [(lambda c: ub[:, c, :T], AF.Silu, "ub")])
            mix_out_and_resid(li, T)

        def mixer_sgu(li, ti, T):
            nchk = T // 128
            vnT = hid

            def mk(i):
                def cb(banks):
                    for q in range(2):
                        b = banks[q]
                        if i < 4:
                            c = 2 * i + q
                            act(zbuf[:, c, :T], ps[b][:, :T], AF.Gelu, [("ps", b)], [("zbuf", c)])
                        else:
                            c = 2 * (i - 4) + q
                            act(mix[:, c, HOFF:HOFF + T], ps[b][:, :T], AF.Gelu, [("ps", b)], [("mix", c)])
                return cb
            run_units(T, [(("b_in", li, i), 2, mk(i)) for i in range(4)],
                      lambda k: xb[:, k, :T], "xb", CT, 3)
            layer_norm(T, "b_ln_g", "b_ln_b", [(lambda c: ub[:, c, :T], AF.Identity, "ub")],
                       mid=lambda: run_units(T, [(("b_in", li, i), 2, mk(i)) for i in range(4, CT)],
                                             lambda k: xb[:, k, :T], "xb", CT, 0))
            for h in range(CT):
                b = next_bank()
                pbf = ps[b][:].bitcast(BF16)
                for ck in range(nchk):
                    P.op("pe", (lambda o_, i_: (lambda hh: hh.transpose(o_, i_, identb[:, :])))(
                        pbf[:, ck * 128:(ck + 1) * 128], ub[:, h, ck * 128:(ck + 1) * 128]),
                        [("ub", h), "identb"], [("ps", b)], inc=(ck == nchk - 1))
                act(vnT[:, h, :T], pbf[:, :T], AF.Copy, [("ps", b)], [("hid", h)])
            for h in range(CT):
                b = next_bank()
                for ck in range(nchk):
                    mm(ps[b][:, ck * 128:(ck + 1) * 128], vnT[:, h, ck * 128:(ck + 1) * 128], wsTm[:, h, :],
                       True, True, [("hid", h), "wsTm"], [("ps", b)], inc=(ck == nchk - 1))
                i = next_tmp()
                o = CPK["bsb"] + h * 128
                for ck in range(nchk):
                    tt("dve", tmp[:, i, ck * 128:(ck + 1) * 128], ps[b][:, ck * 128:(ck + 1) * 128],
                       cpk[:, o:o + 128], ALU.add, [("ps", b)], [("tmp", i)])
                tt("dve", ub[:, h, :T], tmp[:, i, :T], mix[:, h, HOFF:HOFF + T], ALU.mult,
                   [("tmp", i), ("mix", h)], [("ub", h)])
            mix_out_and_resid(li, T)

        psc = sb("psc", [128, 2, HOFF + TMAX], F32)

        def mixer_pool(li, ti, T):
            H = 16

            def mk(i):
                def cb(banks):
                    for q in range(2):
                        c = 2 * i + q
                        b = banks[q]
                        act(mix[:, c, HOFF - H:HOFF], cstate[:, li, c, 0:H], AF.Copy, [("cstate", li, c)], [("mix", c)])
                        act(mix[:, c, HOFF:HOFF + T], ps[b][:, :T], AF.Copy, [("ps", b)], [("mix", c)])
                return cb
            run_units(T, [(("c_in", li, i), 2, mk(i)) for i in range(4)],
                      lambda k: xb[:, k, :T], "xb", CT, 3)
            E = HOFF + T
            for g in range(4):
                w = 2 << g
                for q in range(2):
                    c = 2 * g + q
                    cur = mix[:, c, :]
                    cur_res = ("mix", c)
                    lo = HOFF - H
                    step = 1
                    pi = 0
                    while step < w:
                        nlo = lo + step
                        dst = psc[:, pi, :]
                        tt("dve", dst[:, nlo:E], cur[:, nlo:E], cur[:, nlo - step:E - step], ALU.add,
                           [cur_res], [("psc", pi)])
                        cur, cur_res, lo = dst, ("psc", pi), nlo
                        pi ^= 1
                        step *= 2
                    stt("dve", zb[:, c, :T], cur[:, HOFF:E], 1.0 / w, mix[:, c, HOFF:E], ALU.mult, ALU.subtract,
                        [cur_res, ("mix", c)], [("zb", c)])
                    if ti == 1:
                        i = next_tmp()
                        o = CPK["icnt"] + g * 16
                        tt("dve", tmp[:, i, 0:16], cur[:, HOFF:HOFF + 16], cpk[:, o:o + 16], ALU.mult,
                           [cur_res], [("tmp", i)])
                        tt("dve", zb[:, c, 0:16], tmp[:, i, 0:16], mix[:, c, HOFF:HOFF + 16], ALU.subtract,
                           [("tmp", i), ("mix", c), ("zb", c)], [("zb", c)])
                    act(cstate[:, li, c, 0:H], mix[:, c, E - H:E], AF.Copy, [("mix", c)], [("cstate", li, c)],
                        scale=(pvcol("flag") if ti == 0 else 1.0))
            s = load_chunk(("c_grp", li), True)
            for g in range(4):
                for dd in range(2):
                    c = 2 * g + dd
                    b = next_bank()
                    for kk in range(2):
                        o = ((g * 2 + kk) * 2 + dd) * 128
                        mm(ps[b][:, :T], wslot[s][:, o:o + 128], zb[:, 2 * g + kk, :T], kk == 0, kk == 1,
                           [("zb", 2 * g + kk), ("wslot", s)], [("ps", b)], inc=(kk == 1))
                    act(ub[:, c, :T], ps[b][:, :T], AF.Copy, [("ps", b)], [("ub", c)], scale=pvcol("c_scale", c))
            mix_out_and_resid(li, T)

        def ffn(li, ti, T):
            par = ti % 2
            fo = CPK[("fdw", li)]
            hrow = 2 * li + par

            def wcol(kk, c):
                o = fo + kk * 2 * NP + c
                return cpk[:, o:o + 1]

            def wrow(kk):
                o = fo + kk * 2 * NP
                return cpk[:, o:o + 2 * NP]
            hres = [("hp", li, par, c) for c in range(2 * NP)]
            tt("pool", edge[:, :, 0], hp[:, hrow, :, 1], wrow(1), ALU.mult, hres, ["edge"])
            tt("pool", edge[:, :, 2], hp[:, hrow, :, 0], wrow(0), ALU.mult, hres, ["edge"])
            tt("pool", edge[:, :, 0], edge[:, :, 0], edge[:, :, 2], ALU.add, ["edge"], ["edge"])
            tt("pool", edge[:, :, 1], hp[:, hrow, :, 1], wrow(0), ALU.mult, hres, ["edge"])

            def mk(j):
                def cb(banks):
                    cs = (j, NP + j)
                    ai = ((2 * j) % 4, (2 * j + 1) % 4)
                    for q in range(2):
                        b, c, a = banks[q], cs[q], ai[q]
                        acc = cacc[:, a, :]
                        act(acc[:, 2:T], ps[b][:, 2:T], AF.Copy, [("ps", b)], [("cacc", a)], scale=wcol(2, c))
                        act(acc[:, 0:1], ps[b][:, 0:1], AF.Identity, [("ps", b), "edge"], [("cacc", a)],
                            scale=wcol(2, c), bias=edge[:, c, 0:1])
                        act(acc[:, 1:2], ps[b][:, 1:2], AF.Identity, [("ps", b), "edge"], [("cacc", a)],
                            scale=wcol(2, c), bias=edge[:, c, 1:2])
                        stt("dve", acc[:, 1:T], ps[b][:, 0:T - 1], wcol(1, c), acc[:, 1:T], ALU.mult, ALU.add,
                            [("ps", b), ("cacc", a)], [("cacc", a)])
                        stt("dve", acc[:, 2:T], ps[b][:, 0:T - 2], wcol(0, c), acc[:, 2:T], ALU.mult, ALU.add,
                            [("ps", b), ("cacc", a)], [("cacc", a)])
                        act(hp[:, 2 * li + 1 - par, c, :], ps[b][:, T - 2:T], AF.Copy, [("ps", b)],
                            [("hp", li, 1 - par, c)], scale=(pvcol("flag") if ti == 0 else 1.0))
                    i = next_tmp()
                    act(tmp[:, i, :T], cacc[:, ai[0], :T], AF.Silu, [("cacc", ai[0])], [("tmp", i)])
                    tt("pool", hid[:, j, :T], tmp[:, i, :T], cacc[:, ai[1], :T], ALU.mult,
                       [("tmp", i), ("cacc", ai[1])], [("hid", j)])
                return cb
            run_units(T, [(("f_up", li, j), 2, mk(j)) for j in range(NP)],
                      lambda k: xb[:, k, :T], "xb", CT, 3)

            def mkd(m):
                def cb(banks):
                    resid(m, banks[0], T)
                return cb
            run_units(T, [(("f_down", li, m), 1, mkd(m)) for m in range(CT)],
                      lambda k: hid[:, k, :T], "hid", NP, 0)

        all_x = [("xres", c) for c in range(CT)]
        P.dma("sp", lambda h: h.dma_start(out=cpk[:, :], in_=cpk_d[:, :]), "cst", writes=("cpk",))
        zflat = zbuf[:].rearrange("p c t -> p (c t)")
        P.dma("sp", lambda h: h.dma_start(out=zflat[:, 0:STG_COLS], in_=stg_d[:, :]), "cst",
              writes=[("zbuf", c) for c in range(CT)])
        cst_tag = ("cst", 32)
        for e_ in ("pe", "act", "dve", "pool"):
            P.wait(e_, cst_tag)
        P.op("dve", lambda h: h.memset(ones[:, :], 1.0 / D), (), ("ones",))
        P.op("dve", lambda h: h.memset(hp[:].rearrange("p a c t -> p (a c t)"), 0.0), (),
             [("hp", l_, a, c) for l_ in range(DEPTH) for a in range(2) for c in range(2 * NP)])
        P.op("dve", lambda h: h.memset(cstate[:].rearrange("p a c t -> p (a c t)"), 0.0), (),
             [("cstate", l_, c) for l_ in range(DEPTH) for c in range(CT)])
        for c in range(CT):
            P.op("dve", (lambda cc: (lambda h: h.memset(mix[:, cc, :], 0.0)))(c), (), (("mix", c),))
        zr = [("zbuf", c) for c in range(CT)] + ["cpk"]
        for h_ in range(CT):
            tt("dve", wsTm[:, h_, :], zflat[:, h_ * 128:(h_ + 1) * 128], zflat[:, CT * 128:CT * 128 + 128],
               ALU.mult, zr, ("wsTm",))
        P.op("dve", lambda h: h.tensor_copy(out=identb[:, :], in_=zflat[:, CT * 128 + 128:CT * 128 + 256]),
             zr, ("identb",))

        n_tiles = len(tiles)

        def load_x(ti):
            t0_, T_ = tiles[ti]
            P.dma("sp", (lambda a_, b_: (lambda h: h.dma_start(out=xres[:, :, :b_], in_=xin[:, :, a_:a_ + b_])))(t0_, T_),
                  "xin", reads=(), writes=all_x)

        for li in range(DEPTH):
            for w in (1, 2):
                for (dst, srcn) in ((("ag", li, w), ("ln%d_g" % w, li)), (("ab", li, w), ("ln%d_b" % w, li))):
                    o_d, o_s = CPK[dst], CPK[srcn]
                    ts("dve", cpk[:, o_d:o_d + CT], cpk[:, o_s:o_s + CT], ALPHA, None, ALU.mult, None,
                       ["cpk"], ["cpk"])
        load_x(0)
        cast_done = set()

        def cast_xb(ti_):
            T_ = tiles[ti_][1]
            for c in range(CT):
                act(xb[:, c, :T_], xres[:, c, :T_], AF.Copy, [("xres", c), "cpk"], [("xb", c)])
            cast_done.add(ti_)

        for ti, (t0, T) in enumerate(tiles):
            stored = ti >= n_tiles - n_store_tiles
            xmode["ag"], xmode["ab"] = None, None
            if ti not in cast_done:
                cast_xb(ti)
            for li_idx, li in enumerate(layers):
                last_layer = li_idx == len(layers) - 1
                kind = li % 3
                if kind == 0:
                    mixer_conf(li, ti, T)
                elif kind == 1:
                    mixer_sgu(li, ti, T)
                else:
                    mixer_pool(li, ti, T)
                post_ln(T, li, 1, False)
                ffn(li, ti, T)
                if last_layer:
                    if ti + 1 < n_tiles:
                        load_x(ti + 1)
                    if stored:
                        post_ln(T, li, 2, True,
                                after_rstd=((lambda t_=ti + 1: cast_xb(t_)) if ti + 1 < n_tiles else None))
                else:
                    post_ln(T, li, 2, False)
            if stored:
                o0 = sum(tt_[1] for tt_ in tiles[n_tiles - n_store_tiles:ti])
                P.dma("act", (lambda a, b_: (lambda h: h.dma_start(out=yout[:, :, a:a + b_], in_=zbuf[:, :, :b_])))(o0, T),
                      "out", reads=[("zbuf", c) for c in range(CT)], writes=())
        P.wait("act", ("out", P.dmacnt.get("out", 0)))

        handles = {}

        def emit(ename, h):
            for item in P.streams[ename]:
                if item[0] == "wait":
                    h.wait_ge(sems[item[1]], item[2])
                elif item[0] == "op":
                    ins = item[1](h)
                    if item[2] is not None:
                        ins.then_inc(sems[item[2]], 1)
                else:
                    item[1](h).then_inc(sems[item[2]], 16)

        with nc.Block() as block:
            @block.sync
            def _(h):
                emit("sp", h)

            @block.gpsimd
            def _(h):
                emit("pool", h)

            @block.scalar
            def _(h):
                emit("act", h)

            @block.vector
            def _(h):
                emit("dve", h)

            @block.tensor
            def _(h):
                emit("pe", h)
    return nc


def make_tiles(tok_per_core):
    tiles = [(0, HALO)]
    t = HALO
    while t < HALO + tok_per_core:
        T = min(TMAX, HALO + tok_per_core - t)
        tiles.append((t, T))
        t += T
    return tiles


def run(inputs, layers=(0, 1, 2, 3), trace=False):
    x = np.asarray(inputs["x"], np.float32)
    inp = {k: np.asarray(v, np.float32) for k, v in inputs.items()}
    B, S, _ = x.shape
    cores_per_seq = NCORES // B
    tpc = S // cores_per_seq
    tiles = make_tiles(tpc)
    n_store = len(tiles) - 1
    layers = list(layers)
    nc = build_program(layers, tiles, n_store, HALO + tpc, tpc)
    wpk = pack_weights(inp, layers)
    stg = pack_stage(inp)
    in_maps = []
    for core in range(NCORES):
        b, seg = divmod(core, cores_per_seq)
        p0 = seg * tpc
        xs = np.zeros((HALO + tpc, D), np.float32)
        if seg == 0:
            xs[HALO:] = x[b, 0:tpc]
        else:
            xs[:] = x[b, p0 - HALO:p0 + tpc]
        xin = np.ascontiguousarray(xs.reshape(HALO + tpc, CT, 128).transpose(2, 1, 0))
        in_maps.append({"xin": xin, "wpk": wpk, "cpk": pack_cpk(inp, seg == 0), "stg": stg})
    res = run_bass_kernel_spmd(nc, in_maps, core_ids=list(range(NCORES)), trace=trace)
    out = np.zeros((B, S, D), np.float32)
    for core in range(NCORES):
        b, seg = divmod(core, cores_per_seq)
        y = res.results[core]["yout"]
        out[b, seg * tpc:(seg + 1) * tpc] = y.transpose(2, 1, 0).reshape(tpc, D)
    return out, res


def kernel(**inputs):
    out, _ = run(inputs)
    return out
```

```python
import numpy as np
import concourse.bass as bass
import concourse.mybir as mybir
from concourse.bass_utils import run_bass_kernel_spmd

F32 = mybir.dt.float32
BF16 = mybir.dt.bfloat16
AF = mybir.ActivationFunctionType
ALU = mybir.AluOpType

D = 1024
CT = 8
DFF = 2816
NP = 22
DEPTH = 4
NCORES = 8
HALO = 256
TMAX = 512
ALPHA = float((2 * DEPTH) ** 0.25)
EPS = 1e-5
SLOT_COLS = 3072
NSLOT = 7
LOOKAHEAD = 5
NCAST_SEM = 8
CAST_AHEAD = 10
CONVW = 31
NDVE = 7
HOFF = 32


class Layout:
    def __init__(self):
        self.off = {}
        self.n = 0

    def add(self, name, ncols):
        self.off[name] = (self.n, ncols)
        self.n += ncols

    def __getitem__(self, name):
        return self.off[name][0]


def make_cpk_layout():
    L = Layout()
    for li in range(DEPTH):
        for nm in ("ln1_g", "ln1_b", "ln2_g", "ln2_b"):
            L.add((nm, li), CT)
        L.add(("fdw", li), 3 * 2 * NP)
    for j in range(2):
        L.add(("a_dw", j), CT * CONVW)
        L.add(("a_dw_b", j), CT)
        L.add(("a_ln_g", j), CT)
        L.add(("a_ln_b", j), CT)
    L.add("b_ln_g", CT)
    L.add("b_ln_b", CT)
    L.add("c_scale", CT)
    L.add("bsb", CT * 128)
    L.add("flag", 1)
    L.add("eps", 1)
    L.add("icnt", 4 * 16)
    for li in range(DEPTH):
        for w in (1, 2):
            L.add(("ag", li, w), CT)
            L.add(("ab", li, w), CT)
    return L


CPK = make_cpk_layout()
STG_COLS = CT * 128 + 128 + 128


def layer_chunks(li):
    kind = li % 3
    ch = []
    if kind == 0:
        ch.append((("a_in", li, 0), 8 * 2 * 128))
        ch.append((("a_in", li, 1), 8 * 2 * 128))
        for m in range(CT):
            if m + 2 < CT:
                ch.append((("a_in", li, m + 2), 8 * 2 * 128))
            ch.append((("a_dg", li, m), (CONVW - NDVE) * 128))
        for h in range(4):
            ch.append((("mix_out", li, h), 8 * 2 * 128))
    elif kind == 1:
        for i in range(CT):
            ch.append((("b_in", li, i), 8 * 2 * 128))
        for h in range(4):
            ch.append((("mix_out", li, h), 8 * 2 * 128))
    else:
        for i in range(4):
            ch.append((("c_in", li, i), 8 * 2 * 128))
        ch.append((("c_grp", li), 4 * 2 * 2 * 128))
        for h in range(4):
            ch.append((("mix_out", li, h), 8 * 2 * 128))
    for j in range(NP):
        ch.append((("f_up", li, j), 8 * 2 * 128))
    for m in range(CT):
        ch.append((("f_down", li, m), NP * 128))
    return ch


def all_chunks(layers):
    off = 0
    out = []
    for li in layers:
        for key, n in layer_chunks(li):
            out.append((key, off, n))
            off += n
    return out, off


def pack_blocks(W, blocks):
    K = W.shape[0]
    kt = K // 128
    Wr = W.reshape(kt, 128, W.shape[1] // 128, 128)
    sel = Wr[:, :, blocks, :]
    return np.ascontiguousarray(sel.transpose(1, 0, 2, 3)).reshape(128, -1)


def pack_weights(inp, layers):
    chunks, total = all_chunks(layers)
    wpk = np.zeros((128, total), np.float32)
    for key, off, n in chunks:
        kind = key[0]
        li = key[1]
        j = li // 3
        if kind == "a_in":
            m = key[2]
            v = pack_blocks(inp["a_w_in"][j], [m, CT + m])
        elif kind == "a_dg":
            m = key[2]
            dw = inp["a_dw"][j][:, m * 128:(m + 1) * 128]
            v = np.zeros((128, CONVW, 128), np.float32)
            v[np.arange(128), :, np.arange(128)] = dw.T
            v = np.ascontiguousarray(v[:, NDVE:, :]).reshape(128, -1)
        elif kind == "b_in":
            i = key[2]
            if i < 4:
                v = pack_blocks(inp["b_w_in"][j], [CT + 2 * i, CT + 2 * i + 1])
            else:
                v = pack_blocks(inp["b_w_in"][j], [2 * (i - 4), 2 * (i - 4) + 1])
        elif kind == "c_in":
            i = key[2]
            v = pack_blocks(inp["c_w_in"][j], [2 * i, 2 * i + 1])
        elif kind == "c_grp":
            G = inp["c_w_grp"][j]
            Gr = G.reshape(4, 2, 128, 2, 128)
            v = np.ascontiguousarray(Gr.transpose(2, 0, 1, 3, 4)).reshape(128, -1)
        elif kind == "mix_out":
            h = key[2]
            W = {0: inp["a_w_out"], 1: inp["b_w_out"], 2: inp["c_w_out"]}[li % 3][j]
            v = pack_blocks(W, [2 * h + q for q in range(2)])
        elif kind == "f_up":
            v = pack_blocks(inp["f_w_up"][li], [key[2], NP + key[2]])
        elif kind == "f_down":
            v = pack_blocks(inp["f_w_down"][li], [key[2]])
        assert v.shape[1] == n, (key, v.shape, n)
        wpk[:, off:off + n] = v
    return wpk


def vec8(v):
    return np.ascontiguousarray(v.reshape(-1, 128).T)


def pack_cpk(inp, core_is_start):
    c = np.zeros((128, CPK.n), np.float32)

    def put(name, arr):
        o, n = CPK.off[name]
        assert arr.shape == (128, n), (name, arr.shape, n)
        c[:, o:o + n] = arr

    for li in range(DEPTH):
        for nm in ("ln1_g", "ln1_b", "ln2_g", "ln2_b"):
            put((nm, li), vec8(inp[nm][li]))
        fd = inp["f_dw"][li]
        put(("fdw", li), np.ascontiguousarray(
            fd.reshape(3, 2 * NP, 128).transpose(2, 0, 1)).reshape(128, -1))
    for j in range(2):
        dw = inp["a_dw"][j]
        put(("a_dw", j), np.ascontiguousarray(
            dw.reshape(CONVW, CT, 128).transpose(2, 1, 0)).reshape(128, -1))
        put(("a_dw_b", j), vec8(inp["a_dw_b"][j]))
        put(("a_ln_g", j), vec8(inp["a_ln_g"][j]))
        put(("a_ln_b", j), vec8(inp["a_ln_b"][j]))
    put("b_ln_g", vec8(inp["b_ln_g"][0]))
    put("b_ln_b", vec8(inp["b_ln_b"][0]))
    put("c_scale", vec8(inp["c_scale"][0]))
    put("bsb", np.broadcast_to(inp["b_bs"][0].reshape(1, CT * 128), (128, CT * 128)))
    put("flag", np.full((128, 1), 0.0 if core_is_start else 1.0, np.float32))
    put("eps", np.full((128, 1), EPS, np.float32))
    ic = np.zeros((4, 16), np.float32)
    for g, w in enumerate((2, 4, 8, 16)):
        for t in range(16):
            ic[g, t] = 1.0 / (min(t + 1, w) if core_is_start else w)
    put("icnt", np.broadcast_to(ic.reshape(1, 64), (128, 64)))
    return c


def pack_stage(inp):
    s = np.zeros((128, STG_COLS), np.float32)
    ws = inp["b_ws"][0]
    s[:, :CT * 128] = np.ascontiguousarray(ws.transpose(2, 0, 1)).reshape(128, CT * 128)
    idx = np.arange(128)
    s[:, CT * 128:CT * 128 + 128] = (idx[:, None] <= idx[None, :]).astype(np.float32)
    s[:, CT * 128 + 128:] = np.eye(128, dtype=np.float32)
    return s


class Planner:
    def __init__(self):
        self.streams = {e: [] for e in ("pe", "act", "dve", "pool", "sp")}
        self.cnt = {e: 0 for e in self.streams}
        self.seen = {e: {} for e in self.streams}
        self.lastw = {}
        self.readers = {}
        self.dmacnt = {}

    def _deps(self, eng, reads, writes):
        need = {}

        def add(d, is_raw):
            if d is None:
                return
            k, v = d
            if k == eng and (eng == "pe" or not is_raw):
                return
            if need.get(k, 0) < v:
                need[k] = v
        for r in reads:
            add(self.lastw.get(r), True)
        for r in writes:
            add(self.lastw.get(r), False)
            for rd in self.readers.get(r, ()):
                add(rd, False)
        for k, v in need.items():
            if self.seen[eng].get(k, 0) < v:
                self.seen[eng][k] = v
                self.streams[eng].append(("wait", k, v))

    def _record(self, tag, reads, writes):
        for r in writes:
            self.lastw[r] = tag
            self.readers[r] = []
        for r in reads:
            self.readers.setdefault(r, []).append(tag)

    def op(self, eng, fn, reads=(), writes=(), inc=True):
        self._deps(eng, reads, writes)
        if inc:
            self.cnt[eng] += 1
            tag = (eng, self.cnt[eng])
        else:
            tag = (eng, self.cnt[eng] + 1)
        self.streams[eng].append(("op", fn, eng if inc else None))
        self._record(tag, reads, writes)

    def dma(self, eng, fn, semkey, reads=(), writes=(), extra_waits=()):
        self._deps(eng, reads, writes)
        for k, v in extra_waits:
            if self.seen[eng].get(k, 0) < v:
                self.seen[eng][k] = v
                self.streams[eng].append(("wait", k, v))
        self.dmacnt[semkey] = self.dmacnt.get(semkey, 0) + 16
        tag = (semkey, self.dmacnt[semkey])
        self.streams[eng].append(("dma", fn, semkey))
        self._record(tag, reads, writes)
        return tag

    def wait(self, eng, tag):
        k, v = tag
        if self.seen[eng].get(k, 0) < v:
            self.seen[eng][k] = v
            self.streams[eng].append(("wait", k, v))


def build_program(layers, tiles, n_store_tiles, tok_in, tok_out):
    nc = bass.Bass("TRN2", target_bir_lowering=False)
    chunks, wcols = all_chunks(layers)
    chunk_by_key = {k: (i, off, n) for i, (k, off, n) in enumerate(chunks)}
    nchunk = len(chunks)

    xin = nc.dram_tensor("xin", [128, CT, tok_in], F32, kind="ExternalInput").ap()
    wpk = nc.dram_tensor("wpk", [128, wcols], F32, kind="ExternalInput").ap()
    cpk_d = nc.dram_tensor("cpk", [128, CPK.n], F32, kind="ExternalInput").ap()
    stg_d = nc.dram_tensor("stg", [128, STG_COLS], F32, kind="ExternalInput").ap()
    yout = nc.dram_tensor("yout", [128, CT, tok_out], F32, kind="ExternalOutput").ap()
    wsc = nc.dram_tensor("wsc", [128, wcols], BF16, kind="Internal").ap()

    P = Planner()
    import contextlib
    es = contextlib.ExitStack()
    with es:
        def sb(name, shape, dt):
            return es.enter_context(nc.sbuf_tensor(name, shape, dt))

        xres = sb("xres", [128, CT, TMAX], F32)
        xb = sb("xb", [128, CT, TMAX], BF16)
        zbuf = sb("zbuf", [128, CT, TMAX], F32)
        zb = sb("zb", [128, CT, TMAX], BF16)
        zsq = sb("zsq", [128, CT, TMAX], BF16)
        ub = sb("ub", [128, CT, TMAX], BF16)
        hid = sb("hid", [128, NP, TMAX], BF16)
        mix = sb("mix", [128, CT, HOFF + TMAX], F32)
        lnt = sb("lnt", [128, 3, TMAX], F32)
        tmp = sb("tmp", [128, 4, TMAX], F32)
        cacc = sb("cacc", [128, 4, TMAX], F32)
        cpk = sb("cpk_sb", [128, CPK.n], F32)
        wsTm = sb("wsTm", [128, CT, 128], BF16)
        identb = sb("identb", [128, 128], BF16)
        ones = sb("ones", [128, 128], BF16)
        hp = sb("hp", [128, DEPTH * 2, 2 * NP, 2], F32)
        cstate = sb("cstate", [128, DEPTH, CT, 32], F32)
        edge = sb("edge", [128, 2 * NP, 3], F32)
        scr = sb("scr", [128, 8], F32)
        wslot = [sb(f"wslot{i}", [128, SLOT_COLS], BF16) for i in range(NSLOT)]
        ps = [es.enter_context(nc.psum_tensor(f"ps{i}", [128, TMAX], F32)) for i in range(8)]

        sems = {}
        for e in ("pe", "act", "dve", "pool", "sp"):
            sems[e] = es.enter_context(nc.semaphore(f"c_{e}"))
        for i in range(NSLOT):
            sems[("w", i)] = es.enter_context(nc.semaphore(f"w{i}"))
        for i in range(NSLOT):
            sems[("wb", i)] = es.enter_context(nc.semaphore(f"wb{i}"))
        for k in ("xin", "out", "cst"):
            sems[k] = es.enter_context(nc.semaphore(f"d_{k}"))

        def pvcol(name, c=0, n=1):
            o = CPK[name] + c
            return cpk[:, o:o + n]

        def act(out, in_, func, reads, writes, scale=1.0, bias=0.0):
            P.op("act", lambda h: h.activation(out=out, in_=in_, func=func, bias=bias, scale=scale),
                 reads, writes)

        def tt(eng, out, in0, in1, op, reads, writes):
            P.op(eng, lambda h: h.tensor_tensor(out=out, in0=in0, in1=in1, op=op), reads, writes)

        def stt(eng, out, in0, scalar, in1, op0, op1, reads, writes):
            P.op(eng, lambda h: h.scalar_tensor_tensor(out=out, in0=in0, scalar=scalar, in1=in1,
                                                       op0=op0, op1=op1), reads, writes)

        def ts(eng, out, in0, s1, s2, op0, op1, reads, writes):
            if op1 is None:
                P.op(eng, lambda h: h.tensor_scalar(out=out, in0=in0, scalar1=s1, scalar2=None, op0=op0),
                     reads, writes)
            else:
                P.op(eng, lambda h: h.tensor_scalar(out=out, in0=in0, scalar1=s1, scalar2=s2,
                                                    op0=op0, op1=op1), reads, writes)

        def mm(out, lhsT, rhs, start, stop, reads, writes, inc):
            P.op("pe", lambda h: h.matmul(out, lhsT, rhs, start=start, stop=stop), reads, writes, inc=inc)

        bank_rr = [0]

        pinned = set()

        def next_bank():
            while True:
                b = bank_rr[0]
                bank_rr[0] = (b + 1) % 8
                if b not in pinned:
                    return b

        tmp_rr = [0]
        cacc_rr = [0]

        def next_tmp():
            i = tmp_rr[0]
            tmp_rr[0] = (i + 1) % 4
            return i

        wstate = {"n": 0, "issued": 0}

        def ensure_issued(upto):
            while wstate["issued"] < min(upto, nchunk):
                kk = wstate["issued"]
                key, off, n = chunks[kk]
                s = kk % NSLOT
                P.dma("pool", (lambda o, nn, ss: (lambda h: h.dma_start(out=wslot[ss][:, 0:nn], in_=wpk[:, o:o + nn])))(off, n, s),
                      ("w", s), reads=(), writes=(("wslot", s),))
                P.dma("sp", (lambda o, nn, ss: (lambda h: h.dma_start(out=wsc[:, o:o + nn], in_=wslot[ss][:, 0:nn])))(off, n, s),
                      ("wb", s), reads=(("wslot", s),), writes=(("wsc", kk),))
                wstate["issued"] += 1

        def load_chunk(key, first_pass):
            ci, off, n = chunk_by_key[key]
            k = wstate["n"]
            wstate["n"] += 1
            assert k % nchunk == ci, (key, k, ci)
            s = k % NSLOT
            if k < nchunk:
                ensure_issued(k + LOOKAHEAD)
                return s
            P.dma("sp", (lambda o, nn, ss: (lambda h: h.dma_start(out=wslot[ss][:, 0:nn], in_=wsc[:, o:o + nn])))(off, n, s),
                  ("w", s), reads=(("wsc", ci),), writes=(("wslot", s),))
            return s


        def lhs(slot, nblk, k, blk):
            o = (k * nblk + blk) * 128
            return wslot[slot][:, o:o + 128]

        def layer_norm(T, gname, bname, outs, zbias=None, t_to_xres=False, mid=None, after_rstd=None):
            z, zres = zbuf, "zbuf"
            bm, be = next_bank(), next_bank()
            for c in range(CT):
                zbv = pvcol(zbias, c) if zbias is not None else 0.0
                act(zb[:, c, :T], z[:, c, :T], AF.Identity if zbias is not None else AF.Copy,
                    [(zres, c)], [("zb", c)], bias=zbv)
                act(zsq[:, c, :T], z[:, c, :T], AF.Square, [(zres, c)], [("zsq", c)], bias=zbv)
            act(scr[:, 0:1], pvcol("eps"), AF.Ln, [], ["scr"])
            for c in range(CT):
                mm(ps[bm][:, :T], ones[:, :], zb[:, c, :T], c == 0, c == CT - 1,
                   [("zb", c), "ones"], [("ps", bm)], inc=(c == CT - 1))
            for c in range(CT):
                mm(ps[be][:, :T], ones[:, :], zsq[:, c, :T], c == 0, c == CT - 1,
                   [("zsq", c), "ones"], [("ps", be)], inc=(c == CT - 1))
            if mid is not None:
                pinned.update((bm, be))
                mid()
                pinned.difference_update((bm, be))
            msq, var, rstd = (lnt[:, i, :T] for i in range(3))
            act(msq, ps[bm][:, :T], AF.Square, [("ps", bm)], [("lnt", 0)])
            tt("dve", var, ps[be][:, :T], msq, ALU.subtract, [("ps", be), ("lnt", 0)], [("lnt", 1)])
            act(var, var, AF.Ln, [("lnt", 1)], [("lnt", 1)], bias=pvcol("eps"))
            act(rstd, var, AF.Exp, [("lnt", 1)], [("lnt", 2)], scale=-0.5)
            slots = {}

            def p1(c):
                i = next_tmp()
                slots[c] = i
                t = tmp[:, i, :T]
                if zbias is not None:
                    stt("dve", t, z[:, c, :T], pvcol(zbias, c), ps[bm][:, :T], ALU.add, ALU.subtract,
                        [(zres, c), ("ps", bm)], [("tmp", i)])
                else:
                    tt("dve", t, z[:, c, :T], ps[bm][:, :T], ALU.subtract, [(zres, c), ("ps", bm)], [("tmp", i)])

            def p2(c):
                i = slots[c]
                t = tmp[:, i, :T]
                if t_to_xres:
                    tt("dve", xres[:, c, :T], t, rstd, ALU.mult, [("tmp", i), ("lnt", 2)], [("xres", c)])
                    src_ap, src_res = xres[:, c, :T], ("xres", c)
                else:
                    tt("dve", t, t, rstd, ALU.mult, [("tmp", i), ("lnt", 2)], [("tmp", i)])
                    src_ap, src_res = t, ("tmp", i)
                for (apf, func, res) in outs:
                    act(apf(c), src_ap, func, [src_res], [(res, c)],
                        scale=pvcol(gname, c), bias=pvcol(bname, c))
            p1(0)
            p1(1)
            if after_rstd is not None:
                after_rstd()
            for c in range(CT):
                p2(c)
                if c + 2 < CT:
                    p1(c + 2)

        xmode = {"ag": None, "ab": None}

        def resid(m, bank, T):
            sc = ALPHA if xmode["ag"] is None else pvcol(xmode["ag"], m)
            stt("dve", zbuf[:, m, :T], xres[:, m, :T], sc, ps[bank][:, :T], ALU.mult, ALU.add,
                [("xres", m), ("ps", bank)], [("zbuf", m)])

        def post_ln(T, li, w, final, after_rstd=None):
            gname, bname = ("ln%d_g" % w, li), ("ln%d_b" % w, li)
            zbias = xmode["ab"]
            if final:
                layer_norm(T, gname, bname, [(lambda c: zbuf[:, c, :T], AF.Identity, "zbuf")], zbias=zbias,
                           after_rstd=after_rstd)
            else:
                layer_norm(T, gname, bname, [(lambda c: xb[:, c, :T], AF.Identity, "xb")], zbias=zbias,
                           t_to_xres=True)
                xmode["ag"], xmode["ab"] = ("ag", li, w), ("ab", li, w)

        def run_units(T, units, rhs, rhs_res, nK, nko):
            def one(s, nblk, banks, k):
                for q in range(nblk):
                    mm(ps[banks[q]][:, :T], lhs(s, nblk, k, q), rhs(k), k == 0, k == nK - 1,
                       [(rhs_res, k), ("wslot", s)], [("ps", banks[q])], inc=(k == nK - 1))
            head = []
            for (key, nblk, cb) in units[:nko]:
                s = load_chunk(key, True)
                head.append((s, nblk, [next_bank() for _ in range(nblk)], cb))
            for k in range(nK):
                for (s, nblk, banks, cb) in head:
                    one(s, nblk, banks, k)
            for (s, nblk, banks, cb) in head:
                cb(banks)
            for (key, nblk, cb) in units[nko:]:
                s = load_chunk(key, True)
                banks = [next_bank() for _ in range(nblk)]
                for q in range(nblk):
                    for k in range(nK):
                        mm(ps[banks[q]][:, :T], lhs(s, nblk, k, q), rhs(k), k == 0, k == nK - 1,
                           [(rhs_res, k), ("wslot", s)], [("ps", banks[q])], inc=(k == nK - 1))
                cb(banks)

        def mix_out_and_resid(li, T):
            def mk(h):
                def cb(banks):
                    for q in range(2):
                        resid(2 * h + q, banks[q], T)
                return cb
            run_units(T, [(("mix_out", li, h), 2, mk(h)) for h in range(4)],
                      lambda k: ub[:, k, :T], "ub", CT, 2)

        def mixer_conf(li, ti, T):
            j = li // 3
            mixb = mix[:].bitcast(BF16)
            base = HOFF - (CONVW - 1)

            def glu(m, banks):
                ba, bg = banks
                i = next_tmp()
                act(tmp[:, i, :T], ps[bg][:, :T], AF.Sigmoid, [("ps", bg)], [("tmp", i)])
                act(mixb[:, m, base:HOFF], cstate[:, li, m, 0:CONVW - 1], AF.Copy,
                    [("cstate", li, m)], [("mix", m)])
                tt("dve", mixb[:, m, HOFF:HOFF + T], ps[ba][:, :T], tmp[:, i, :T], ALU.mult,
                   [("ps", ba), ("tmp", i)], [("mix", m)])

            def conv(m):
                sd = load_chunk(("a_dg", li, m), True)
                bc = next_bank()
                a = cacc_rr[0]
                cacc_rr[0] = (a + 1) % 4
                acc = cacc[:, a, :T]
                o = CPK[("a_dw", j)] + m * CONVW
                ts("dve", acc, mixb[:, m, base:base + T], cpk[:, o:o + 1], None, ALU.mult, None,
                   [("mix", m)], [("cacc", a)])
                for k in range(1, NDVE):
                    stt("dve", acc, mixb[:, m, base + k:base + k + T], cpk[:, o + k:o + k + 1], acc,
                        ALU.mult, ALU.add, [("mix", m), ("cacc", a)], [("cacc", a)])
                for k in range(NDVE, CONVW):
                    mm(ps[bc][:, :T], wslot[sd][:, (k - NDVE) * 128:(k - NDVE + 1) * 128],
                       mixb[:, m, base + k:base + k + T], k == NDVE, k == CONVW - 1, [("mix", m), ("wslot", sd)], [("ps", bc)], inc=(k == CONVW - 1))
                stt("dve", zbuf[:, m, :T], ps[bc][:, :T], pvcol(("a_dw_b", j), m), acc, ALU.add, ALU.add,
                    [("ps", bc), ("cacc", a)], [("zbuf", m)])
                act(cstate[:, li, m, 0:CONVW - 1], mixb[:, m, base + T:HOFF + T], AF.Copy, [("mix", m)],
                    [("cstate", li, m)], scale=(pvcol("flag") if ti == 0 else 1.0))

            rhs = lambda k: xb[:, k, :T]
            run_units(T, [(("a_in", li, m), 2, (lambda mm_: (lambda banks: glu(mm_, banks)))(m)) for m in range(2)],
                      rhs, "xb", CT, 2)
            for m in range(CT):
                if m + 2 < CT:
                    run_units(T, [(("a_in", li, m + 2), 2, (lambda mm_: (lambda banks: glu(mm_, banks)))(m + 2))],
                              rhs, "xb", CT, 0)
                conv(m)
            layer_norm(T, ("a_ln_g", j), ("a_ln_b", j), [(lambda c: ub[:, c, :T], AF.Silu, "ub")])
            mix_out_and_resid(li, T)

        def mixer_sgu(li, ti, T):
            nchk = T // 128
            vnT = hid

            def mk(i):
                def cb(banks):
                    for q in range(2):
                        b = banks[q]
                        if i < 4:
                            c = 2 * i + q
                            act(zbuf[:, c, :T], ps[b][:, :T], AF.Gelu, [("ps", b)], [("zbuf", c)])
                        else:
                            c = 2 * (i - 4) + q
                            act(mix[:, c, HOFF:HOFF + T], ps[b][:, :T], AF.Gelu, [("ps", b)], [("mix", c)])
                return cb
            run_units(T, [(("b_in", li, i), 2, mk(i)) for i in range(4)],
                      lambda k: xb[:, k, :T], "xb", CT, 3)
            layer_norm(T, "b_ln_g", "b_ln_b", [(lambda c: ub[:, c, :T], AF.Identity, "ub")],
                       mid=lambda: run_units(T, [(("b_in", li, i), 2, mk(i)) for i in range(4, CT)],
                                             lambda k: xb[:, k, :T], "xb", CT, 0))
            for h in range(CT):
                b = next_bank()
                pbf = ps[b][:].bitcast(BF16)
                for ck in range(nchk):
                    P.op("pe", (lambda o_, i_: (lambda hh: hh.transpose(o_, i_, identb[:, :])))(
                        pbf[:, ck * 128:(ck + 1) * 128], ub[:, h, ck * 128:(ck + 1) * 128]),
                        [("ub", h), "identb"], [("ps", b)], inc=(ck == nchk - 1))
                act(vnT[:, h, :T], pbf[:, :T], AF.Copy, [("ps", b)], [("hid", h)])
            for h in range(CT):
                b = next_bank()
                for ck in range(nchk):
                    mm(ps[b][:, ck * 128:(ck + 1) * 128], vnT[:, h, ck * 128:(ck + 1) * 128], wsTm[:, h, :],
                       True, True, [("hid", h), "wsTm"], [("ps", b)], inc=(ck == nchk - 1))
                i = next_tmp()
                o = CPK["bsb"] + h * 128
                for ck in range(nchk):
                    tt("dve", tmp[:, i, ck * 128:(ck + 1) * 128], ps[b][:, ck * 128:(ck + 1) * 128],
                       cpk[:, o:o + 128], ALU.add, [("ps", b)], [("tmp", i)])
                tt("dve", ub[:, h, :T], tmp[:, i, :T], mix[:, h, HOFF:HOFF + T], ALU.mult,
                   [("tmp", i), ("mix", h)], [("ub", h)])
            mix_out_and_resid(li, T)

        psc = sb("psc", [128, 2, HOFF + TMAX], F32)

        def mixer_pool(li, ti, T):
            H = 16

            def mk(i):
                def cb(banks):
                    for q in range(2):
                        c = 2 * i + q
                        b = banks[q]
                        act(mix[:, c, HOFF - H:HOFF], cstate[:, li, c, 0:H], AF.Copy, [("cstate", li, c)], [("mix", c)])
                        act(mix[:, c, HOFF:HOFF + T], ps[b][:, :T], AF.Copy, [("ps", b)], [("mix", c)])
                return cb
            run_units(T, [(("c_in", li, i), 2, mk(i)) for i in range(4)],
                      lambda k: xb[:, k, :T], "xb", CT, 3)
            E = HOFF + T
            for g in range(4):
                w = 2 << g
                for q in range(2):
                    c = 2 * g + q
                    cur = mix[:, c, :]
                    cur_res = ("mix", c)
                    lo = HOFF - H
                    step = 1
                    pi = 0
                    while step < w:
                        nlo = lo + step
                        dst = psc[:, pi, :]
                        tt("dve", dst[:, nlo:E], cur[:, nlo:E], cur[:, nlo - step:E - step], ALU.add,
                           [cur_res], [("psc", pi)])
                        cur, cur_res, lo = dst, ("psc", pi), nlo
                        pi ^= 1
                        step *= 2
                    stt("dve", zb[:, c, :T], cur[:, HOFF:E], 1.0 / w, mix[:, c, HOFF:E], ALU.mult, ALU.subtract,
                        [cur_res, ("mix", c)], [("zb", c)])
                    if ti == 1:
                        i = next_tmp()
                        o = CPK["icnt"] + g * 16
                        tt("dve", tmp[:, i, 0:16], cur[:, HOFF:HOFF + 16], cpk[:, o:o + 16], ALU.mult,
                           [cur_res], [("tmp", i)])
                        tt("dve", zb[:, c, 0:16], tmp[:, i, 0:16], mix[:, c, HOFF:HOFF + 16], ALU.subtract,
                           [("tmp", i), ("mix", c), ("zb", c)], [("zb", c)])
                    act(cstate[:, li, c, 0:H], mix[:, c, E - H:E], AF.Copy, [("mix", c)], [("cstate", li, c)],
                        scale=(pvcol("flag") if ti == 0 else 1.0))
            s = load_chunk(("c_grp", li), True)
            for g in range(4):
                for dd in range(2):
                    c = 2 * g + dd
                    b = next_bank()
                    for kk in range(2):
                        o = ((g * 2 + kk) * 2 + dd) * 128
                        mm(ps[b][:, :T], wslot[s][:, o:o + 128], zb[:, 2 * g + kk, :T], kk == 0, kk == 1,
                           [("zb", 2 * g + kk), ("wslot", s)], [("ps", b)], inc=(kk == 1))
                    act(ub[:, c, :T], ps[b][:, :T], AF.Copy, [("ps", b)], [("ub", c)], scale=pvcol("c_scale", c))
            mix_out_and_resid(li, T)

        def ffn(li, ti, T, state_only=False):
            par = ti % 2
            fo = CPK[("fdw", li)]
            hrow = 2 * li + par

            def wcol(kk, c):
                o = fo + kk * 2 * NP + c
                return cpk[:, o:o + 1]

            def wrow(kk):
                o = fo + kk * 2 * NP
                return cpk[:, o:o + 2 * NP]
            hres = [("hp", li, par, c) for c in range(2 * NP)]
            tt("pool", edge[:, :, 0], hp[:, hrow, :, 1], wrow(1), ALU.mult, hres, ["edge"])
            tt("pool", edge[:, :, 2], hp[:, hrow, :, 0], wrow(0), ALU.mult, hres, ["edge"])
            tt("pool", edge[:, :, 0], edge[:, :, 0], edge[:, :, 2], ALU.add, ["edge"], ["edge"])
            tt("pool", edge[:, :, 1], hp[:, hrow, :, 1], wrow(0), ALU.mult, hres, ["edge"])

            def mk(j):
                def cb(banks):
                    cs = (j, NP + j)
                    ai = ((2 * j) % 4, (2 * j + 1) % 4)
                    for q in range(2):
                        b, c, a = banks[q], cs[q], ai[q]
                        if state_only:
                            act(hp[:, 2 * li + 1 - par, c, :], ps[b][:, T - 2:T], AF.Copy, [("ps", b)],
                                [("hp", li, 1 - par, c)], scale=(pvcol("flag") if ti == 0 else 1.0))
                            continue
                        acc = cacc[:, a, :]
                        act(acc[:, 2:T], ps[b][:, 2:T], AF.Copy, [("ps", b)], [("cacc", a)], scale=wcol(2, c))
                        act(acc[:, 0:1], ps[b][:, 0:1], AF.Identity, [("ps", b), "edge"], [("cacc", a)],
                            scale=wcol(2, c), bias=edge[:, c, 0:1])
                        act(acc[:, 1:2], ps[b][:, 1:2], AF.Identity, [("ps", b), "edge"], [("cacc", a)],
                            scale=wcol(2, c), bias=edge[:, c, 1:2])
                        stt("dve", acc[:, 1:T], ps[b][:, 0:T - 1], wcol(1, c), acc[:, 1:T], ALU.mult, ALU.add,
                            [("ps", b), ("cacc", a)], [("cacc", a)])
                        stt("dve", acc[:, 2:T], ps[b][:, 0:T - 2], wcol(0, c), acc[:, 2:T], ALU.mult, ALU.add,
                            [("ps", b), ("cacc", a)], [("cacc", a)])
                        act(hp[:, 2 * li + 1 - par, c, :], ps[b][:, T - 2:T], AF.Copy, [("ps", b)],
                            [("hp", li, 1 - par, c)], scale=(pvcol("flag") if ti == 0 else 1.0))
                    if state_only:
                        return
                    i = next_tmp()
                    act(tmp[:, i, :T], cacc[:, ai[0], :T], AF.Silu, [("cacc", ai[0])], [("tmp", i)])
                    tt("pool", hid[:, j, :T], tmp[:, i, :T], cacc[:, ai[1], :T], ALU.mult,
                       [("tmp", i), ("cacc", ai[1])], [("hid", j)])
                return cb
            run_units(T, [(("f_up", li, j), 2, mk(j)) for j in range(NP)],
                      lambda k: xb[:, k, :T], "xb", CT, 3)

            def mkd(m):
                def cb(banks):
                    resid(m, banks[0], T)
                return cb
            if state_only:
                for m in range(CT):
                    load_chunk(("f_down", li, m), True)
                return
            run_units(T, [(("f_down", li, m), 1, mkd(m)) for m in range(CT)],
                      lambda k: hid[:, k, :T], "hid", NP, 0)

        all_x = [("xres", c) for c in range(CT)]
        P.dma("sp", lambda h: h.dma_start(out=cpk[:, :], in_=cpk_d[:, :]), "cst", writes=("cpk",))
        zflat = zbuf[:].rearrange("p c t -> p (c t)")
        P.dma("sp", lambda h: h.dma_start(out=zflat[:, 0:STG_COLS], in_=stg_d[:, :]), "cst",
              writes=[("zbuf", c) for c in range(CT)])
        cst_tag = ("cst", 32)
        for e_ in ("pe", "act", "dve", "pool"):
            P.wait(e_, cst_tag)
        P.op("dve", lambda h: h.memset(ones[:, :], 1.0 / D), (), ("ones",))
        P.op("dve", lambda h: h.memset(hp[:].rearrange("p a c t -> p (a c t)"), 0.0), (),
             [("hp", l_, a, c) for l_ in range(DEPTH) for a in range(2) for c in range(2 * NP)])
        P.op("dve", lambda h: h.memset(cstate[:].rearrange("p a c t -> p (a c t)"), 0.0), (),
             [("cstate", l_, c) for l_ in range(DEPTH) for c in range(CT)])
        for c in range(CT):
            P.op("dve", (lambda cc: (lambda h: h.memset(mix[:, cc, :], 0.0)))(c), (), (("mix", c),))
        zr = [("zbuf", c) for c in range(CT)] + ["cpk"]
        for h_ in range(CT):
            tt("dve", wsTm[:, h_, :], zflat[:, h_ * 128:(h_ + 1) * 128], zflat[:, CT * 128:CT * 128 + 128],
               ALU.mult, zr, ("wsTm",))
        P.op("dve", lambda h: h.tensor_copy(out=identb[:, :], in_=zflat[:, CT * 128 + 128:CT * 128 + 256]),
             zr, ("identb",))

        n_tiles = len(tiles)

        def load_x(ti):
            t0_, T_ = tiles[ti]
            P.dma("sp", (lambda a_, b_: (lambda h: h.dma_start(out=xres[:, :, :b_], in_=xin[:, :, a_:a_ + b_])))(t0_, T_),
                  "xin", reads=(), writes=all_x)

        for li in range(DEPTH):
            for w in (1, 2):
                for (dst, srcn) in ((("ag", li, w), ("ln%d_g" % w, li)), (("ab", li, w), ("ln%d_b" % w, li))):
                    o_d, o_s = CPK[dst], CPK[srcn]
                    ts("dve", cpk[:, o_d:o_d + CT], cpk[:, o_s:o_s + CT], ALPHA, None, ALU.mult, None,
                       ["cpk"], ["cpk"])
        load_x(0)
        cast_done = set()

        def cast_xb(ti_):
            T_ = tiles[ti_][1]
            for c in range(CT):
                act(xb[:, c, :T_], xres[:, c, :T_], AF.Copy, [("xres", c), "cpk"], [("xb", c)])
            cast_done.add(ti_)

        for ti, (t0, T) in enumerate(tiles):
            stored = ti >= n_tiles - n_store_tiles
            xmode["ag"], xmode["ab"] = None, None
            if ti not in cast_done:
                cast_xb(ti)
            for li_idx, li in enumerate(layers):
                last_layer = li_idx == len(layers) - 1
                kind = li % 3
                if kind == 0:
                    mixer_conf(li, ti, T)
                elif kind == 1:
                    mixer_sgu(li, ti, T)
                else:
                    mixer_pool(li, ti, T)
                post_ln(T, li, 1, False)
                ffn(li, ti, T, state_only=(last_layer and not stored))
                if last_layer:
                    if ti + 1 < n_tiles:
                        load_x(ti + 1)
                    if stored:
                        post_ln(T, li, 2, True,
                                after_rstd=((lambda t_=ti + 1: cast_xb(t_)) if ti + 1 < n_tiles else None))
                else:
                    post_ln(T, li, 2, False)
            if stored:
                o0 = sum(tt_[1] for tt_ in tiles[n_tiles - n_store_tiles:ti])
                P.dma("act", (lambda a, b_: (lambda h: h.dma_start(out=yout[:, :, a:a + b_], in_=zbuf[:, :, :b_])))(o0, T),
                      "out", reads=[("zbuf", c) for c in range(CT)], writes=())
        P.wait("act", ("out", P.dmacnt.get("out", 0)))

        handles = {}

        def emit(ename, h):
            for item in P.streams[ename]:
                if item[0] == "wait":
                    h.wait_ge(sems[item[1]], item[2])
                elif item[0] == "op":
                    ins = item[1](h)
                    if item[2] is not None:
                        ins.then_inc(sems[item[2]], 1)
                else:
                    item[1](h).then_inc(sems[item[2]], 16)

        with nc.Block() as block:
            @block.sync
            def _(h):
                emit("sp", h)

            @block.gpsimd
            def _(h):
                emit("pool", h)

            @block.scalar
            def _(h):
                emit("act", h)

            @block.vector
            def _(h):
                emit("dve", h)

            @block.tensor
            def _(h):
                emit("pe", h)
    return nc


def make_tiles(tok_per_core):
    tiles = [(0, HALO)]
    t = HALO
    while t < HALO + tok_per_core:
        T = min(TMAX, HALO + tok_per_core - t)
        tiles.append((t, T))
        t += T
    return tiles


def run(inputs, layers=(0, 1, 2, 3), trace=False):
    x = np.asarray(inputs["x"], np.float32)
    inp = {k: np.asarray(v, np.float32) for k, v in inputs.items()}
    B, S, _ = x.shape
    cores_per_seq = NCORES // B
    tpc = S // cores_per_seq
    tiles = make_tiles(tpc)
    n_store = len(tiles) - 1
    layers = list(layers)
    nc = build_program(layers, tiles, n_store, HALO + tpc, tpc)
    wpk = pack_weights(inp, layers)
    stg = pack_stage(inp)
    in_maps = []
    for core in range(NCORES):
        b, seg = divmod(core, cores_per_seq)
        p0 = seg * tpc
        xs = np.zeros((HALO + tpc, D), np.float32)
        if seg == 0:
            xs[HALO:] = x[b, 0:tpc]
        else:
            xs[:] = x[b, p0 - HALO:p0 + tpc]
        xin = np.ascontiguousarray(xs.reshape(HALO + tpc, CT, 128).transpose(2, 1, 0))
        in_maps.append({"xin": xin, "wpk": wpk, "cpk": pack_cpk(inp, seg == 0), "stg": stg})
    res = run_bass_kernel_spmd(nc, in_maps, core_ids=list(range(NCORES)), trace=trace)
    out = np.zeros((B, S, D), np.float32)
    for core in range(NCORES):
        b, seg = divmod(core, cores_per_seq)
        y = res.results[core]["yout"]
        out[b, seg * tpc:(seg + 1) * tpc] = y.transpose(2, 1, 0).reshape(tpc, D)
    return out, res


def kernel(**inputs):
    out, _ = run(inputs)
    return out
```

```python
import numpy as np
import concourse.bass as bass
import concourse.mybir as mybir
from concourse.bass_utils import run_bass_kernel_spmd

F32 = mybir.dt.float32
BF16 = mybir.dt.bfloat16
AF = mybir.ActivationFunctionType
ALU = mybir.AluOpType

D = 1024
CT = 8
DFF = 2816
NP = 22
DEPTH = 4
NCORES = 8
HALO = 256
TMAX = 512
ALPHA = float((2 * DEPTH) ** 0.25)
EPS = 1e-5
SLOT_COLS = 3072
NSLOT = 7
LOOKAHEAD = 5
NCAST_SEM = 8
CAST_AHEAD = 10
CONVW = 31
NWARM = 3
NDVE = 7
HOFF = 32


class Layout:
    def __init__(self):
        self.off = {}
        self.n = 0

    def add(self, name, ncols):
        self.off[name] = (self.n, ncols)
        self.n += ncols

    def __getitem__(self, name):
        return self.off[name][0]


def make_cpk_layout():
    L = Layout()
    for li in range(DEPTH):
        for nm in ("ln1_g", "ln1_b", "ln2_g", "ln2_b"):
            L.add((nm, li), CT)
        L.add(("fdw", li), 3 * 2 * NP)
    for j in range(2):
        L.add(("a_dw", j), CT * CONVW)
        L.add(("a_dw_b", j), CT)
        L.add(("a_ln_g", j), CT)
        L.add(("a_ln_b", j), CT)
    L.add("b_ln_g", CT)
    L.add("b_ln_b", CT)
    L.add("c_scale", CT)
    L.add("bsb", CT * 128)
    L.add("flag", 1)
    L.add("eps", 1)
    L.add("icnt", 4 * 16)
    for li in range(DEPTH):
        for w in (1, 2):
            L.add(("ag", li, w), CT)
            L.add(("ab", li, w), CT)
    return L


CPK = make_cpk_layout()
STG_COLS = CT * 128 + 128 + 128


def layer_chunks(li):
    kind = li % 3
    ch = []
    if kind == 0:
        ch.append((("a_in", li, 0), 8 * 2 * 128))
        ch.append((("a_in", li, 1), 8 * 2 * 128))
        for m in range(CT):
            if m + 2 < CT:
                ch.append((("a_in", li, m + 2), 8 * 2 * 128))
            ch.append((("a_dg", li, m), (CONVW - NDVE) * 128))
        for h in range(4):
            ch.append((("mix_out", li, h), 8 * 2 * 128))
    elif kind == 1:
        for i in range(CT):
            ch.append((("b_in", li, i), 8 * 2 * 128))
        for h in range(4):
            ch.append((("mix_out", li, h), 8 * 2 * 128))
    else:
        for i in range(4):
            ch.append((("c_in", li, i), 8 * 2 * 128))
        ch.append((("c_grp", li), 4 * 2 * 2 * 128))
        for h in range(4):
            ch.append((("mix_out", li, h), 8 * 2 * 128))
    for j in range(NP):
        ch.append((("f_up", li, j), 8 * 2 * 128))
    for m in range(CT):
        ch.append((("f_down", li, m), NP * 128))
    return ch


def all_chunks(layers):
    off = 0
    out = []
    for li in layers:
        for key, n in layer_chunks(li):
            out.append((key, off, n))
            off += n
    return out, off


def pack_blocks(W, blocks):
    K = W.shape[0]
    kt = K // 128
    Wr = W.reshape(kt, 128, W.shape[1] // 128, 128)
    sel = Wr[:, :, blocks, :]
    return np.ascontiguousarray(sel.transpose(1, 0, 2, 3)).reshape(128, -1)


def pack_weights(inp, layers):
    chunks, total = all_chunks(layers)
    wpk = np.zeros((128, total), np.float32)
    for key, off, n in chunks:
        kind = key[0]
        li = key[1]
        j = li // 3
        if kind == "a_in":
            m = key[2]
            v = pack_blocks(inp["a_w_in"][j], [m, CT + m])
        elif kind == "a_dg":
            m = key[2]
            dw = inp["a_dw"][j][:, m * 128:(m + 1) * 128]
            v = np.zeros((128, CONVW, 128), np.float32)
            v[np.arange(128), :, np.arange(128)] = dw.T
            v = np.ascontiguousarray(v[:, NDVE:, :]).reshape(128, -1)
        elif kind == "b_in":
            i = key[2]
            if i < 4:
                v = pack_blocks(inp["b_w_in"][j], [CT + 2 * i, CT + 2 * i + 1])
            else:
                v = pack_blocks(inp["b_w_in"][j], [2 * (i - 4), 2 * (i - 4) + 1])
        elif kind == "c_in":
            i = key[2]
            v = pack_blocks(inp["c_w_in"][j], [2 * i, 2 * i + 1])
        elif kind == "c_grp":
            G = inp["c_w_grp"][j]
            Gr = G.reshape(4, 2, 128, 2, 128)
            v = np.ascontiguousarray(Gr.transpose(2, 0, 1, 3, 4)).reshape(128, -1)
        elif kind == "mix_out":
            h = key[2]
            W = {0: inp["a_w_out"], 1: inp["b_w_out"], 2: inp["c_w_out"]}[li % 3][j]
            v = pack_blocks(W, [2 * h + q for q in range(2)])
        elif kind == "f_up":
            v = pack_blocks(inp["f_w_up"][li], [key[2], NP + key[2]])
        elif kind == "f_down":
            v = pack_blocks(inp["f_w_down"][li], [key[2]])
        assert v.shape[1] == n, (key, v.shape, n)
        wpk[:, off:off + n] = v
    return wpk


def vec8(v):
    return np.ascontiguousarray(v.reshape(-1, 128).T)


def pack_cpk(inp, core_is_start):
    c = np.zeros((128, CPK.n), np.float32)

    def put(name, arr):
        o, n = CPK.off[name]
        assert arr.shape == (128, n), (name, arr.shape, n)
        c[:, o:o + n] = arr

    for li in range(DEPTH):
        for nm in ("ln1_g", "ln1_b", "ln2_g", "ln2_b"):
            put((nm, li), vec8(inp[nm][li]))
        fd = inp["f_dw"][li]
        put(("fdw", li), np.ascontiguousarray(
            fd.reshape(3, 2 * NP, 128).transpose(2, 0, 1)).reshape(128, -1))
    for j in range(2):
        dw = inp["a_dw"][j]
        put(("a_dw", j), np.ascontiguousarray(
            dw.reshape(CONVW, CT, 128).transpose(2, 1, 0)).reshape(128, -1))
        put(("a_dw_b", j), vec8(inp["a_dw_b"][j]))
        put(("a_ln_g", j), vec8(inp["a_ln_g"][j]))
        put(("a_ln_b", j), vec8(inp["a_ln_b"][j]))
    put("b_ln_g", vec8(inp["b_ln_g"][0]))
    put("b_ln_b", vec8(inp["b_ln_b"][0]))
    put("c_scale", vec8(inp["c_scale"][0]))
    put("bsb", np.broadcast_to(inp["b_bs"][0].reshape(1, CT * 128), (128, CT * 128)))
    put("flag", np.full((128, 1), 0.0 if core_is_start else 1.0, np.float32))
    put("eps", np.full((128, 1), EPS, np.float32))
    ic = np.zeros((4, 16), np.float32)
    for g, w in enumerate((2, 4, 8, 16)):
        for t in range(16):
            ic[g, t] = 1.0 / (min(t + 1, w) if core_is_start else w)
    put("icnt", np.broadcast_to(ic.reshape(1, 64), (128, 64)))
    return c


def pack_stage(inp):
    s = np.zeros((128, STG_COLS), np.float32)
    ws = inp["b_ws"][0]
    s[:, :CT * 128] = np.ascontiguousarray(ws.transpose(2, 0, 1)).reshape(128, CT * 128)
    idx = np.arange(128)
    s[:, CT * 128:CT * 128 + 128] = (idx[:, None] <= idx[None, :]).astype(np.float32)
    s[:, CT * 128 + 128:] = np.eye(128, dtype=np.float32)
    return s


class Planner:
    def __init__(self):
        self.streams = {e: [] for e in ("pe", "act", "dve", "pool", "sp")}
        self.cnt = {e: 0 for e in self.streams}
        self.seen = {e: {} for e in self.streams}
        self.lastw = {}
        self.readers = {}
        self.dmacnt = {}

    def _deps(self, eng, reads, writes):
        need = {}

        def add(d, is_raw):
            if d is None:
                return
            k, v = d
            if k == eng and (eng == "pe" or not is_raw):
                return
            if need.get(k, 0) < v:
                need[k] = v
        for r in reads:
            add(self.lastw.get(r), True)
        for r in writes:
            add(self.lastw.get(r), False)
            for rd in self.readers.get(r, ()):
                add(rd, False)
        for k, v in need.items():
            if self.seen[eng].get(k, 0) < v:
                self.seen[eng][k] = v
                self.streams[eng].append(("wait", k, v))

    def _record(self, tag, reads, writes):
        for r in writes:
            self.lastw[r] = tag
            self.readers[r] = []
        for r in reads:
            self.readers.setdefault(r, []).append(tag)

    def op(self, eng, fn, reads=(), writes=(), inc=True):
        self._deps(eng, reads, writes)
        if inc:
            self.cnt[eng] += 1
            tag = (eng, self.cnt[eng])
        else:
            tag = (eng, self.cnt[eng] + 1)
        self.streams[eng].append(("op", fn, eng if inc else None))
        self._record(tag, reads, writes)

    def dma(self, eng, fn, semkey, reads=(), writes=(), extra_waits=()):
        self._deps(eng, reads, writes)
        for k, v in extra_waits:
            if self.seen[eng].get(k, 0) < v:
                self.seen[eng][k] = v
                self.streams[eng].append(("wait", k, v))
        self.dmacnt[semkey] = self.dmacnt.get(semkey, 0) + 16
        tag = (semkey, self.dmacnt[semkey])
        self.streams[eng].append(("dma", fn, semkey))
        self._record(tag, reads, writes)
        return tag

    def wait(self, eng, tag):
        k, v = tag
        if self.seen[eng].get(k, 0) < v:
            self.seen[eng][k] = v
            self.streams[eng].append(("wait", k, v))


def build_program(layers, tiles, n_store_tiles, tok_in, tok_out):
    nc = bass.Bass("TRN2", target_bir_lowering=False)
    chunks, wcols = all_chunks(layers)
    chunk_by_key = {k: (i, off, n) for i, (k, off, n) in enumerate(chunks)}
    nchunk = len(chunks)

    xin = nc.dram_tensor("xin", [128, CT, tok_in], F32, kind="ExternalInput").ap()
    wpk = nc.dram_tensor("wpk", [128, wcols], F32, kind="ExternalInput").ap()
    cpk_d = nc.dram_tensor("cpk", [128, CPK.n], F32, kind="ExternalInput").ap()
    stg_d = nc.dram_tensor("stg", [128, STG_COLS], F32, kind="ExternalInput").ap()
    yout = nc.dram_tensor("yout", [128, CT, tok_out], F32, kind="ExternalOutput").ap()
    wsc = nc.dram_tensor("wsc", [128, wcols], BF16, kind="Internal").ap()

    P = Planner()
    import contextlib
    es = contextlib.ExitStack()
    with es:
        def sb(name, shape, dt):
            return es.enter_context(nc.sbuf_tensor(name, shape, dt))

        xres = sb("xres", [128, CT, TMAX], F32)
        xb = sb("xb", [128, CT, TMAX], BF16)
        zbuf = sb("zbuf", [128, CT, TMAX], F32)
        zb = sb("zb", [128, CT, TMAX], BF16)
        zsq = sb("zsq", [128, CT, TMAX], BF16)
        ub = sb("ub", [128, CT, TMAX], BF16)
        hid = sb("hid", [128, NP, TMAX], BF16)
        mix = sb("mix", [128, CT, HOFF + TMAX], F32)
        lnt = sb("lnt", [128, 3, TMAX], F32)
        tmp = sb("tmp", [128, 4, TMAX], F32)
        cacc = sb("cacc", [128, 4, TMAX], F32)
        cpk = sb("cpk_sb", [128, CPK.n], F32)
        wsTm = sb("wsTm", [128, CT, 128], BF16)
        identb = sb("identb", [128, 128], BF16)
        ones = sb("ones", [128, 128], BF16)
        hp = sb("hp", [128, DEPTH * 2, 2 * NP, 2], F32)
        cstate = sb("cstate", [128, DEPTH, CT, 32], F32)
        edge = sb("edge", [128, 2 * NP, 3], F32)
        scr = sb("scr", [128, 8], F32)
        wslot = [sb(f"wslot{i}", [128, SLOT_COLS], BF16) for i in range(NSLOT)]
        ps = [es.enter_context(nc.psum_tensor(f"ps{i}", [128, TMAX], F32)) for i in range(8)]

        sems = {}
        for e in ("pe", "act", "dve", "pool", "sp"):
            sems[e] = es.enter_context(nc.semaphore(f"c_{e}"))
        for i in range(NSLOT):
            sems[("w", i)] = es.enter_context(nc.semaphore(f"w{i}"))
        for i in range(NSLOT):
            sems[("wb", i)] = es.enter_context(nc.semaphore(f"wb{i}"))
        for k in ("xin", "out", "cst"):
            sems[k] = es.enter_context(nc.semaphore(f"d_{k}"))

        def pvcol(name, c=0, n=1):
            o = CPK[name] + c
            return cpk[:, o:o + n]

        def act(out, in_, func, reads, writes, scale=1.0, bias=0.0):
            P.op("act", lambda h: h.activation(out=out, in_=in_, func=func, bias=bias, scale=scale),
                 reads, writes)

        def tt(eng, out, in0, in1, op, reads, writes):
            P.op(eng, lambda h: h.tensor_tensor(out=out, in0=in0, in1=in1, op=op), reads, writes)

        def stt(eng, out, in0, scalar, in1, op0, op1, reads, writes):
            P.op(eng, lambda h: h.scalar_tensor_tensor(out=out, in0=in0, scalar=scalar, in1=in1,
                                                       op0=op0, op1=op1), reads, writes)

        def ts(eng, out, in0, s1, s2, op0, op1, reads, writes):
            if op1 is None:
                P.op(eng, lambda h: h.tensor_scalar(out=out, in0=in0, scalar1=s1, scalar2=None, op0=op0),
                     reads, writes)
            else:
                P.op(eng, lambda h: h.tensor_scalar(out=out, in0=in0, scalar1=s1, scalar2=s2,
                                                    op0=op0, op1=op1), reads, writes)

        def mm(out, lhsT, rhs, start, stop, reads, writes, inc):
            P.op("pe", lambda h: h.matmul(out, lhsT, rhs, start=start, stop=stop), reads, writes, inc=inc)

        bank_rr = [0]

        pinned = set()

        def next_bank():
            while True:
                b = bank_rr[0]
                bank_rr[0] = (b + 1) % 8
                if b not in pinned:
                    return b

        tmp_rr = [0]
        cacc_rr = [0]

        def next_tmp():
            i = tmp_rr[0]
            tmp_rr[0] = (i + 1) % 4
            return i

        wstate = {"n": 0, "issued": 0}

        def ensure_issued(upto):
            while wstate["issued"] < min(upto, nchunk):
                kk = wstate["issued"]
                key, off, n = chunks[kk]
                s = kk % NSLOT
                P.dma("pool", (lambda o, nn, ss: (lambda h: h.dma_start(out=wslot[ss][:, 0:nn], in_=wpk[:, o:o + nn])))(off, n, s),
                      ("w", s), reads=(), writes=(("wslot", s),))
                P.dma("sp", (lambda o, nn, ss: (lambda h: h.dma_start(out=wsc[:, o:o + nn], in_=wslot[ss][:, 0:nn])))(off, n, s),
                      ("wb", s), reads=(("wslot", s),), writes=(("wsc", kk),))
                wstate["issued"] += 1

        def load_chunk(key, first_pass):
            ci, off, n = chunk_by_key[key]
            k = wstate["n"]
            wstate["n"] += 1
            assert k % nchunk == ci, (key, k, ci)
            s = k % NSLOT
            if k < nchunk:
                ensure_issued(k + LOOKAHEAD)
                return s
            P.dma("sp", (lambda o, nn, ss: (lambda h: h.dma_start(out=wslot[ss][:, 0:nn], in_=wsc[:, o:o + nn])))(off, n, s),
                  ("w", s), reads=(("wsc", ci),), writes=(("wslot", s),))
            return s


        def lhs(slot, nblk, k, blk):
            o = (k * nblk + blk) * 128
            return wslot[slot][:, o:o + 128]

        def layer_norm(T, gname, bname, outs, zbias=None, t_to_xres=False, mid=None, after_rstd=None):
            z, zres = zbuf, "zbuf"
            bm, be = next_bank(), next_bank()
            for c in range(CT):
                zbv = pvcol(zbias, c) if zbias is not None else 0.0
                act(zb[:, c, :T], z[:, c, :T], AF.Identity if zbias is not None else AF.Copy,
                    [(zres, c)], [("zb", c)], bias=zbv)
                act(zsq[:, c, :T], z[:, c, :T], AF.Square, [(zres, c)], [("zsq", c)], bias=zbv)
            act(scr[:, 0:1], pvcol("eps"), AF.Ln, [], ["scr"])
            for c in range(CT):
                mm(ps[bm][:, :T], ones[:, :], zb[:, c, :T], c == 0, c == CT - 1,
                   [("zb", c), "ones"], [("ps", bm)], inc=(c == CT - 1))
            for c in range(CT):
                mm(ps[be][:, :T], ones[:, :], zsq[:, c, :T], c == 0, c == CT - 1,
                   [("zsq", c), "ones"], [("ps", be)], inc=(c == CT - 1))
            if mid is not None:
                pinned.update((bm, be))
                mid()
                pinned.difference_update((bm, be))
            msq, var, rstd = (lnt[:, i, :T] for i in range(3))
            act(msq, ps[bm][:, :T], AF.Square, [("ps", bm)], [("lnt", 0)])
            tt("dve", var, ps[be][:, :T], msq, ALU.subtract, [("ps", be), ("lnt", 0)], [("lnt", 1)])

            def keep_warm(gate):
                for r in range(NWARM):
                    mm(ps[be][:, :T], ones[:, :], zb[:, 0, :T], True, True,
                       [("zb", 0), "ones"] + gate, [("ps", be)], inc=(r == NWARM - 1))
            keep_warm([])
            act(var, var, AF.Ln, [("lnt", 1)], [("lnt", 1)], bias=pvcol("eps"))
            keep_warm([("lnt", 1)])
            act(rstd, var, AF.Exp, [("lnt", 1)], [("lnt", 2)], scale=-0.5)
            keep_warm([("lnt", 2)])
            slots = {}

            def p1(c):
                i = next_tmp()
                slots[c] = i
                t = tmp[:, i, :T]
                if zbias is not None:
                    stt("dve", t, z[:, c, :T], pvcol(zbias, c), ps[bm][:, :T], ALU.add, ALU.subtract,
                        [(zres, c), ("ps", bm)], [("tmp", i)])
                else:
                    tt("dve", t, z[:, c, :T], ps[bm][:, :T], ALU.subtract, [(zres, c), ("ps", bm)], [("tmp", i)])

            def p2(c):
                i = slots[c]
                t = tmp[:, i, :T]
                if t_to_xres:
                    tt("dve", xres[:, c, :T], t, rstd, ALU.mult, [("tmp", i), ("lnt", 2)], [("xres", c)])
                    src_ap, src_res = xres[:, c, :T], ("xres", c)
                else:
                    tt("dve", t, t, rstd, ALU.mult, [("tmp", i), ("lnt", 2)], [("tmp", i)])
                    src_ap, src_res = t, ("tmp", i)
                for (apf, func, res) in outs:
                    act(apf(c), src_ap, func, [src_res], [(res, c)],
                        scale=pvcol(gname, c), bias=pvcol(bname, c))
            p1(0)
            p1(1)
            if after_rstd is not None:
                after_rstd()
            for c in range(CT):
                p2(c)
                if c + 2 < CT:
                    p1(c + 2)

        xmode = {"ag": None, "ab": None}

        def resid(m, bank, T):
            sc = ALPHA if xmode["ag"] is None else pvcol(xmode["ag"], m)
            stt("dve", zbuf[:, m, :T], xres[:, m, :T], sc, ps[bank][:, :T], ALU.mult, ALU.add,
                [("xres", m), ("ps", bank)], [("zbuf", m)])

        def post_ln(T, li, w, final, after_rstd=None):
            gname, bname = ("ln%d_g" % w, li), ("ln%d_b" % w, li)
            zbias = xmode["ab"]
            if final:
                layer_norm(T, gname, bname, [(lambda c: zbuf[:, c, :T], AF.Identity, "zbuf")], zbias=zbias,
                           after_rstd=after_rstd)
            else:
                layer_norm(T, gname, bname, [(lambda c: xb[:, c, :T], AF.Identity, "xb")], zbias=zbias,
                           t_to_xres=True)
                xmode["ag"], xmode["ab"] = ("ag", li, w), ("ab", li, w)

        def run_units(T, units, rhs, rhs_res, nK, nko):
            def one(s, nblk, banks, k):
                for q in range(nblk):
                    mm(ps[banks[q]][:, :T], lhs(s, nblk, k, q), rhs(k), k == 0, k == nK - 1,
                       [(rhs_res, k), ("wslot", s)], [("ps", banks[q])], inc=(k == nK - 1))
            head = []
            for (key, nblk, cb) in units[:nko]:
                s = load_chunk(key, True)
                head.append((s, nblk, [next_bank() for _ in range(nblk)], cb))
            for k in range(nK):
                for (s, nblk, banks, cb) in head:
                    one(s, nblk, banks, k)
            for (s, nblk, banks, cb) in head:
                cb(banks)
            for (key, nblk, cb) in units[nko:]:
                s = load_chunk(key, True)
                banks = [next_bank() for _ in range(nblk)]
                for q in range(nblk):
                    for k in range(nK):
                        mm(ps[banks[q]][:, :T], lhs(s, nblk, k, q), rhs(k), k == 0, k == nK - 1,
                           [(rhs_res, k), ("wslot", s)], [("ps", banks[q])], inc=(k == nK - 1))
                cb(banks)

        def mix_out_and_resid(li, T):
            def mk(h):
                def cb(banks):
                    for q in range(2):
                        resid(2 * h + q, banks[q], T)
                return cb
            run_units(T, [(("mix_out", li, h), 2, mk(h)) for h in range(4)],
                      lambda k: ub[:, k, :T], "ub", CT, 2)

        def mixer_conf(li, ti, T):
            j = li // 3
            mixb = mix[:].bitcast(BF16)
            base = HOFF - (CONVW - 1)

            def glu(m, banks):
                ba, bg = banks
                i = next_tmp()
                act(tmp[:, i, :T], ps[bg][:, :T], AF.Sigmoid, [("ps", bg)], [("tmp", i)])
                act(mixb[:, m, base:HOFF], cstate[:, li, m, 0:CONVW - 1], AF.Copy,
                    [("cstate", li, m)], [("mix", m)])
                tt("dve", mixb[:, m, HOFF:HOFF + T], ps[ba][:, :T], tmp[:, i, :T], ALU.mult,
                   [("ps", ba), ("tmp", i)], [("mix", m)])

            def conv(m):
                sd = load_chunk(("a_dg", li, m), True)
                bc = next_bank()
                a = cacc_rr[0]
                cacc_rr[0] = (a + 1) % 4
                acc = cacc[:, a, :T]
                o = CPK[("a_dw", j)] + m * CONVW
                ts("dve", acc, mixb[:, m, base:base + T], cpk[:, o:o + 1], None, ALU.mult, None,
                   [("mix", m)], [("cacc", a)])
                for k in range(1, NDVE):
                    stt("dve", acc, mixb[:, m, base + k:base + k + T], cpk[:, o + k:o + k + 1], acc,
                        ALU.mult, ALU.add, [("mix", m), ("cacc", a)], [("cacc", a)])
                for k in range(NDVE, CONVW):
                    mm(ps[bc][:, :T], wslot[sd][:, (k - NDVE) * 128:(k - NDVE + 1) * 128],
                       mixb[:, m, base + k:base + k + T], k == NDVE, k == CONVW - 1, [("mix", m), ("wslot", sd)], [("ps", bc)], inc=(k == CONVW - 1))
                stt("dve", zbuf[:, m, :T], ps[bc][:, :T], pvcol(("a_dw_b", j), m), acc, ALU.add, ALU.add,
                    [("ps", bc), ("cacc", a)], [("zbuf", m)])
                act(cstate[:, li, m, 0:CONVW - 1], mixb[:, m, base + T:HOFF + T], AF.Copy, [("mix", m)],
                    [("cstate", li, m)], scale=(pvcol("flag") if ti == 0 else 1.0))

            rhs = lambda k: xb[:, k, :T]
            run_units(T, [(("a_in", li, m), 2, (lambda mm_: (lambda banks: glu(mm_, banks)))(m)) for m in range(2)],
                      rhs, "xb", CT, 2)
            for m in range(CT):
                if m + 2 < CT:
                    run_units(T, [(("a_in", li, m + 2), 2, (lambda mm_: (lambda banks: glu(mm_, banks)))(m + 2))],
                              rhs, "xb", CT, 0)
                conv(m)
            layer_norm(T, ("a_ln_g", j), ("a_ln_b", j), [(lambda c: ub[:, c, :T], AF.Silu, "ub")])
            mix_out_and_resid(li, T)

        def mixer_sgu(li, ti, T):
            nchk = T // 128
            vnT = hid

            def mk(i):
                def cb(banks):
                    for q in range(2):
                        b = banks[q]
                        if i < 4:
                            c = 2 * i + q
                            act(zbuf[:, c, :T], ps[b][:, :T], AF.Gelu, [("ps", b)], [("zbuf", c)])
                        else:
                            c = 2 * (i - 4) + q
                            act(mix[:, c, HOFF:HOFF + T], ps[b][:, :T], AF.Gelu, [("ps", b)], [("mix", c)])
                return cb
            run_units(T, [(("b_in", li, i), 2, mk(i)) for i in range(4)],
                      lambda k: xb[:, k, :T], "xb", CT, 3)
            layer_norm(T, "b_ln_g", "b_ln_b", [(lambda c: ub[:, c, :T], AF.Identity, "ub")],
                       mid=lambda: run_units(T, [(("b_in", li, i), 2, mk(i)) for i in range(4, CT)],
                                             lambda k: xb[:, k, :T], "xb", CT, 0))
            for h in range(CT):
                b = next_bank()
                pbf = ps[b][:].bitcast(BF16)
                for ck in range(nchk):
                    P.op("pe", (lambda o_, i_: (lambda hh: hh.transpose(o_, i_, identb[:, :])))(
                        pbf[:, ck * 128:(ck + 1) * 128], ub[:, h, ck * 128:(ck + 1) * 128]),
                        [("ub", h), "identb"], [("ps", b)], inc=(ck == nchk - 1))
                act(vnT[:, h, :T], pbf[:, :T], AF.Copy, [("ps", b)], [("hid", h)])
            for h in range(CT):
                b = next_bank()
                for ck in range(nchk):
                    mm(ps[b][:, ck * 128:(ck + 1) * 128], vnT[:, h, ck * 128:(ck + 1) * 128], wsTm[:, h, :],
                       True, True, [("hid", h), "wsTm"], [("ps", b)], inc=(ck == nchk - 1))
                i = next_tmp()
                o = CPK["bsb"] + h * 128
                for ck in range(nchk):
                    tt("dve", tmp[:, i, ck * 128:(ck + 1) * 128], ps[b][:, ck * 128:(ck + 1) * 128],
                       cpk[:, o:o + 128], ALU.add, [("ps", b)], [("tmp", i)])
                tt("dve", ub[:, h, :T], tmp[:, i, :T], mix[:, h, HOFF:HOFF + T], ALU.mult,
                   [("tmp", i), ("mix", h)], [("ub", h)])
            mix_out_and_resid(li, T)

        psc = sb("psc", [128, 2, HOFF + TMAX], F32)

        def mixer_pool(li, ti, T):
            H = 16

            def mk(i):
                def cb(banks):
                    for q in range(2):
                        c = 2 * i + q
                        b = banks[q]
                        act(mix[:, c, HOFF - H:HOFF], cstate[:, li, c, 0:H], AF.Copy, [("cstate", li, c)], [("mix", c)])
                        act(mix[:, c, HOFF:HOFF + T], ps[b][:, :T], AF.Copy, [("ps", b)], [("mix", c)])
                return cb
            run_units(T, [(("c_in", li, i), 2, mk(i)) for i in range(4)],
                      lambda k: xb[:, k, :T], "xb", CT, 3)
            E = HOFF + T
            for g in range(4):
                w = 2 << g
                for q in range(2):
                    c = 2 * g + q
                    cur = mix[:, c, :]
                    cur_res = ("mix", c)
                    lo = HOFF - H
                    step = 1
                    pi = 0
                    while step < w:
                        nlo = lo + step
                        dst = psc[:, pi, :]
                        tt("dve", dst[:, nlo:E], cur[:, nlo:E], cur[:, nlo - step:E - step], ALU.add,
                           [cur_res], [("psc", pi)])
                        cur, cur_res, lo = dst, ("psc", pi), nlo
                        pi ^= 1
                        step *= 2
                    stt("dve", zb[:, c, :T], cur[:, HOFF:E], 1.0 / w, mix[:, c, HOFF:E], ALU.mult, ALU.subtract,
                        [cur_res, ("mix", c)], [("zb", c)])
                    if ti == 1:
                        i = next_tmp()
                        o = CPK["icnt"] + g * 16
                        tt("dve", tmp[:, i, 0:16], cur[:, HOFF:HOFF + 16], cpk[:, o:o + 16], ALU.mult,
                           [cur_res], [("tmp", i)])
                        tt("dve", zb[:, c, 0:16], tmp[:, i, 0:16], mix[:, c, HOFF:HOFF + 16], ALU.subtract,
                           [("tmp", i), ("mix", c), ("zb", c)], [("zb", c)])
                    act(cstate[:, li, c, 0:H], mix[:, c, E - H:E], AF.Copy, [("mix", c)], [("cstate", li, c)],
                        scale=(pvcol("flag") if ti == 0 else 1.0))
            s = load_chunk(("c_grp", li), True)
            for g in range(4):
                for dd in range(2):
                    c = 2 * g + dd
                    b = next_bank()
                    for kk in range(2):
                        o = ((g * 2 + kk) * 2 + dd) * 128
                        mm(ps[b][:, :T], wslot[s][:, o:o + 128], zb[:, 2 * g + kk, :T], kk == 0, kk == 1,
                           [("zb", 2 * g + kk), ("wslot", s)], [("ps", b)], inc=(kk == 1))
                    act(ub[:, c, :T], ps[b][:, :T], AF.Copy, [("ps", b)], [("ub", c)], scale=pvcol("c_scale", c))
            mix_out_and_resid(li, T)

        def ffn(li, ti, T, state_only=False):
            par = ti % 2
            fo = CPK[("fdw", li)]
            hrow = 2 * li + par

            def wcol(kk, c):
                o = fo + kk * 2 * NP + c
                return cpk[:, o:o + 1]

            def wrow(kk):
                o = fo + kk * 2 * NP
                return cpk[:, o:o + 2 * NP]
            hres = [("hp", li, par, c) for c in range(2 * NP)]
            tt("pool", edge[:, :, 0], hp[:, hrow, :, 1], wrow(1), ALU.mult, hres, ["edge"])
            tt("pool", edge[:, :, 2], hp[:, hrow, :, 0], wrow(0), ALU.mult, hres, ["edge"])
            tt("pool", edge[:, :, 0], edge[:, :, 0], edge[:, :, 2], ALU.add, ["edge"], ["edge"])
            tt("pool", edge[:, :, 1], hp[:, hrow, :, 1], wrow(0), ALU.mult, hres, ["edge"])

            def mk(j):
                def cb(banks):
                    cs = (j, NP + j)
                    ai = ((2 * j) % 4, (2 * j + 1) % 4)
                    for q in range(2):
                        b, c, a = banks[q], cs[q], ai[q]
                        if state_only:
                            act(hp[:, 2 * li + 1 - par, c, :], ps[b][:, T - 2:T], AF.Copy, [("ps", b)],
                                [("hp", li, 1 - par, c)], scale=(pvcol("flag") if ti == 0 else 1.0))
                            continue
                        acc = cacc[:, a, :]
                        act(acc[:, 2:T], ps[b][:, 2:T], AF.Copy, [("ps", b)], [("cacc", a)], scale=wcol(2, c))
                        act(acc[:, 0:1], ps[b][:, 0:1], AF.Identity, [("ps", b), "edge"], [("cacc", a)],
                            scale=wcol(2, c), bias=edge[:, c, 0:1])
                        act(acc[:, 1:2], ps[b][:, 1:2], AF.Identity, [("ps", b), "edge"], [("cacc", a)],
                            scale=wcol(2, c), bias=edge[:, c, 1:2])
                        stt("dve", acc[:, 1:T], ps[b][:, 0:T - 1], wcol(1, c), acc[:, 1:T], ALU.mult, ALU.add,
                            [("ps", b), ("cacc", a)], [("cacc", a)])
                        stt("dve", acc[:, 2:T], ps[b][:, 0:T - 2], wcol(0, c), acc[:, 2:T], ALU.mult, ALU.add,
                            [("ps", b), ("cacc", a)], [("cacc", a)])
                        act(hp[:, 2 * li + 1 - par, c, :], ps[b][:, T - 2:T], AF.Copy, [("ps", b)],
                            [("hp", li, 1 - par, c)], scale=(pvcol("flag") if ti == 0 else 1.0))
                    if state_only:
                        return
                    i = next_tmp()
                    act(tmp[:, i, :T], cacc[:, ai[0], :T], AF.Silu, [("cacc", ai[0])], [("tmp", i)])
                    tt("pool", hid[:, j, :T], tmp[:, i, :T], cacc[:, ai[1], :T], ALU.mult,
                       [("tmp", i), ("cacc", ai[1])], [("hid", j)])
                return cb
            run_units(T, [(("f_up", li, j), 2, mk(j)) for j in range(NP)],
                      lambda k: xb[:, k, :T], "xb", CT, 3)

            def mkd(m):
                def cb(banks):
                    resid(m, banks[0], T)
                return cb
            if state_only:
                for m in range(CT):
                    load_chunk(("f_down", li, m), True)
                return
            run_units(T, [(("f_down", li, m), 1, mkd(m)) for m in range(CT)],
                      lambda k: hid[:, k, :T], "hid", NP, 0)

        all_x = [("xres", c) for c in range(CT)]
        P.dma("sp", lambda h: h.dma_start(out=cpk[:, :], in_=cpk_d[:, :]), "cst", writes=("cpk",))
        zflat = zbuf[:].rearrange("p c t -> p (c t)")
        P.dma("sp", lambda h: h.dma_start(out=zflat[:, 0:STG_COLS], in_=stg_d[:, :]), "cst",
              writes=[("zbuf", c) for c in range(CT)])
        cst_tag = ("cst", 32)
        for e_ in ("pe", "act", "dve", "pool"):
            P.wait(e_, cst_tag)
        P.op("dve", lambda h: h.memset(ones[:, :], 1.0 / D), (), ("ones",))
        P.op("dve", lambda h: h.memset(hp[:].rearrange("p a c t -> p (a c t)"), 0.0), (),
             [("hp", l_, a, c) for l_ in range(DEPTH) for a in range(2) for c in range(2 * NP)])
        P.op("dve", lambda h: h.memset(cstate[:].rearrange("p a c t -> p (a c t)"), 0.0), (),
             [("cstate", l_, c) for l_ in range(DEPTH) for c in range(CT)])
        for c in range(CT):
            P.op("dve", (lambda cc: (lambda h: h.memset(mix[:, cc, :], 0.0)))(c), (), (("mix", c),))
        zr = [("zbuf", c) for c in range(CT)] + ["cpk"]
        for h_ in range(CT):
            tt("dve", wsTm[:, h_, :], zflat[:, h_ * 128:(h_ + 1) * 128], zflat[:, CT * 128:CT * 128 + 128],
               ALU.mult, zr, ("wsTm",))
        P.op("dve", lambda h: h.tensor_copy(out=identb[:, :], in_=zflat[:, CT * 128 + 128:CT * 128 + 256]),
             zr, ("identb",))

        n_tiles = len(tiles)

        def load_x(ti):
            t0_, T_ = tiles[ti]
            P.dma("sp", (lambda a_, b_: (lambda h: h.dma_start(out=xres[:, :, :b_], in_=xin[:, :, a_:a_ + b_])))(t0_, T_),
                  "xin", reads=(), writes=all_x)

        for li in range(DEPTH):
            for w in (1, 2):
                for (dst, srcn) in ((("ag", li, w), ("ln%d_g" % w, li)), (("ab", li, w), ("ln%d_b" % w, li))):
                    o_d, o_s = CPK[dst], CPK[srcn]
                    ts("dve", cpk[:, o_d:o_d + CT], cpk[:, o_s:o_s + CT], ALPHA, None, ALU.mult, None,
                       ["cpk"], ["cpk"])
        load_x(0)
        cast_done = set()

        def cast_xb(ti_):
            T_ = tiles[ti_][1]
            for c in range(CT):
                act(xb[:, c, :T_], xres[:, c, :T_], AF.Copy, [("xres", c), "cpk"], [("xb", c)])
            cast_done.add(ti_)

        for ti, (t0, T) in enumerate(tiles):
            stored = ti >= n_tiles - n_store_tiles
            xmode["ag"], xmode["ab"] = None, None
            if ti not in cast_done:
                cast_xb(ti)
            for li_idx, li in enumerate(layers):
                last_layer = li_idx == len(layers) - 1
                kind = li % 3
                if kind == 0:
                    mixer_conf(li, ti, T)
                elif kind == 1:
                    mixer_sgu(li, ti, T)
                else:
                    mixer_pool(li, ti, T)
                post_ln(T, li, 1, False)
                ffn(li, ti, T, state_only=(last_layer and not stored))
                if last_layer:
                    if ti + 1 < n_tiles:
                        load_x(ti + 1)
                    if stored:
                        post_ln(T, li, 2, True,
                                after_rstd=((lambda t_=ti + 1: cast_xb(t_)) if ti + 1 < n_tiles else None))
                else:
                    post_ln(T, li, 2, False)
            if stored:
                o0 = sum(tt_[1] for tt_ in tiles[n_tiles - n_store_tiles:ti])
                P.dma("act", (lambda a, b_: (lambda h: h.dma_start(out=yout[:, :, a:a + b_], in_=zbuf[:, :, :b_])))(o0, T),
                      "out", reads=[("zbuf", c) for c in range(CT)], writes=())
        P.wait("act", ("out", P.dmacnt.get("out", 0)))

        handles = {}

        def emit(ename, h):
            for item in P.streams[ename]:
                if item[0] == "wait":
                    h.wait_ge(sems[item[1]], item[2])
                elif item[0] == "op":
                    ins = item[1](h)
                    if item[2] is not None:
                        ins.then_inc(sems[item[2]], 1)
                else:
                    item[1](h).then_inc(sems[item[2]], 16)

        with nc.Block() as block:
            @block.sync
            def _(h):
                emit("sp", h)

            @block.gpsimd
            def _(h):
                emit("pool", h)

            @block.scalar
            def _(h):
                emit("act", h)

            @block.vector
            def _(h):
                emit("dve", h)

            @block.tensor
            def _(h):
                emit("pe", h)
    return nc


def make_tiles(tok_per_core):
    tiles = [(0, HALO)]
    t = HALO
    while t < HALO + tok_per_core:
        T = min(TMAX, HALO + tok_per_core - t)
        tiles.append((t, T))
        t += T
    return tiles


def run(inputs, layers=(0, 1, 2, 3), trace=False):
    x = np.asarray(inputs["x"], np.float32)
    inp = {k: np.asarray(v, np.float32) for k, v in inputs.items()}
    B, S, _ = x.shape
    cores_per_seq = NCORES // B
    tpc = S // cores_per_seq
    tiles = make_tiles(tpc)
    n_store = len(tiles) - 1
    layers = list(layers)
    nc = build_program(layers, tiles, n_store, HALO + tpc, tpc)
    wpk = pack_weights(inp, layers)
    stg = pack_stage(inp)
    in_maps = []
    for core in range(NCORES):
        b, seg = divmod(core, cores_per_seq)
        p0 = seg * tpc
        xs = np.zeros((HALO + tpc, D), np.float32)
        if seg == 0:
            xs[HALO:] = x[b, 0:tpc]
        else:
            xs[:] = x[b, p0 - HALO:p0 + tpc]
        xin = np.ascontiguousarray(xs.reshape(HALO + tpc, CT, 128).transpose(2, 1, 0))
        in_maps.append({"xin": xin, "wpk": wpk, "cpk": pack_cpk(inp, seg == 0), "stg": stg})
    res = run_bass_kernel_spmd(nc, in_maps, core_ids=list(range(NCORES)), trace=trace)
    out = np.zeros((B, S, D), np.float32)
    for core in range(NCORES):
        b, seg = divmod(core, cores_per_seq)
        y = res.results[core]["yout"]
        out[b, seg * tpc:(seg + 1) * tpc] = y.transpose(2, 1, 0).reshape(tpc, D)
    return out, res


def kernel(**inputs):
    out, _ = run(inputs)
    return out
```

```python
import numpy as np
import concourse.bass as bass
import concourse.mybir as mybir
from concourse.bass_utils import run_bass_kernel_spmd

F32 = mybir.dt.float32
BF16 = mybir.dt.bfloat16
AF = mybir.ActivationFunctionType
ALU = mybir.AluOpType

D = 1024
CT = 8
DFF = 2816
NP = 22
DEPTH = 4
NCORES = 8
HALO = 256
TMAX = 512
ALPHA = float((2 * DEPTH) ** 0.25)
EPS = 1e-5
SLOT_COLS = 3072
NSLOT = 7
LOOKAHEAD = 5
NCAST_SEM = 8
CAST_AHEAD = 10
CONVW = 31
NWARM = 5
NDVE = 7
HOFF = 32


class Layout:
    def __init__(self):
        self.off = {}
        self.n = 0

    def add(self, name, ncols):
        self.off[name] = (self.n, ncols)
        self.n += ncols

    def __getitem__(self, name):
        return self.off[name][0]


def make_cpk_layout():
    L = Layout()
    for li in range(DEPTH):
        for nm in ("ln1_g", "ln1_b", "ln2_g", "ln2_b"):
            L.add((nm, li), CT)
        L.add(("fdw", li), 3 * 2 * NP)
    for j in range(2):
        L.add(("a_dw", j), CT * CONVW)
        L.add(("a_dw_b", j), CT)
        L.add(("a_ln_g", j), CT)
        L.add(("a_ln_b", j), CT)
    L.add("b_ln_g", CT)
    L.add("b_ln_b", CT)
    L.add("c_scale", CT)
    L.add("bsb", CT * 128)
    L.add("flag", 1)
    L.add("eps", 1)
    L.add("icnt", 4 * 16)
    for li in range(DEPTH):
        for w in (1, 2):
            L.add(("ag", li, w), CT)
            L.add(("ab", li, w), CT)
    return L


CPK = make_cpk_layout()
STG_COLS = CT * 128 + 128 + 128


def layer_chunks(li):
    kind = li % 3
    ch = []
    if kind == 0:
        ch.append((("a_in", li, 0), 8 * 2 * 128))
        ch.append((("a_in", li, 1), 8 * 2 * 128))
        for m in range(CT):
            if m + 2 < CT:
                ch.append((("a_in", li, m + 2), 8 * 2 * 128))
            ch.append((("a_dg", li, m), (CONVW - NDVE) * 128))
        for h in range(4):
            ch.append((("mix_out", li, h), 8 * 2 * 128))
    elif kind == 1:
        for i in range(CT):
            ch.append((("b_in", li, i), 8 * 2 * 128))
        for h in range(4):
            ch.append((("mix_out", li, h), 8 * 2 * 128))
    else:
        for i in range(4):
            ch.append((("c_in", li, i), 8 * 2 * 128))
        ch.append((("c_grp", li), 4 * 2 * 2 * 128))
        for h in range(4):
            ch.append((("mix_out", li, h), 8 * 2 * 128))
    for j in range(NP):
        ch.append((("f_up", li, j), 8 * 2 * 128))
    for m in range(CT):
        ch.append((("f_down", li, m), NP * 128))
    return ch


def all_chunks(layers):
    off = 0
    out = []
    for li in layers:
        for key, n in layer_chunks(li):
            out.append((key, off, n))
            off += n
    return out, off


def pack_blocks(W, blocks):
    K = W.shape[0]
    kt = K // 128
    Wr = W.reshape(kt, 128, W.shape[1] // 128, 128)
    sel = Wr[:, :, blocks, :]
    return np.ascontiguousarray(sel.transpose(1, 0, 2, 3)).reshape(128, -1)


def pack_weights(inp, layers):
    chunks, total = all_chunks(layers)
    wpk = np.zeros((128, total), np.float32)
    for key, off, n in chunks:
        kind = key[0]
        li = key[1]
        j = li // 3
        if kind == "a_in":
            m = key[2]
            v = pack_blocks(inp["a_w_in"][j], [m, CT + m])
        elif kind == "a_dg":
            m = key[2]
            dw = inp["a_dw"][j][:, m * 128:(m + 1) * 128]
            v = np.zeros((128, CONVW, 128), np.float32)
            v[np.arange(128), :, np.arange(128)] = dw.T
            v = np.ascontiguousarray(v[:, NDVE:, :]).reshape(128, -1)
        elif kind == "b_in":
            i = key[2]
            if i < 4:
                v = pack_blocks(inp["b_w_in"][j], [CT + 2 * i, CT + 2 * i + 1])
            else:
                v = pack_blocks(inp["b_w_in"][j], [2 * (i - 4), 2 * (i - 4) + 1])
        elif kind == "c_in":
            i = key[2]
            v = pack_blocks(inp["c_w_in"][j], [2 * i, 2 * i + 1])
        elif kind == "c_grp":
            G = inp["c_w_grp"][j]
            Gr = G.reshape(4, 2, 128, 2, 128)
            v = np.ascontiguousarray(Gr.transpose(2, 0, 1, 3, 4)).reshape(128, -1)
        elif kind == "mix_out":
            h = key[2]
            W = {0: inp["a_w_out"], 1: inp["b_w_out"], 2: inp["c_w_out"]}[li % 3][j]
            v = pack_blocks(W, [2 * h + q for q in range(2)])
        elif kind == "f_up":
            v = pack_blocks(inp["f_w_up"][li], [key[2], NP + key[2]])
        elif kind == "f_down":
            v = pack_blocks(inp["f_w_down"][li], [key[2]])
        assert v.shape[1] == n, (key, v.shape, n)
        wpk[:, off:off + n] = v
    return wpk


def vec8(v):
    return np.ascontiguousarray(v.reshape(-1, 128).T)


def pack_cpk(inp, core_is_start):
    c = np.zeros((128, CPK.n), np.float32)

    def put(name, arr):
        o, n = CPK.off[name]
        assert arr.shape == (128, n), (name, arr.shape, n)
        c[:, o:o + n] = arr

    for li in range(DEPTH):
        for nm in ("ln1_g", "ln1_b", "ln2_g", "ln2_b"):
            put((nm, li), vec8(inp[nm][li]))
        fd = inp["f_dw"][li]
        put(("fdw", li), np.ascontiguousarray(
            fd.reshape(3, 2 * NP, 128).transpose(2, 0, 1)).reshape(128, -1))
    for j in range(2):
        dw = inp["a_dw"][j]
        put(("a_dw", j), np.ascontiguousarray(
            dw.reshape(CONVW, CT, 128).transpose(2, 1, 0)).reshape(128, -1))
        put(("a_dw_b", j), vec8(inp["a_dw_b"][j]))
        put(("a_ln_g", j), vec8(inp["a_ln_g"][j]))
        put(("a_ln_b", j), vec8(inp["a_ln_b"][j]))
    put("b_ln_g", vec8(inp["b_ln_g"][0]))
    put("b_ln_b", vec8(inp["b_ln_b"][0]))
    put("c_scale", vec8(inp["c_scale"][0]))
    put("bsb", np.broadcast_to(inp["b_bs"][0].reshape(1, CT * 128), (128, CT * 128)))
    put("flag", np.full((128, 1), 0.0 if core_is_start else 1.0, np.float32))
    put("eps", np.full((128, 1), EPS, np.float32))
    ic = np.zeros((4, 16), np.float32)
    for g, w in enumerate((2, 4, 8, 16)):
        for t in range(16):
            ic[g, t] = 1.0 / (min(t + 1, w) if core_is_start else w)
    put("icnt", np.broadcast_to(ic.reshape(1, 64), (128, 64)))
    return c


def pack_stage(inp):
    s = np.zeros((128, STG_COLS), np.float32)
    ws = inp["b_ws"][0]
    s[:, :CT * 128] = np.ascontiguousarray(ws.transpose(2, 0, 1)).reshape(128, CT * 128)
    idx = np.arange(128)
    s[:, CT * 128:CT * 128 + 128] = (idx[:, None] <= idx[None, :]).astype(np.float32)
    s[:, CT * 128 + 128:] = np.eye(128, dtype=np.float32)
    return s


class Planner:
    def __init__(self):
        self.streams = {e: [] for e in ("pe", "act", "dve", "pool", "sp")}
        self.cnt = {e: 0 for e in self.streams}
        self.seen = {e: {} for e in self.streams}
        self.lastw = {}
        self.readers = {}
        self.dmacnt = {}

    def _deps(self, eng, reads, writes):
        need = {}

        def add(d, is_raw):
            if d is None:
                return
            k, v = d
            if k == eng and (eng == "pe" or not is_raw):
                return
            if need.get(k, 0) < v:
                need[k] = v
        for r in reads:
            add(self.lastw.get(r), True)
        for r in writes:
            add(self.lastw.get(r), False)
            for rd in self.readers.get(r, ()):
                add(rd, False)
        for k, v in need.items():
            if self.seen[eng].get(k, 0) < v:
                self.seen[eng][k] = v
                self.streams[eng].append(("wait", k, v))

    def _record(self, tag, reads, writes):
        for r in writes:
            self.lastw[r] = tag
            self.readers[r] = []
        for r in reads:
            self.readers.setdefault(r, []).append(tag)

    def op(self, eng, fn, reads=(), writes=(), inc=True):
        self._deps(eng, reads, writes)
        if inc:
            self.cnt[eng] += 1
            tag = (eng, self.cnt[eng])
        else:
            tag = (eng, self.cnt[eng] + 1)
        self.streams[eng].append(("op", fn, eng if inc else None))
        self._record(tag, reads, writes)

    def dma(self, eng, fn, semkey, reads=(), writes=(), extra_waits=()):
        self._deps(eng, reads, writes)
        for k, v in extra_waits:
            if self.seen[eng].get(k, 0) < v:
                self.seen[eng][k] = v
                self.streams[eng].append(("wait", k, v))
        self.dmacnt[semkey] = self.dmacnt.get(semkey, 0) + 16
        tag = (semkey, self.dmacnt[semkey])
        self.streams[eng].append(("dma", fn, semkey))
        self._record(tag, reads, writes)
        return tag

    def wait(self, eng, tag):
        k, v = tag
        if self.seen[eng].get(k, 0) < v:
            self.seen[eng][k] = v
            self.streams[eng].append(("wait", k, v))


def build_program(layers, tiles, n_store_tiles, tok_in, tok_out):
    nc = bass.Bass("TRN2", target_bir_lowering=False)
    chunks, wcols = all_chunks(layers)
    chunk_by_key = {k: (i, off, n) for i, (k, off, n) in enumerate(chunks)}
    nchunk = len(chunks)

    xin = nc.dram_tensor("xin", [128, CT, tok_in], F32, kind="ExternalInput").ap()
    wpk = nc.dram_tensor("wpk", [128, wcols], F32, kind="ExternalInput").ap()
    cpk_d = nc.dram_tensor("cpk", [128, CPK.n], F32, kind="ExternalInput").ap()
    stg_d = nc.dram_tensor("stg", [128, STG_COLS], F32, kind="ExternalInput").ap()
    yout = nc.dram_tensor("yout", [128, CT, tok_out], F32, kind="ExternalOutput").ap()
    wsc = nc.dram_tensor("wsc", [128, wcols], BF16, kind="Internal").ap()

    P = Planner()
    import contextlib
    es = contextlib.ExitStack()
    with es:
        def sb(name, shape, dt):
            return es.enter_context(nc.sbuf_tensor(name, shape, dt))

        xres = sb("xres", [128, CT, TMAX], F32)
        xb = sb("xb", [128, CT, TMAX], BF16)
        zbuf = sb("zbuf", [128, CT, TMAX], F32)
        zb = sb("zb", [128, CT, TMAX], BF16)
        zsq = sb("zsq", [128, CT, TMAX], BF16)
        ub = sb("ub", [128, CT, TMAX], BF16)
        hid = sb("hid", [128, NP, TMAX], BF16)
        mix = sb("mix", [128, CT, HOFF + TMAX], F32)
        lnt = sb("lnt", [128, 3, TMAX], F32)
        tmp = sb("tmp", [128, 4, TMAX], F32)
        cacc = sb("cacc", [128, 4, TMAX], F32)
        cpk = sb("cpk_sb", [128, CPK.n], F32)
        wsTm = sb("wsTm", [128, CT, 128], BF16)
        identb = sb("identb", [128, 128], BF16)
        ones = sb("ones", [128, 128], BF16)
        hp = sb("hp", [128, DEPTH * 2, 2 * NP, 2], F32)
        cstate = sb("cstate", [128, DEPTH, CT, 32], F32)
        edge = sb("edge", [128, 2 * NP, 3], F32)
        scr = sb("scr", [128, 8], F32)
        wslot = [sb(f"wslot{i}", [128, SLOT_COLS], BF16) for i in range(NSLOT)]
        ps = [es.enter_context(nc.psum_tensor(f"ps{i}", [128, TMAX], F32)) for i in range(8)]

        sems = {}
        for e in ("pe", "act", "dve", "pool", "sp"):
            sems[e] = es.enter_context(nc.semaphore(f"c_{e}"))
        for i in range(NSLOT):
            sems[("w", i)] = es.enter_context(nc.semaphore(f"w{i}"))
        for i in range(NSLOT):
            sems[("wb", i)] = es.enter_context(nc.semaphore(f"wb{i}"))
        for k in ("xin", "out", "cst"):
            sems[k] = es.enter_context(nc.semaphore(f"d_{k}"))

        def pvcol(name, c=0, n=1):
            o = CPK[name] + c
            return cpk[:, o:o + n]

        def act(out, in_, func, reads, writes, scale=1.0, bias=0.0):
            P.op("act", lambda h: h.activation(out=out, in_=in_, func=func, bias=bias, scale=scale),
                 reads, writes)

        def tt(eng, out, in0, in1, op, reads, writes):
            P.op(eng, lambda h: h.tensor_tensor(out=out, in0=in0, in1=in1, op=op), reads, writes)

        def stt(eng, out, in0, scalar, in1, op0, op1, reads, writes):
            P.op(eng, lambda h: h.scalar_tensor_tensor(out=out, in0=in0, scalar=scalar, in1=in1,
                                                       op0=op0, op1=op1), reads, writes)

        def ts(eng, out, in0, s1, s2, op0, op1, reads, writes):
            if op1 is None:
                P.op(eng, lambda h: h.tensor_scalar(out=out, in0=in0, scalar1=s1, scalar2=None, op0=op0),
                     reads, writes)
            else:
                P.op(eng, lambda h: h.tensor_scalar(out=out, in0=in0, scalar1=s1, scalar2=s2,
                                                    op0=op0, op1=op1), reads, writes)

        def mm(out, lhsT, rhs, start, stop, reads, writes, inc):
            P.op("pe", lambda h: h.matmul(out, lhsT, rhs, start=start, stop=stop), reads, writes, inc=inc)

        bank_rr = [0]

        pinned = set()

        def next_bank():
            while True:
                b = bank_rr[0]
                bank_rr[0] = (b + 1) % 8
                if b not in pinned:
                    return b

        tmp_rr = [0]
        cacc_rr = [0]

        def next_tmp():
            i = tmp_rr[0]
            tmp_rr[0] = (i + 1) % 4
            return i

        wstate = {"n": 0, "issued": 0}

        def ensure_issued(upto):
            while wstate["issued"] < min(upto, nchunk):
                kk = wstate["issued"]
                key, off, n = chunks[kk]
                s = kk % NSLOT
                P.dma("pool", (lambda o, nn, ss: (lambda h: h.dma_start(out=wslot[ss][:, 0:nn], in_=wpk[:, o:o + nn])))(off, n, s),
                      ("w", s), reads=(), writes=(("wslot", s),))
                P.dma("sp", (lambda o, nn, ss: (lambda h: h.dma_start(out=wsc[:, o:o + nn], in_=wslot[ss][:, 0:nn])))(off, n, s),
                      ("wb", s), reads=(("wslot", s),), writes=(("wsc", kk),))
                wstate["issued"] += 1

        def load_chunk(key, first_pass):
            ci, off, n = chunk_by_key[key]
            k = wstate["n"]
            wstate["n"] += 1
            assert k % nchunk == ci, (key, k, ci)
            s = k % NSLOT
            if k < nchunk:
                ensure_issued(k + LOOKAHEAD)
                return s
            P.dma("sp", (lambda o, nn, ss: (lambda h: h.dma_start(out=wslot[ss][:, 0:nn], in_=wsc[:, o:o + nn])))(off, n, s),
                  ("w", s), reads=(("wsc", ci),), writes=(("wslot", s),))
            return s


        def lhs(slot, nblk, k, blk):
            o = (k * nblk + blk) * 128
            return wslot[slot][:, o:o + 128]

        def layer_norm(T, gname, bname, outs, zbias=None, t_to_xres=False, mid=None, after_rstd=None):
            z, zres = zbuf, "zbuf"
            bm, be = next_bank(), next_bank()
            for c in range(CT):
                zbv = pvcol(zbias, c) if zbias is not None else 0.0
                act(zb[:, c, :T], z[:, c, :T], AF.Identity if zbias is not None else AF.Copy,
                    [(zres, c)], [("zb", c)], bias=zbv)
                act(zsq[:, c, :T], z[:, c, :T], AF.Square, [(zres, c)], [("zsq", c)], bias=zbv)
            act(scr[:, 0:1], pvcol("eps"), AF.Ln, [], ["scr"])
            for c in range(CT):
                mm(ps[bm][:, :T], ones[:, :], zb[:, c, :T], c == 0, c == CT - 1,
                   [("zb", c), "ones"], [("ps", bm)], inc=(c == CT - 1))
            for c in range(CT):
                mm(ps[be][:, :T], ones[:, :], zsq[:, c, :T], c == 0, c == CT - 1,
                   [("zsq", c), "ones"], [("ps", be)], inc=(c == CT - 1))
            if mid is not None:
                pinned.update((bm, be))
                mid()
                pinned.difference_update((bm, be))
            msq, var, rstd = (lnt[:, i, :T] for i in range(3))
            act(msq, ps[bm][:, :T], AF.Square, [("ps", bm)], [("lnt", 0)])
            tt("dve", var, ps[be][:, :T], msq, ALU.subtract, [("ps", be), ("lnt", 0)], [("lnt", 1)])

            def keep_warm(bank, n, gate):
                for r in range(n):
                    mm(ps[bank][:, :T], ones[:, :], zb[:, 0, :T], True, True,
                       [("zb", 0), "ones"] + gate, [("ps", bank)], inc=(r == n - 1))
            peek = bank_rr[0]
            while peek in pinned:
                peek = (peek + 1) % 8
            keep_warm(peek, 6, [])
            keep_warm(be, 8, [])
            act(var, var, AF.Ln, [("lnt", 1)], [("lnt", 1)], bias=pvcol("eps"))
            keep_warm(be, 8, [("lnt", 1)])
            act(rstd, var, AF.Exp, [("lnt", 1)], [("lnt", 2)], scale=-0.5)
            slots = {}

            def p1(c):
                i = next_tmp()
                slots[c] = i
                t = tmp[:, i, :T]
                if zbias is not None:
                    stt("dve", t, z[:, c, :T], pvcol(zbias, c), ps[bm][:, :T], ALU.add, ALU.subtract,
                        [(zres, c), ("ps", bm)], [("tmp", i)])
                else:
                    tt("dve", t, z[:, c, :T], ps[bm][:, :T], ALU.subtract, [(zres, c), ("ps", bm)], [("tmp", i)])

            def p2(c):
                i = slots[c]
                t = tmp[:, i, :T]
                if t_to_xres:
                    tt("dve", xres[:, c, :T], t, rstd, ALU.mult, [("tmp", i), ("lnt", 2)], [("xres", c)])
                    src_ap, src_res = xres[:, c, :T], ("xres", c)
                else:
                    tt("dve", t, t, rstd, ALU.mult, [("tmp", i), ("lnt", 2)], [("tmp", i)])
                    src_ap, src_res = t, ("tmp", i)
                for (apf, func, res) in outs:
                    act(apf(c), src_ap, func, [src_res], [(res, c)],
                        scale=pvcol(gname, c), bias=pvcol(bname, c))
            p1(0)
            p1(1)
            if after_rstd is not None:
                after_rstd()
            for c in range(CT):
                p2(c)
                if c + 2 < CT:
                    p1(c + 2)

        xmode = {"ag": None, "ab": None}

        def resid(m, bank, T):
            sc = ALPHA if xmode["ag"] is None else pvcol(xmode["ag"], m)
            stt("dve", zbuf[:, m, :T], xres[:, m, :T], sc, ps[bank][:, :T], ALU.mult, ALU.add,
                [("xres", m), ("ps", bank)], [("zbuf", m)])

        def post_ln(T, li, w, final, after_rstd=None):
            gname, bname = ("ln%d_g" % w, li), ("ln%d_b" % w, li)
            zbias = xmode["ab"]
            if final:
                layer_norm(T, gname, bname, [(lambda c: zbuf[:, c, :T], AF.Identity, "zbuf")], zbias=zbias,
                           after_rstd=after_rstd)
            else:
                layer_norm(T, gname, bname, [(lambda c: xb[:, c, :T], AF.Identity, "xb")], zbias=zbias,
                           t_to_xres=True)
                xmode["ag"], xmode["ab"] = ("ag", li, w), ("ab", li, w)

        def run_units(T, units, rhs, rhs_res, nK, nko):
            def one(s, nblk, banks, k):
                for q in range(nblk):
                    mm(ps[banks[q]][:, :T], lhs(s, nblk, k, q), rhs(k), k == 0, k == nK - 1,
                       [(rhs_res, k), ("wslot", s)], [("ps", banks[q])], inc=(k == nK - 1))
            head = []
            for (key, nblk, cb) in units[:nko]:
                s = load_chunk(key, True)
                head.append((s, nblk, [next_bank() for _ in range(nblk)], cb))
            for k in range(nK):
                for (s, nblk, banks, cb) in head:
                    one(s, nblk, banks, k)
            for (s, nblk, banks, cb) in head:
                cb(banks)
            for (key, nblk, cb) in units[nko:]:
                s = load_chunk(key, True)
                banks = [next_bank() for _ in range(nblk)]
                for q in range(nblk):
                    for k in range(nK):
                        mm(ps[banks[q]][:, :T], lhs(s, nblk, k, q), rhs(k), k == 0, k == nK - 1,
                           [(rhs_res, k), ("wslot", s)], [("ps", banks[q])], inc=(k == nK - 1))
                cb(banks)

        def mix_out_and_resid(li, T):
            def mk(h):
                def cb(banks):
                    for q in range(2):
                        resid(2 * h + q, banks[q], T)
                return cb
            run_units(T, [(("mix_out", li, h), 2, mk(h)) for h in range(4)],
                      lambda k: ub[:, k, :T], "ub", CT, 2)

        def mixer_conf(li, ti, T):
            j = li // 3
            mixb = mix[:].bitcast(BF16)
            base = HOFF - (CONVW - 1)

            def glu(m, banks):
                ba, bg = banks
                i = next_tmp()
                act(tmp[:, i, :T], ps[bg][:, :T], AF.Sigmoid, [("ps", bg)], [("tmp", i)])
                act(mixb[:, m, base:HOFF], cstate[:, li, m, 0:CONVW - 1], AF.Copy,
                    [("cstate", li, m)], [("mix", m)])
                tt("dve", mixb[:, m, HOFF:HOFF + T], ps[ba][:, :T], tmp[:, i, :T], ALU.mult,
                   [("ps", ba), ("tmp", i)], [("mix", m)])

            def conv(m):
                sd = load_chunk(("a_dg", li, m), True)
                bc = next_bank()
                a = cacc_rr[0]
                cacc_rr[0] = (a + 1) % 4
                acc = cacc[:, a, :T]
                o = CPK[("a_dw", j)] + m * CONVW
                ts("dve", acc, mixb[:, m, base:base + T], cpk[:, o:o + 1], None, ALU.mult, None,
                   [("mix", m)], [("cacc", a)])
                for k in range(1, NDVE):
                    stt("dve", acc, mixb[:, m, base + k:base + k + T], cpk[:, o + k:o + k + 1], acc,
                        ALU.mult, ALU.add, [("mix", m), ("cacc", a)], [("cacc", a)])
                for k in range(NDVE, CONVW):
                    mm(ps[bc][:, :T], wslot[sd][:, (k - NDVE) * 128:(k - NDVE + 1) * 128],
                       mixb[:, m, base + k:base + k + T], k == NDVE, k == CONVW - 1, [("mix", m), ("wslot", sd)], [("ps", bc)], inc=(k == CONVW - 1))
                stt("dve", zbuf[:, m, :T], ps[bc][:, :T], pvcol(("a_dw_b", j), m), acc, ALU.add, ALU.add,
                    [("ps", bc), ("cacc", a)], [("zbuf", m)])
                act(cstate[:, li, m, 0:CONVW - 1], mixb[:, m, base + T:HOFF + T], AF.Copy, [("mix", m)],
                    [("cstate", li, m)], scale=(pvcol("flag") if ti == 0 else 1.0))

            rhs = lambda k: xb[:, k, :T]
            run_units(T, [(("a_in", li, m), 2, (lambda mm_: (lambda banks: glu(mm_, banks)))(m)) for m in range(2)],
                      rhs, "xb", CT, 2)
            for m in range(CT):
                if m + 2 < CT:
                    run_units(T, [(("a_in", li, m + 2), 2, (lambda mm_: (lambda banks: glu(mm_, banks)))(m + 2))],
                              rhs, "xb", CT, 0)
                conv(m)
            layer_norm(T, ("a_ln_g", j), ("a_ln_b", j), [(lambda c: ub[:, c, :T], AF.Silu, "ub")])
            mix_out_and_resid(li, T)

        def mixer_sgu(li, ti, T):
            nchk = T // 128
            vnT = hid

            def mk(i):
                def cb(banks):
                    for q in range(2):
                        b = banks[q]
                        if i < 4:
                            c = 2 * i + q
                            act(zbuf[:, c, :T], ps[b][:, :T], AF.Gelu, [("ps", b)], [("zbuf", c)])
                        else:
                            c = 2 * (i - 4) + q
                            act(mix[:, c, HOFF:HOFF + T], ps[b][:, :T], AF.Gelu, [("ps", b)], [("mix", c)])
                return cb
            run_units(T, [(("b_in", li, i), 2, mk(i)) for i in range(4)],
                      lambda k: xb[:, k, :T], "xb", CT, 3)
            layer_norm(T, "b_ln_g", "b_ln_b", [(lambda c: ub[:, c, :T], AF.Identity, "ub")],
                       mid=lambda: run_units(T, [(("b_in", li, i), 2, mk(i)) for i in range(4, CT)],
                                             lambda k: xb[:, k, :T], "xb", CT, 0))
            for h in range(CT):
                b = next_bank()
                pbf = ps[b][:].bitcast(BF16)
                for ck in range(nchk):
                    P.op("pe", (lambda o_, i_: (lambda hh: hh.transpose(o_, i_, identb[:, :])))(
                        pbf[:, ck * 128:(ck + 1) * 128], ub[:, h, ck * 128:(ck + 1) * 128]),
                        [("ub", h), "identb"], [("ps", b)], inc=(ck == nchk - 1))
                act(vnT[:, h, :T], pbf[:, :T], AF.Copy, [("ps", b)], [("hid", h)])
            for h in range(CT):
                b = next_bank()
                for ck in range(nchk):
                    mm(ps[b][:, ck * 128:(ck + 1) * 128], vnT[:, h, ck * 128:(ck + 1) * 128], wsTm[:, h, :],
                       True, True, [("hid", h), "wsTm"], [("ps", b)], inc=(ck == nchk - 1))
                i = next_tmp()
                o = CPK["bsb"] + h * 128
                for ck in range(nchk):
                    tt("dve", tmp[:, i, ck * 128:(ck + 1) * 128], ps[b][:, ck * 128:(ck + 1) * 128],
                       cpk[:, o:o + 128], ALU.add, [("ps", b)], [("tmp", i)])
                tt("dve", ub[:, h, :T], tmp[:, i, :T], mix[:, h, HOFF:HOFF + T], ALU.mult,
                   [("tmp", i), ("mix", h)], [("ub", h)])
            mix_out_and_resid(li, T)

        psc = sb("psc", [128, 2, HOFF + TMAX], F32)

        def mixer_pool(li, ti, T):
            H = 16

            def mk(i):
                def cb(banks):
                    for q in range(2):
                        c = 2 * i + q
                        b = banks[q]
                        act(mix[:, c, HOFF - H:HOFF], cstate[:, li, c, 0:H], AF.Copy, [("cstate", li, c)], [("mix", c)])
                        act(mix[:, c, HOFF:HOFF + T], ps[b][:, :T], AF.Copy, [("ps", b)], [("mix", c)])
                return cb
            run_units(T, [(("c_in", li, i), 2, mk(i)) for i in range(4)],
                      lambda k: xb[:, k, :T], "xb", CT, 3)
            E = HOFF + T
            for g in range(4):
                w = 2 << g
                for q in range(2):
                    c = 2 * g + q
                    cur = mix[:, c, :]
                    cur_res = ("mix", c)
                    lo = HOFF - H
                    step = 1
                    pi = 0
                    while step < w:
                        nlo = lo + step
                        dst = psc[:, pi, :]
                        tt("dve", dst[:, nlo:E], cur[:, nlo:E], cur[:, nlo - step:E - step], ALU.add,
                           [cur_res], [("psc", pi)])
                        cur, cur_res, lo = dst, ("psc", pi), nlo
                        pi ^= 1
                        step *= 2
                    stt("dve", zb[:, c, :T], cur[:, HOFF:E], 1.0 / w, mix[:, c, HOFF:E], ALU.mult, ALU.subtract,
                        [cur_res, ("mix", c)], [("zb", c)])
                    if ti == 1:
                        i = next_tmp()
                        o = CPK["icnt"] + g * 16
                        tt("dve", tmp[:, i, 0:16], cur[:, HOFF:HOFF + 16], cpk[:, o:o + 16], ALU.mult,
                           [cur_res], [("tmp", i)])
                        tt("dve", zb[:, c, 0:16], tmp[:, i, 0:16], mix[:, c, HOFF:HOFF + 16], ALU.subtract,
                           [("tmp", i), ("mix", c), ("zb", c)], [("zb", c)])
                    act(cstate[:, li, c, 0:H], mix[:, c, E - H:E], AF.Copy, [("mix", c)], [("cstate", li, c)],
                        scale=(pvcol("flag") if ti == 0 else 1.0))
            s = load_chunk(("c_grp", li), True)
            for g in range(4):
                for dd in range(2):
                    c = 2 * g + dd
                    b = next_bank()
                    for kk in range(2):
                        o = ((g * 2 + kk) * 2 + dd) * 128
                        mm(ps[b][:, :T], wslot[s][:, o:o + 128], zb[:, 2 * g + kk, :T], kk == 0, kk == 1,
                           [("zb", 2 * g + kk), ("wslot", s)], [("ps", b)], inc=(kk == 1))
                    act(ub[:, c, :T], ps[b][:, :T], AF.Copy, [("ps", b)], [("ub", c)], scale=pvcol("c_scale", c))
            mix_out_and_resid(li, T)

        def ffn(li, ti, T, state_only=False):
            par = ti % 2
            fo = CPK[("fdw", li)]
            hrow = 2 * li + par

            def wcol(kk, c):
                o = fo + kk * 2 * NP + c
                return cpk[:, o:o + 1]

            def wrow(kk):
                o = fo + kk * 2 * NP
                return cpk[:, o:o + 2 * NP]
            hres = [("hp", li, par, c) for c in range(2 * NP)]
            tt("pool", edge[:, :, 0], hp[:, hrow, :, 1], wrow(1), ALU.mult, hres, ["edge"])
            tt("pool", edge[:, :, 2], hp[:, hrow, :, 0], wrow(0), ALU.mult, hres, ["edge"])
            tt("pool", edge[:, :, 0], edge[:, :, 0], edge[:, :, 2], ALU.add, ["edge"], ["edge"])
            tt("pool", edge[:, :, 1], hp[:, hrow, :, 1], wrow(0), ALU.mult, hres, ["edge"])

            def mk(j):
                def cb(banks):
                    cs = (j, NP + j)
                    ai = ((2 * j) % 4, (2 * j + 1) % 4)
                    for q in range(2):
                        b, c, a = banks[q], cs[q], ai[q]
                        if state_only:
                            act(hp[:, 2 * li + 1 - par, c, :], ps[b][:, T - 2:T], AF.Copy, [("ps", b)],
                                [("hp", li, 1 - par, c)], scale=(pvcol("flag") if ti == 0 else 1.0))
                            continue
                        acc = cacc[:, a, :]
                        act(acc[:, 2:T], ps[b][:, 2:T], AF.Copy, [("ps", b)], [("cacc", a)], scale=wcol(2, c))
                        act(acc[:, 0:1], ps[b][:, 0:1], AF.Identity, [("ps", b), "edge"], [("cacc", a)],
                            scale=wcol(2, c), bias=edge[:, c, 0:1])
                        act(acc[:, 1:2], ps[b][:, 1:2], AF.Identity, [("ps", b), "edge"], [("cacc", a)],
                            scale=wcol(2, c), bias=edge[:, c, 1:2])
                        stt("dve", acc[:, 1:T], ps[b][:, 0:T - 1], wcol(1, c), acc[:, 1:T], ALU.mult, ALU.add,
                            [("ps", b), ("cacc", a)], [("cacc", a)])
                        stt("dve", acc[:, 2:T], ps[b][:, 0:T - 2], wcol(0, c), acc[:, 2:T], ALU.mult, ALU.add,
                            [("ps", b), ("cacc", a)], [("cacc", a)])
                        act(hp[:, 2 * li + 1 - par, c, :], ps[b][:, T - 2:T], AF.Copy, [("ps", b)],
                            [("hp", li, 1 - par, c)], scale=(pvcol("flag") if ti == 0 else 1.0))
                    if state_only:
                        return
                    i = next_tmp()
                    act(tmp[:, i, :T], cacc[:, ai[0], :T], AF.Silu, [("cacc", ai[0])], [("tmp", i)])
                    tt("pool", hid[:, j, :T], tmp[:, i, :T], cacc[:, ai[1], :T], ALU.mult,
                       [("tmp", i), ("cacc", ai[1])], [("hid", j)])
                return cb
            run_units(T, [(("f_up", li, j), 2, mk(j)) for j in range(NP)],
                      lambda k: xb[:, k, :T], "xb", CT, 3)

            def mkd(m):
                def cb(banks):
                    resid(m, banks[0], T)
                return cb
            if state_only:
                for m in range(CT):
                    load_chunk(("f_down", li, m), True)
                return
            run_units(T, [(("f_down", li, m), 1, mkd(m)) for m in range(CT)],
                      lambda k: hid[:, k, :T], "hid", NP, 0)

        all_x = [("xres", c) for c in range(CT)]
        P.dma("sp", lambda h: h.dma_start(out=cpk[:, :], in_=cpk_d[:, :]), "cst", writes=("cpk",))
        zflat = zbuf[:].rearrange("p c t -> p (c t)")
        P.dma("sp", lambda h: h.dma_start(out=zflat[:, 0:STG_COLS], in_=stg_d[:, :]), "cst",
              writes=[("zbuf", c) for c in range(CT)])
        cst_tag = ("cst", 32)
        for e_ in ("pe", "act", "dve", "pool"):
            P.wait(e_, cst_tag)
        P.op("dve", lambda h: h.memset(ones[:, :], 1.0 / D), (), ("ones",))
        P.op("dve", lambda h: h.memset(hp[:].rearrange("p a c t -> p (a c t)"), 0.0), (),
             [("hp", l_, a, c) for l_ in range(DEPTH) for a in range(2) for c in range(2 * NP)])
        P.op("dve", lambda h: h.memset(cstate[:].rearrange("p a c t -> p (a c t)"), 0.0), (),
             [("cstate", l_, c) for l_ in range(DEPTH) for c in range(CT)])
        for c in range(CT):
            P.op("dve", (lambda cc: (lambda h: h.memset(mix[:, cc, :], 0.0)))(c), (), (("mix", c),))
        zr = [("zbuf", c) for c in range(CT)] + ["cpk"]
        for h_ in range(CT):
            tt("dve", wsTm[:, h_, :], zflat[:, h_ * 128:(h_ + 1) * 128], zflat[:, CT * 128:CT * 128 + 128],
               ALU.mult, zr, ("wsTm",))
        P.op("dve", lambda h: h.tensor_copy(out=identb[:, :], in_=zflat[:, CT * 128 + 128:CT * 128 + 256]),
             zr, ("identb",))

        n_tiles = len(tiles)

        def load_x(ti):
            t0_, T_ = tiles[ti]
            P.dma("sp", (lambda a_, b_: (lambda h: h.dma_start(out=xres[:, :, :b_], in_=xin[:, :, a_:a_ + b_])))(t0_, T_),
                  "xin", reads=(), writes=all_x)

        for li in range(DEPTH):
            for w in (1, 2):
                for (dst, srcn) in ((("ag", li, w), ("ln%d_g" % w, li)), (("ab", li, w), ("ln%d_b" % w, li))):
                    o_d, o_s = CPK[dst], CPK[srcn]
                    ts("dve", cpk[:, o_d:o_d + CT], cpk[:, o_s:o_s + CT], ALPHA, None, ALU.mult, None,
                       ["cpk"], ["cpk"])
        load_x(0)
        cast_done = set()

        def cast_xb(ti_):
            T_ = tiles[ti_][1]
            for c in range(CT):
                act(xb[:, c, :T_], xres[:, c, :T_], AF.Copy, [("xres", c), "cpk"], [("xb", c)])
            cast_done.add(ti_)

        for ti, (t0, T) in enumerate(tiles):
            stored = ti >= n_tiles - n_store_tiles
            xmode["ag"], xmode["ab"] = None, None
            if ti not in cast_done:
                cast_xb(ti)
            for li_idx, li in enumerate(layers):
                last_layer = li_idx == len(layers) - 1
                kind = li % 3
                if kind == 0:
                    mixer_conf(li, ti, T)
                elif kind == 1:
                    mixer_sgu(li, ti, T)
                else:
                    mixer_pool(li, ti, T)
                post_ln(T, li, 1, False)
                ffn(li, ti, T, state_only=(last_layer and not stored))
                if last_layer:
                    if ti + 1 < n_tiles:
                        load_x(ti + 1)
                    if stored:
                        post_ln(T, li, 2, True,
                                after_rstd=((lambda t_=ti + 1: cast_xb(t_)) if ti + 1 < n_tiles else None))
                else:
                    post_ln(T, li, 2, False)
            if stored:
                o0 = sum(tt_[1] for tt_ in tiles[n_tiles - n_store_tiles:ti])
                P.dma("act", (lambda a, b_: (lambda h: h.dma_start(out=yout[:, :, a:a + b_], in_=zbuf[:, :, :b_])))(o0, T),
                      "out", reads=[("zbuf", c) for c in range(CT)], writes=())
        P.wait("act", ("out", P.dmacnt.get("out", 0)))

        handles = {}

        def emit(ename, h):
            for item in P.streams[ename]:
                if item[0] == "wait":
                    h.wait_ge(sems[item[1]], item[2])
                elif item[0] == "op":
                    ins = item[1](h)
                    if item[2] is not None:
                        ins.then_inc(sems[item[2]], 1)
                else:
                    item[1](h).then_inc(sems[item[2]], 16)

        with nc.Block() as block:
            @block.sync
            def _(h):
                emit("sp", h)

            @block.gpsimd
            def _(h):
                emit("pool", h)

            @block.scalar
            def _(h):
                emit("act", h)

            @block.vector
            def _(h):
                emit("dve", h)

            @block.tensor
            def _(h):
                emit("pe", h)
    return nc


def make_tiles(tok_per_core):
    tiles = [(0, HALO)]
    t = HALO
    while t < HALO + tok_per_core:
        T = min(TMAX, HALO + tok_per_core - t)
        tiles.append((t, T))
        t += T
    return tiles


def run(inputs, layers=(0, 1, 2, 3), trace=False):
    x = np.asarray(inputs["x"], np.float32)
    inp = {k: np.asarray(v, np.float32) for k, v in inputs.items()}
    B, S, _ = x.shape
    cores_per_seq = NCORES // B
    tpc = S // cores_per_seq
    tiles = make_tiles(tpc)
    n_store = len(tiles) - 1
    layers = list(layers)
    nc = build_program(layers, tiles, n_store, HALO + tpc, tpc)
    wpk = pack_weights(inp, layers)
    stg = pack_stage(inp)
    in_maps = []
    for core in range(NCORES):
        b, seg = divmod(core, cores_per_seq)
        p0 = seg * tpc
        xs = np.zeros((HALO + tpc, D), np.float32)
        if seg == 0:
            xs[HALO:] = x[b, 0:tpc]
        else:
            xs[:] = x[b, p0 - HALO:p0 + tpc]
        xin = np.ascontiguousarray(xs.reshape(HALO + tpc, CT, 128).transpose(2, 1, 0))
        in_maps.append({"xin": xin, "wpk": wpk, "cpk": pack_cpk(inp, seg == 0), "stg": stg})
    res = run_bass_kernel_spmd(nc, in_maps, core_ids=list(range(NCORES)), trace=trace)
    out = np.zeros((B, S, D), np.float32)
    for core in range(NCORES):
        b, seg = divmod(core, cores_per_seq)
        y = res.results[core]["yout"]
        out[b, seg * tpc:(seg + 1) * tpc] = y.transpose(2, 1, 0).reshape(tpc, D)
    return out, res


def kernel(**inputs):
    out, _ = run(inputs)
    return out
```

```python
import numpy as np
import concourse.bass as bass
import concourse.mybir as mybir
from concourse.bass_utils import run_bass_kernel_spmd

F32 = mybir.dt.float32
BF16 = mybir.dt.bfloat16
AF = mybir.ActivationFunctionType
ALU = mybir.AluOpType

D = 1024
CT = 8
DFF = 2816
NP = 22
DEPTH = 4
NCORES = 8
HALO = 256
TMAX = 512
ALPHA = float((2 * DEPTH) ** 0.25)
EPS = 1e-5
SLOT_COLS = 3072
NSLOT = 7
LOOKAHEAD = 4
NCAST_SEM = 8
CAST_AHEAD = 10
CONVW = 31
NWARM = 5
NDVE = 7
HOFF = 32


class Layout:
    def __init__(self):
        self.off = {}
        self.n = 0

    def add(self, name, ncols):
        self.off[name] = (self.n, ncols)
        self.n += ncols

    def __getitem__(self, name):
        return self.off[name][0]


def make_cpk_layout():
    L = Layout()
    for li in range(DEPTH):
        for nm in ("ln1_g", "ln1_b", "ln2_g", "ln2_b"):
            L.add((nm, li), CT)
        L.add(("fdw", li), 3 * 2 * NP)
    for j in range(2):
        L.add(("a_dw", j), CT * CONVW)
        L.add(("a_dw_b", j), CT)
        L.add(("a_ln_g", j), CT)
        L.add(("a_ln_b", j), CT)
    L.add("b_ln_g", CT)
    L.add("b_ln_b", CT)
    L.add("c_scale", CT)
    L.add("bsb", CT * 128)
    L.add("flag", 1)
    L.add("eps", 1)
    L.add("icnt", 4 * 16)
    for li in range(DEPTH):
        for w in (1, 2):
            L.add(("ag", li, w), CT)
            L.add(("ab", li, w), CT)
    return L


CPK = make_cpk_layout()
STG_COLS = CT * 128 + 128 + 128


def layer_chunks(li):
    kind = li % 3
    ch = []
    if kind == 0:
        ch.append((("a_in", li, 0), 8 * 2 * 128))
        ch.append((("a_in", li, 1), 8 * 2 * 128))
        for m in range(CT):
            if m + 2 < CT:
                ch.append((("a_in", li, m + 2), 8 * 2 * 128))
            ch.append((("a_dg", li, m), (CONVW - NDVE) * 128))
        for h in range(4):
            ch.append((("mix_out", li, h), 8 * 2 * 128))
    elif kind == 1:
        for i in range(CT):
            ch.append((("b_in", li, i), 8 * 2 * 128))
        for h in range(4):
            ch.append((("mix_out", li, h), 8 * 2 * 128))
    else:
        for i in range(4):
            ch.append((("c_in", li, i), 8 * 2 * 128))
        ch.append((("c_grp", li), 4 * 2 * 2 * 128))
        for h in range(4):
            ch.append((("mix_out", li, h), 8 * 2 * 128))
    for j in range(NP):
        ch.append((("f_up", li, j), 8 * 2 * 128))
    for m in range(CT):
        ch.append((("f_down", li, m), NP * 128))
    return ch


def all_chunks(layers):
    off = 0
    out = []
    for li in layers:
        for key, n in layer_chunks(li):
            out.append((key, off, n))
            off += n
    return out, off


def pack_blocks(W, blocks):
    K = W.shape[0]
    kt = K // 128
    Wr = W.reshape(kt, 128, W.shape[1] // 128, 128)
    sel = Wr[:, :, blocks, :]
    return np.ascontiguousarray(sel.transpose(1, 0, 2, 3)).reshape(128, -1)


def pack_weights(inp, layers):
    chunks, total = all_chunks(layers)
    wpk = np.zeros((128, total), np.float32)
    for key, off, n in chunks:
        kind = key[0]
        li = key[1]
        j = li // 3
        if kind == "a_in":
            m = key[2]
            v = pack_blocks(inp["a_w_in"][j], [m, CT + m])
        elif kind == "a_dg":
            m = key[2]
            dw = inp["a_dw"][j][:, m * 128:(m + 1) * 128]
            v = np.zeros((128, CONVW, 128), np.float32)
            v[np.arange(128), :, np.arange(128)] = dw.T
            v = np.ascontiguousarray(v[:, NDVE:, :]).reshape(128, -1)
        elif kind == "b_in":
            i = key[2]
            if i < 4:
                v = pack_blocks(inp["b_w_in"][j], [CT + 2 * i, CT + 2 * i + 1])
            else:
                v = pack_blocks(inp["b_w_in"][j], [2 * (i - 4), 2 * (i - 4) + 1])
        elif kind == "c_in":
            i = key[2]
            v = pack_blocks(inp["c_w_in"][j], [2 * i, 2 * i + 1])
        elif kind == "c_grp":
            G = inp["c_w_grp"][j]
            Gr = G.reshape(4, 2, 128, 2, 128)
            v = np.ascontiguousarray(Gr.transpose(2, 0, 1, 3, 4)).reshape(128, -1)
        elif kind == "mix_out":
            h = key[2]
            W = {0: inp["a_w_out"], 1: inp["b_w_out"], 2: inp["c_w_out"]}[li % 3][j]
            v = pack_blocks(W, [2 * h + q for q in range(2)])
        elif kind == "f_up":
            v = pack_blocks(inp["f_w_up"][li], [key[2], NP + key[2]])
        elif kind == "f_down":
            v = pack_blocks(inp["f_w_down"][li], [key[2]])
        assert v.shape[1] == n, (key, v.shape, n)
        wpk[:, off:off + n] = v
    return wpk


def vec8(v):
    return np.ascontiguousarray(v.reshape(-1, 128).T)


def pack_cpk(inp, core_is_start):
    c = np.zeros((128, CPK.n), np.float32)

    def put(name, arr):
        o, n = CPK.off[name]
        assert arr.shape == (128, n), (name, arr.shape, n)
        c[:, o:o + n] = arr

    for li in range(DEPTH):
        for nm in ("ln1_g", "ln1_b", "ln2_g", "ln2_b"):
            put((nm, li), vec8(inp[nm][li]))
        fd = inp["f_dw"][li]
        put(("fdw", li), np.ascontiguousarray(
            fd.reshape(3, 2 * NP, 128).transpose(2, 0, 1)).reshape(128, -1))
    for j in range(2):
        dw = inp["a_dw"][j]
        put(("a_dw", j), np.ascontiguousarray(
            dw.reshape(CONVW, CT, 128).transpose(2, 1, 0)).reshape(128, -1))
        put(("a_dw_b", j), vec8(inp["a_dw_b"][j]))
        put(("a_ln_g", j), vec8(inp["a_ln_g"][j]))
        put(("a_ln_b", j), vec8(inp["a_ln_b"][j]))
    put("b_ln_g", vec8(inp["b_ln_g"][0]))
    put("b_ln_b", vec8(inp["b_ln_b"][0]))
    put("c_scale", vec8(inp["c_scale"][0]))
    put("bsb", np.broadcast_to(inp["b_bs"][0].reshape(1, CT * 128), (128, CT * 128)))
    put("flag", np.full((128, 1), 0.0 if core_is_start else 1.0, np.float32))
    put("eps", np.full((128, 1), EPS, np.float32))
    ic = np.zeros((4, 16), np.float32)
    for g, w in enumerate((2, 4, 8, 16)):
        for t in range(16):
            ic[g, t] = 1.0 / (min(t + 1, w) if core_is_start else w)
    put("icnt", np.broadcast_to(ic.reshape(1, 64), (128, 64)))
    return c


def pack_stage(inp):
    s = np.zeros((128, STG_COLS), np.float32)
    ws = inp["b_ws"][0]
    s[:, :CT * 128] = np.ascontiguousarray(ws.transpose(2, 0, 1)).reshape(128, CT * 128)
    idx = np.arange(128)
    s[:, CT * 128:CT * 128 + 128] = (idx[:, None] <= idx[None, :]).astype(np.float32)
    s[:, CT * 128 + 128:] = np.eye(128, dtype=np.float32)
    return s


class Planner:
    def __init__(self):
        self.streams = {e: [] for e in ("pe", "act", "dve", "pool", "sp")}
        self.cnt = {e: 0 for e in self.streams}
        self.seen = {e: {} for e in self.streams}
        self.lastw = {}
        self.readers = {}
        self.dmacnt = {}

    def _deps(self, eng, reads, writes):
        need = {}

        def add(d, is_raw):
            if d is None:
                return
            k, v = d
            if k == eng and (eng == "pe" or not is_raw):
                return
            if need.get(k, 0) < v:
                need[k] = v
        for r in reads:
            add(self.lastw.get(r), True)
        for r in writes:
            add(self.lastw.get(r), False)
            for rd in self.readers.get(r, ()):
                add(rd, False)
        for k, v in need.items():
            if self.seen[eng].get(k, 0) < v:
                self.seen[eng][k] = v
                self.streams[eng].append(("wait", k, v))

    def _record(self, tag, reads, writes):
        for r in writes:
            self.lastw[r] = tag
            self.readers[r] = []
        for r in reads:
            self.readers.setdefault(r, []).append(tag)

    def op(self, eng, fn, reads=(), writes=(), inc=True):
        self._deps(eng, reads, writes)
        if inc:
            self.cnt[eng] += 1
            tag = (eng, self.cnt[eng])
        else:
            tag = (eng, self.cnt[eng] + 1)
        self.streams[eng].append(("op", fn, eng if inc else None))
        self._record(tag, reads, writes)

    def dma(self, eng, fn, semkey, reads=(), writes=(), extra_waits=()):
        self._deps(eng, reads, writes)
        for k, v in extra_waits:
            if self.seen[eng].get(k, 0) < v:
                self.seen[eng][k] = v
                self.streams[eng].append(("wait", k, v))
        self.dmacnt[semkey] = self.dmacnt.get(semkey, 0) + 16
        tag = (semkey, self.dmacnt[semkey])
        self.streams[eng].append(("dma", fn, semkey))
        self._record(tag, reads, writes)
        return tag

    def wait(self, eng, tag):
        k, v = tag
        if self.seen[eng].get(k, 0) < v:
            self.seen[eng][k] = v
            self.streams[eng].append(("wait", k, v))


def build_program(layers, tiles, n_store_tiles, tok_in, tok_out):
    nc = bass.Bass("TRN2", target_bir_lowering=False)
    chunks, wcols = all_chunks(layers)
    chunk_by_key = {k: (i, off, n) for i, (k, off, n) in enumerate(chunks)}
    nchunk = len(chunks)

    xin = nc.dram_tensor("xin", [128, CT, tok_in], F32, kind="ExternalInput").ap()
    wpk = nc.dram_tensor("wpk", [128, wcols], F32, kind="ExternalInput").ap()
    cpk_d = nc.dram_tensor("cpk", [128, CPK.n], F32, kind="ExternalInput").ap()
    stg_d = nc.dram_tensor("stg", [128, STG_COLS], F32, kind="ExternalInput").ap()
    yout = nc.dram_tensor("yout", [128, CT, tok_out], F32, kind="ExternalOutput").ap()
    wsc = nc.dram_tensor("wsc", [128, wcols], BF16, kind="Internal").ap()

    P = Planner()
    import contextlib
    es = contextlib.ExitStack()
    with es:
        def sb(name, shape, dt):
            return es.enter_context(nc.sbuf_tensor(name, shape, dt))

        xres = sb("xres", [128, CT, TMAX], F32)
        xb = sb("xb", [128, CT, TMAX], BF16)
        zbuf = sb("zbuf", [128, CT, TMAX], F32)
        zb = sb("zb", [128, CT, TMAX], BF16)
        zsq = sb("zsq", [128, CT, TMAX], BF16)
        ub = sb("ub", [128, CT, TMAX], BF16)
        hid = sb("hid", [128, NP, TMAX], BF16)
        mix = sb("mix", [128, CT, HOFF + TMAX], F32)
        lnt = sb("lnt", [128, 3, TMAX], F32)
        tmp = sb("tmp", [128, 4, TMAX], F32)
        cacc = sb("cacc", [128, 4, TMAX], F32)
        cpk = sb("cpk_sb", [128, CPK.n], F32)
        wsTm = sb("wsTm", [128, CT, 128], BF16)
        identb = sb("identb", [128, 128], BF16)
        ones = sb("ones", [128, 128], BF16)
        hp = sb("hp", [128, DEPTH * 2, 2 * NP, 2], F32)
        cstate = sb("cstate", [128, DEPTH, CT, 32], F32)
        edge = sb("edge", [128, 2 * NP, 3], F32)
        scr = sb("scr", [128, 8], F32)
        wslot = [sb(f"wslot{i}", [128, SLOT_COLS], BF16) for i in range(NSLOT)]
        ps = [es.enter_context(nc.psum_tensor(f"ps{i}", [128, TMAX], F32)) for i in range(8)]

        sems = {}
        for e in ("pe", "act", "dve", "pool", "sp"):
            sems[e] = es.enter_context(nc.semaphore(f"c_{e}"))
        for i in range(NSLOT):
            sems[("w", i)] = es.enter_context(nc.semaphore(f"w{i}"))
        for i in range(NSLOT):
            sems[("wb", i)] = es.enter_context(nc.semaphore(f"wb{i}"))
        for k in ("xin", "out", "cst"):
            sems[k] = es.enter_context(nc.semaphore(f"d_{k}"))

        def pvcol(name, c=0, n=1):
            o = CPK[name] + c
            return cpk[:, o:o + n]

        def act(out, in_, func, reads, writes, scale=1.0, bias=0.0):
            P.op("act", lambda h: h.activation(out=out, in_=in_, func=func, bias=bias, scale=scale),
                 reads, writes)

        def tt(eng, out, in0, in1, op, reads, writes):
            P.op(eng, lambda h: h.tensor_tensor(out=out, in0=in0, in1=in1, op=op), reads, writes)

        def stt(eng, out, in0, scalar, in1, op0, op1, reads, writes):
            P.op(eng, lambda h: h.scalar_tensor_tensor(out=out, in0=in0, scalar=scalar, in1=in1,
                                                       op0=op0, op1=op1), reads, writes)

        def ts(eng, out, in0, s1, s2, op0, op1, reads, writes):
            if op1 is None:
                P.op(eng, lambda h: h.tensor_scalar(out=out, in0=in0, scalar1=s1, scalar2=None, op0=op0),
                     reads, writes)
            else:
                P.op(eng, lambda h: h.tensor_scalar(out=out, in0=in0, scalar1=s1, scalar2=s2,
                                                    op0=op0, op1=op1), reads, writes)

        def mm(out, lhsT, rhs, start, stop, reads, writes, inc):
            P.op("pe", lambda h: h.matmul(out, lhsT, rhs, start=start, stop=stop), reads, writes, inc=inc)

        bank_rr = [0]

        pinned = set()

        def next_bank():
            while True:
                b = bank_rr[0]
                bank_rr[0] = (b + 1) % 8
                if b not in pinned:
                    return b

        tmp_rr = [0]
        cacc_rr = [0]

        def next_tmp():
            i = tmp_rr[0]
            tmp_rr[0] = (i + 1) % 4
            return i

        wstate = {"n": 0, "issued": 0}

        def ensure_issued(upto):
            while wstate["issued"] < min(upto, nchunk):
                kk = wstate["issued"]
                key, off, n = chunks[kk]
                s = kk % NSLOT
                P.dma("pool", (lambda o, nn, ss: (lambda h: h.dma_start(out=wslot[ss][:, 0:nn], in_=wpk[:, o:o + nn])))(off, n, s),
                      ("w", s), reads=(), writes=(("wslot", s),))
                P.dma("sp", (lambda o, nn, ss: (lambda h: h.dma_start(out=wsc[:, o:o + nn], in_=wslot[ss][:, 0:nn])))(off, n, s),
                      ("wb", s), reads=(("wslot", s),), writes=(("wsc", kk),))
                wstate["issued"] += 1

        def load_chunk(key, first_pass):
            ci, off, n = chunk_by_key[key]
            k = wstate["n"]
            wstate["n"] += 1
            assert k % nchunk == ci, (key, k, ci)
            s = k % NSLOT
            if k < nchunk:
                ensure_issued(k + LOOKAHEAD)
                return s
            P.dma("sp", (lambda o, nn, ss: (lambda h: h.dma_start(out=wslot[ss][:, 0:nn], in_=wsc[:, o:o + nn])))(off, n, s),
                  ("w", s), reads=(("wsc", ci),), writes=(("wslot", s),))
            return s


        def lhs(slot, nblk, k, blk):
            o = (k * nblk + blk) * 128
            return wslot[slot][:, o:o + 128]

        def layer_norm(T, gname, bname, outs, zbias=None, t_to_xres=False, mid=None, after_rstd=None):
            z, zres = zbuf, "zbuf"
            bm, be = next_bank(), next_bank()
            for c in range(CT):
                zbv = pvcol(zbias, c) if zbias is not None else 0.0
                act(zb[:, c, :T], z[:, c, :T], AF.Identity if zbias is not None else AF.Copy,
                    [(zres, c)], [("zb", c)], bias=zbv)
                act(zsq[:, c, :T], z[:, c, :T], AF.Square, [(zres, c)], [("zsq", c)], bias=zbv)
            act(scr[:, 0:1], pvcol("eps"), AF.Ln, [], ["scr"])
            for c in range(CT):
                mm(ps[bm][:, :T], ones[:, :], zb[:, c, :T], c == 0, c == CT - 1,
                   [("zb", c), "ones"], [("ps", bm)], inc=(c == CT - 1))
            for c in range(CT):
                mm(ps[be][:, :T], ones[:, :], zsq[:, c, :T], c == 0, c == CT - 1,
                   [("zsq", c), "ones"], [("ps", be)], inc=(c == CT - 1))
            if mid is not None:
                pinned.update((bm, be))
                mid()
                pinned.difference_update((bm, be))
            msq, var, rstd = (lnt[:, i, :T] for i in range(3))
            act(msq, ps[bm][:, :T], AF.Square, [("ps", bm)], [("lnt", 0)])
            tt("dve", var, ps[be][:, :T], msq, ALU.subtract, [("ps", be), ("lnt", 0)], [("lnt", 1)])

            def keep_warm(bank, n, gate):
                for r in range(n):
                    mm(ps[bank][:, :T], ones[:, :], zb[:, 0, :T], True, True,
                       [("zb", 0), "ones"] + gate, [("ps", bank)], inc=(r == n - 1))
            peek = bank_rr[0]
            while peek in pinned:
                peek = (peek + 1) % 8
            keep_warm(peek, 6, [])
            keep_warm(be, 8, [])
            act(var, var, AF.Ln, [("lnt", 1)], [("lnt", 1)], bias=pvcol("eps"))
            keep_warm(be, 8, [("lnt", 1)])
            act(rstd, var, AF.Exp, [("lnt", 1)], [("lnt", 2)], scale=-0.5)
            slots = {}

            def p1(c):
                i = next_tmp()
                slots[c] = i
                t = tmp[:, i, :T]
                if zbias is not None:
                    stt("dve", t, z[:, c, :T], pvcol(zbias, c), ps[bm][:, :T], ALU.add, ALU.subtract,
                        [(zres, c), ("ps", bm)], [("tmp", i)])
                else:
                    tt("dve", t, z[:, c, :T], ps[bm][:, :T], ALU.subtract, [(zres, c), ("ps", bm)], [("tmp", i)])

            def p2(c):
                i = slots[c]
                t = tmp[:, i, :T]
                if t_to_xres:
                    tt("dve", xres[:, c, :T], t, rstd, ALU.mult, [("tmp", i), ("lnt", 2)], [("xres", c)])
                    src_ap, src_res = xres[:, c, :T], ("xres", c)
                else:
                    tt("dve", t, t, rstd, ALU.mult, [("tmp", i), ("lnt", 2)], [("tmp", i)])
                    src_ap, src_res = t, ("tmp", i)
                for (apf, func, res) in outs:
                    act(apf(c), src_ap, func, [src_res], [(res, c)],
                        scale=pvcol(gname, c), bias=pvcol(bname, c))
            p1(0)
            p1(1)
            if after_rstd is not None:
                after_rstd()
            for c in range(CT):
                p2(c)
                if c + 2 < CT:
                    p1(c + 2)

        xmode = {"ag": None, "ab": None}

        def resid(m, bank, T):
            sc = ALPHA if xmode["ag"] is None else pvcol(xmode["ag"], m)
            stt("dve", zbuf[:, m, :T], xres[:, m, :T], sc, ps[bank][:, :T], ALU.mult, ALU.add,
                [("xres", m), ("ps", bank)], [("zbuf", m)])

        def post_ln(T, li, w, final, after_rstd=None):
            gname, bname = ("ln%d_g" % w, li), ("ln%d_b" % w, li)
            zbias = xmode["ab"]
            if final:
                layer_norm(T, gname, bname, [(lambda c: zbuf[:, c, :T], AF.Identity, "zbuf")], zbias=zbias,
                           after_rstd=after_rstd)
            else:
                layer_norm(T, gname, bname, [(lambda c: xb[:, c, :T], AF.Identity, "xb")], zbias=zbias,
                           t_to_xres=True)
                xmode["ag"], xmode["ab"] = ("ag", li, w), ("ab", li, w)

        def run_units(T, units, rhs, rhs_res, nK, nko):
            def one(s, nblk, banks, k):
                for q in range(nblk):
                    mm(ps[banks[q]][:, :T], lhs(s, nblk, k, q), rhs(k), k == 0, k == nK - 1,
                       [(rhs_res, k), ("wslot", s)], [("ps", banks[q])], inc=(k == nK - 1))
            head = []
            for (key, nblk, cb) in units[:nko]:
                s = load_chunk(key, True)
                head.append((s, nblk, [next_bank() for _ in range(nblk)], cb))
            for k in range(nK):
                for (s, nblk, banks, cb) in head:
                    one(s, nblk, banks, k)
            for (s, nblk, banks, cb) in head:
                cb(banks)
            for (key, nblk, cb) in units[nko:]:
                s = load_chunk(key, True)
                banks = [next_bank() for _ in range(nblk)]
                for q in range(nblk):
                    for k in range(nK):
                        mm(ps[banks[q]][:, :T], lhs(s, nblk, k, q), rhs(k), k == 0, k == nK - 1,
                           [(rhs_res, k), ("wslot", s)], [("ps", banks[q])], inc=(k == nK - 1))
                cb(banks)

        def mix_out_and_resid(li, T):
            def mk(h):
                def cb(banks):
                    for q in range(2):
                        resid(2 * h + q, banks[q], T)
                return cb
            run_units(T, [(("mix_out", li, h), 2, mk(h)) for h in range(4)],
                      lambda k: ub[:, k, :T], "ub", CT, 2)

        def mixer_conf(li, ti, T):
            j = li // 3
            mixb = mix[:].bitcast(BF16)
            base = HOFF - (CONVW - 1)

            def glu(m, banks):
                ba, bg = banks
                i = next_tmp()
                act(tmp[:, i, :T], ps[bg][:, :T], AF.Sigmoid, [("ps", bg)], [("tmp", i)])
                act(mixb[:, m, base:HOFF], cstate[:, li, m, 0:CONVW - 1], AF.Copy,
                    [("cstate", li, m)], [("mix", m)])
                tt("dve", mixb[:, m, HOFF:HOFF + T], ps[ba][:, :T], tmp[:, i, :T], ALU.mult,
                   [("ps", ba), ("tmp", i)], [("mix", m)])

            def conv(m):
                sd = load_chunk(("a_dg", li, m), True)
                bc = next_bank()
                a = cacc_rr[0]
                cacc_rr[0] = (a + 1) % 4
                acc = cacc[:, a, :T]
                o = CPK[("a_dw", j)] + m * CONVW
                ts("dve", acc, mixb[:, m, base:base + T], cpk[:, o:o + 1], None, ALU.mult, None,
                   [("mix", m)], [("cacc", a)])
                for k in range(1, NDVE):
                    stt("dve", acc, mixb[:, m, base + k:base + k + T], cpk[:, o + k:o + k + 1], acc,
                        ALU.mult, ALU.add, [("mix", m), ("cacc", a)], [("cacc", a)])
                for k in range(NDVE, CONVW):
                    mm(ps[bc][:, :T], wslot[sd][:, (k - NDVE) * 128:(k - NDVE + 1) * 128],
                       mixb[:, m, base + k:base + k + T], k == NDVE, k == CONVW - 1, [("mix", m), ("wslot", sd)], [("ps", bc)], inc=(k == CONVW - 1))
                stt("dve", zbuf[:, m, :T], ps[bc][:, :T], pvcol(("a_dw_b", j), m), acc, ALU.add, ALU.add,
                    [("ps", bc), ("cacc", a)], [("zbuf", m)])
                act(cstate[:, li, m, 0:CONVW - 1], mixb[:, m, base + T:HOFF + T], AF.Copy, [("mix", m)],
                    [("cstate", li, m)], scale=(pvcol("flag") if ti == 0 else 1.0))

            rhs = lambda k: xb[:, k, :T]
            run_units(T, [(("a_in", li, m), 2, (lambda mm_: (lambda banks: glu(mm_, banks)))(m)) for m in range(2)],
                      rhs, "xb", CT, 2)
            for m in range(CT):
                if m + 2 < CT:
                    run_units(T, [(("a_in", li, m + 2), 2, (lambda mm_: (lambda banks: glu(mm_, banks)))(m + 2))],
                              rhs, "xb", CT, 0)
                conv(m)
            layer_norm(T, ("a_ln_g", j), ("a_ln_b", j), [(lambda c: ub[:, c, :T], AF.Silu, "ub")])
            mix_out_and_resid(li, T)

        def mixer_sgu(li, ti, T):
            nchk = T // 128
            vnT = hid

            def mk(i):
                def cb(banks):
                    for q in range(2):
                        b = banks[q]
                        if i < 4:
                            c = 2 * i + q
                            act(zbuf[:, c, :T], ps[b][:, :T], AF.Gelu, [("ps", b)], [("zbuf", c)])
                        else:
                            c = 2 * (i - 4) + q
                            act(mix[:, c, HOFF:HOFF + T], ps[b][:, :T], AF.Gelu, [("ps", b)], [("mix", c)])
                return cb
            run_units(T, [(("b_in", li, i), 2, mk(i)) for i in range(4)],
                      lambda k: xb[:, k, :T], "xb", CT, 3)
            layer_norm(T, "b_ln_g", "b_ln_b", [(lambda c: ub[:, c, :T], AF.Identity, "ub")],
                       mid=lambda: run_units(T, [(("b_in", li, i), 2, mk(i)) for i in range(4, CT)],
                                             lambda k: xb[:, k, :T], "xb", CT, 0))
            for h in range(CT):
                b = next_bank()
                pbf = ps[b][:].bitcast(BF16)
                for ck in range(nchk):
                    P.op("pe", (lambda o_, i_: (lambda hh: hh.transpose(o_, i_, identb[:, :])))(
                        pbf[:, ck * 128:(ck + 1) * 128], ub[:, h, ck * 128:(ck + 1) * 128]),
                        [("ub", h), "identb"], [("ps", b)], inc=(ck == nchk - 1))
                act(vnT[:, h, :T], pbf[:, :T], AF.Copy, [("ps", b)], [("hid", h)])
            for h in range(CT):
                b = next_bank()
                for ck in range(nchk):
                    mm(ps[b][:, ck * 128:(ck + 1) * 128], vnT[:, h, ck * 128:(ck + 1) * 128], wsTm[:, h, :],
                       True, True, [("hid", h), "wsTm"], [("ps", b)], inc=(ck == nchk - 1))
                i = next_tmp()
                o = CPK["bsb"] + h * 128
                for ck in range(nchk):
                    tt("dve", tmp[:, i, ck * 128:(ck + 1) * 128], ps[b][:, ck * 128:(ck + 1) * 128],
                       cpk[:, o:o + 128], ALU.add, [("ps", b)], [("tmp", i)])
                tt("dve", ub[:, h, :T], tmp[:, i, :T], mix[:, h, HOFF:HOFF + T], ALU.mult,
                   [("tmp", i), ("mix", h)], [("ub", h)])
            mix_out_and_resid(li, T)

        psc = sb("psc", [128, 2, HOFF + TMAX], F32)

        def mixer_pool(li, ti, T):
            H = 16

            def mk(i):
                def cb(banks):
                    for q in range(2):
                        c = 2 * i + q
                        b = banks[q]
                        act(mix[:, c, HOFF - H:HOFF], cstate[:, li, c, 0:H], AF.Copy, [("cstate", li, c)], [("mix", c)])
                        act(mix[:, c, HOFF:HOFF + T], ps[b][:, :T], AF.Copy, [("ps", b)], [("mix", c)])
                return cb
            run_units(T, [(("c_in", li, i), 2, mk(i)) for i in range(4)],
                      lambda k: xb[:, k, :T], "xb", CT, 3)
            E = HOFF + T
            for g in range(4):
                w = 2 << g
                for q in range(2):
                    c = 2 * g + q
                    cur = mix[:, c, :]
                    cur_res = ("mix", c)
                    lo = HOFF - H
                    step = 1
                    pi = 0
                    while step < w:
                        nlo = lo + step
                        dst = psc[:, pi, :]
                        tt("dve", dst[:, nlo:E], cur[:, nlo:E], cur[:, nlo - step:E - step], ALU.add,
                           [cur_res], [("psc", pi)])
                        cur, cur_res, lo = dst, ("psc", pi), nlo
                        pi ^= 1
                        step *= 2
                    stt("dve", zb[:, c, :T], cur[:, HOFF:E], 1.0 / w, mix[:, c, HOFF:E], ALU.mult, ALU.subtract,
                        [cur_res, ("mix", c)], [("zb", c)])
                    if ti == 1:
                        i = next_tmp()
                        o = CPK["icnt"] + g * 16
                        tt("dve", tmp[:, i, 0:16], cur[:, HOFF:HOFF + 16], cpk[:, o:o + 16], ALU.mult,
                           [cur_res], [("tmp", i)])
                        tt("dve", zb[:, c, 0:16], tmp[:, i, 0:16], mix[:, c, HOFF:HOFF + 16], ALU.subtract,
                           [("tmp", i), ("mix", c), ("zb", c)], [("zb", c)])
                    act(cstate[:, li, c, 0:H], mix[:, c, E - H:E], AF.Copy, [("mix", c)], [("cstate", li, c)],
                        scale=(pvcol("flag") if ti == 0 else 1.0))
            s = load_chunk(("c_grp", li), True)
            for g in range(4):
                for dd in range(2):
                    c = 2 * g + dd
                    b = next_bank()
                    for kk in range(2):
                        o = ((g * 2 + kk) * 2 + dd) * 128
                        mm(ps[b][:, :T], wslot[s][:, o:o + 128], zb[:, 2 * g + kk, :T], kk == 0, kk == 1,
                           [("zb", 2 * g + kk), ("wslot", s)], [("ps", b)], inc=(kk == 1))
                    act(ub[:, c, :T], ps[b][:, :T], AF.Copy, [("ps", b)], [("ub", c)], scale=pvcol("c_scale", c))
            mix_out_and_resid(li, T)

        def ffn(li, ti, T, state_only=False):
            par = ti % 2
            fo = CPK[("fdw", li)]
            hrow = 2 * li + par

            def wcol(kk, c):
                o = fo + kk * 2 * NP + c
                return cpk[:, o:o + 1]

            def wrow(kk):
                o = fo + kk * 2 * NP
                return cpk[:, o:o + 2 * NP]
            hres = [("hp", li, par, c) for c in range(2 * NP)]
            tt("pool", edge[:, :, 0], hp[:, hrow, :, 1], wrow(1), ALU.mult, hres, ["edge"])
            tt("pool", edge[:, :, 2], hp[:, hrow, :, 0], wrow(0), ALU.mult, hres, ["edge"])
            tt("pool", edge[:, :, 0], edge[:, :, 0], edge[:, :, 2], ALU.add, ["edge"], ["edge"])
            tt("pool", edge[:, :, 1], hp[:, hrow, :, 1], wrow(0), ALU.mult, hres, ["edge"])

            def mk(j):
                def cb(banks):
                    cs = (j, NP + j)
                    ai = ((2 * j) % 4, (2 * j + 1) % 4)
                    for q in range(2):
                        b, c, a = banks[q], cs[q], ai[q]
                        if state_only:
                            act(hp[:, 2 * li + 1 - par, c, :], ps[b][:, T - 2:T], AF.Copy, [("ps", b)],
                                [("hp", li, 1 - par, c)], scale=(pvcol("flag") if ti == 0 else 1.0))
                            continue
                        acc = cacc[:, a, :]
                        act(acc[:, 2:T], ps[b][:, 2:T], AF.Copy, [("ps", b)], [("cacc", a)], scale=wcol(2, c))
                        act(acc[:, 0:1], ps[b][:, 0:1], AF.Identity, [("ps", b), "edge"], [("cacc", a)],
                            scale=wcol(2, c), bias=edge[:, c, 0:1])
                        act(acc[:, 1:2], ps[b][:, 1:2], AF.Identity, [("ps", b), "edge"], [("cacc", a)],
                            scale=wcol(2, c), bias=edge[:, c, 1:2])
                        stt("dve", acc[:, 1:T], ps[b][:, 0:T - 1], wcol(1, c), acc[:, 1:T], ALU.mult, ALU.add,
                            [("ps", b), ("cacc", a)], [("cacc", a)])
                        stt("dve", acc[:, 2:T], ps[b][:, 0:T - 2], wcol(0, c), acc[:, 2:T], ALU.mult, ALU.add,
                            [("ps", b), ("cacc", a)], [("cacc", a)])
                        act(hp[:, 2 * li + 1 - par, c, :], ps[b][:, T - 2:T], AF.Copy, [("ps", b)],
                            [("hp", li, 1 - par, c)], scale=(pvcol("flag") if ti == 0 else 1.0))
                    if state_only:
                        return
                    i = next_tmp()
                    act(tmp[:, i, :T], cacc[:, ai[0], :T], AF.Silu, [("cacc", ai[0])], [("tmp", i)])
                    tt("pool", hid[:, j, :T], tmp[:, i, :T], cacc[:, ai[1], :T], ALU.mult,
                       [("tmp", i), ("cacc", ai[1])], [("hid", j)])
                return cb
            run_units(T, [(("f_up", li, j), 2, mk(j)) for j in range(NP)],
                      lambda k: xb[:, k, :T], "xb", CT, 3)

            def mkd(m):
                def cb(banks):
                    resid(m, banks[0], T)
                return cb
            if state_only:
                for m in range(CT):
                    load_chunk(("f_down", li, m), True)
                return
            run_units(T, [(("f_down", li, m), 1, mkd(m)) for m in range(CT)],
                      lambda k: hid[:, k, :T], "hid", NP, 4)

        all_x = [("xres", c) for c in range(CT)]
        P.dma("sp", lambda h: h.dma_start(out=cpk[:, :], in_=cpk_d[:, :]), "cst", writes=("cpk",))
        zflat = zbuf[:].rearrange("p c t -> p (c t)")
        P.dma("sp", lambda h: h.dma_start(out=zflat[:, 0:STG_COLS], in_=stg_d[:, :]), "cst",
              writes=[("zbuf", c) for c in range(CT)])
        cst_tag = ("cst", 32)
        for e_ in ("pe", "act", "dve", "pool"):
            P.wait(e_, cst_tag)
        P.op("dve", lambda h: h.memset(ones[:, :], 1.0 / D), (), ("ones",))
        P.op("dve", lambda h: h.memset(hp[:].rearrange("p a c t -> p (a c t)"), 0.0), (),
             [("hp", l_, a, c) for l_ in range(DEPTH) for a in range(2) for c in range(2 * NP)])
        P.op("dve", lambda h: h.memset(cstate[:].rearrange("p a c t -> p (a c t)"), 0.0), (),
             [("cstate", l_, c) for l_ in range(DEPTH) for c in range(CT)])
        for c in range(CT):
            P.op("dve", (lambda cc: (lambda h: h.memset(mix[:, cc, :], 0.0)))(c), (), (("mix", c),))
        zr = [("zbuf", c) for c in range(CT)] + ["cpk"]
        for h_ in range(CT):
            tt("dve", wsTm[:, h_, :], zflat[:, h_ * 128:(h_ + 1) * 128], zflat[:, CT * 128:CT * 128 + 128],
               ALU.mult, zr, ("wsTm",))
        P.op("dve", lambda h: h.tensor_copy(out=identb[:, :], in_=zflat[:, CT * 128 + 128:CT * 128 + 256]),
             zr, ("identb",))

        n_tiles = len(tiles)

        def load_x(ti):
            t0_, T_ = tiles[ti]
            P.dma("sp", (lambda a_, b_: (lambda h: h.dma_start(out=xres[:, :, :b_], in_=xin[:, :, a_:a_ + b_])))(t0_, T_),
                  "xin", reads=(), writes=all_x)

        for li in range(DEPTH):
            for w in (1, 2):
                for (dst, srcn) in ((("ag", li, w), ("ln%d_g" % w, li)), (("ab", li, w), ("ln%d_b" % w, li))):
                    o_d, o_s = CPK[dst], CPK[srcn]
                    ts("dve", cpk[:, o_d:o_d + CT], cpk[:, o_s:o_s + CT], ALPHA, None, ALU.mult, None,
                       ["cpk"], ["cpk"])
        load_x(0)
        cast_done = set()

        def cast_xb(ti_):
            T_ = tiles[ti_][1]
            for c in range(CT):
                act(xb[:, c, :T_], xres[:, c, :T_], AF.Copy, [("xres", c), "cpk"], [("xb", c)])
            cast_done.add(ti_)

        for ti, (t0, T) in enumerate(tiles):
            stored = ti >= n_tiles - n_store_tiles
            xmode["ag"], xmode["ab"] = None, None
            if ti not in cast_done:
                cast_xb(ti)
            for li_idx, li in enumerate(layers):
                last_layer = li_idx == len(layers) - 1
                kind = li % 3
                if kind == 0:
                    mixer_conf(li, ti, T)
                elif kind == 1:
                    mixer_sgu(li, ti, T)
                else:
                    mixer_pool(li, ti, T)
                post_ln(T, li, 1, False)
                ffn(li, ti, T, state_only=(last_layer and not stored))
                if last_layer:
                    if ti + 1 < n_tiles:
                        load_x(ti + 1)
                    if stored:
                        post_ln(T, li, 2, True,
                                after_rstd=((lambda t_=ti + 1: cast_xb(t_)) if ti + 1 < n_tiles else None))
                else:
                    post_ln(T, li, 2, False)
            if stored:
                o0 = sum(tt_[1] for tt_ in tiles[n_tiles - n_store_tiles:ti])
                P.dma("act", (lambda a, b_: (lambda h: h.dma_start(out=yout[:, :, a:a + b_], in_=zbuf[:, :, :b_])))(o0, T),
                      "out", reads=[("zbuf", c) for c in range(CT)], writes=())
        P.wait("act", ("out", P.dmacnt.get("out", 0)))

        handles = {}

        def emit(ename, h):
            for item in P.streams[ename]:
                if item[0] == "wait":
                    h.wait_ge(sems[item[1]], item[2])
                elif item[0] == "op":
                    ins = item[1](h)
                    if item[2] is not None:
                        ins.then_inc(sems[item[2]], 1)
                else:
                    item[1](h).then_inc(sems[item[2]], 16)

        with nc.Block() as block:
            @block.sync
            def _(h):
                emit("sp", h)

            @block.gpsimd
            def _(h):
                emit("pool", h)

            @block.scalar
            def _(h):
                emit("act", h)

            @block.vector
            def _(h):
                emit("dve", h)

            @block.tensor
            def _(h):
                emit("pe", h)
    return nc


def make_tiles(tok_per_core):
    tiles = [(0, HALO)]
    t = HALO
    while t < HALO + tok_per_core:
        T = min(TMAX, HALO + tok_per_core - t)
        tiles.append((t, T))
        t += T
    return tiles


def run(inputs, layers=(0, 1, 2, 3), trace=False):
    x = np.asarray(inputs["x"], np.float32)
    inp = {k: np.asarray(v, np.float32) for k, v in inputs.items()}
    B, S, _ = x.shape
    cores_per_seq = NCORES // B
    tpc = S // cores_per_seq
    tiles = make_tiles(tpc)
    n_store = len(tiles) - 1
    layers = list(layers)
    nc = build_program(layers, tiles, n_store, HALO + tpc, tpc)
    wpk = pack_weights(inp, layers)
    stg = pack_stage(inp)
    in_maps = []
    for core in range(NCORES):
        b, seg = divmod(core, cores_per_seq)
        p0 = seg * tpc
        xs = np.zeros((HALO + tpc, D), np.float32)
        if seg == 0:
            xs[HALO:] = x[b, 0:tpc]
        else:
            xs[:] = x[b, p0 - HALO:p0 + tpc]
        xin = np.ascontiguousarray(xs.reshape(HALO + tpc, CT, 128).transpose(2, 1, 0))
        in_maps.append({"xin": xin, "wpk": wpk, "cpk": pack_cpk(inp, seg == 0), "stg": stg})
    res = run_bass_kernel_spmd(nc, in_maps, core_ids=list(range(NCORES)), trace=trace)
    out = np.zeros((B, S, D), np.float32)
    for core in range(NCORES):
        b, seg = divmod(core, cores_per_seq)
        y = res.results[core]["yout"]
        out[b, seg * tpc:(seg + 1) * tpc] = y.transpose(2, 1, 0).reshape(tpc, D)
    return out, res


def kernel(**inputs):
    out, _ = run(inputs)
    return out
```
